# Optimizing a Trainium2 kernel written in Bass

```python
import jax, jax.numpy as jnp
from jax import lax
import numpy as np

D_MODEL = 1024
BATCH = 2
SEQ = 8192
DEPTH = 4

HEAD_DIM = 64
PLE_DIM = 256
GRID_W = 64
ROPE_THETA = 10000.0
RMS_EPS = 1e-6
D_FF = 2816
A_HEADS = 12
A_KV_HEADS = 4
A_GROUP = A_HEADS // A_KV_HEADS
A_RADIUS = 128
B_PAIRS = ((128, 1), (512, 4), (2048, 16))
B_SLOTS = 4
B_HEADS = B_SLOTS * len(B_PAIRS)
C_HEADS = D_MODEL // HEAD_DIM
NA_KH = 8
NA_KW = 16

A_Q = A_HEADS * HEAD_DIM
A_KV = A_KV_HEADS * HEAD_DIM
B_W = B_HEADS * HEAD_DIM
AB_IN = A_Q + 2 * A_KV + 3 * B_W
AB_OUT = A_Q + B_SLOTS * HEAD_DIM
C_IN = 3 * C_HEADS * HEAD_DIM
C_OUT = C_HEADS * HEAD_DIM
N_EVEN = (DEPTH + 1) // 2
N_ODD = DEPTH // 2
NEG_INF = -1e30

kernel_name = 'hybrid_banded_dilated_neighbourhood_encoder'


def rmsnorm(x, g):
    xf = x.astype(jnp.float32)
    y = xf * lax.rsqrt(jnp.mean(xf * xf, axis=-1, keepdims=True) + RMS_EPS)
    return (y * g.astype(jnp.float32)).astype(x.dtype)


def swiglu(x, w_gate, w_up, w_down):
    return (jax.nn.silu(x @ w_gate) * (x @ w_up)) @ w_down


def rope(x, pos):
    half = x.shape[-1] // 2
    inv = ROPE_THETA ** (-jnp.arange(half, dtype=jnp.float32) / half)
    ang = pos.astype(jnp.float32)[:, None] * inv[None, :]
    cos = jnp.cos(ang)[None, :, None, :]
    sin = jnp.sin(ang)[None, :, None, :]
    xf = x.astype(jnp.float32)
    x1, x2 = xf[..., :half], xf[..., half:]
    return jnp.concatenate([x1 * cos - x2 * sin, x2 * cos + x1 * sin], axis=-1).astype(x.dtype)


def banded_attention(q, k, v, radius, sink=None):
    n, L, hk, g, dh = q.shape
    bs = radius
    nb = -(-L // bs)
    lp = nb * bs
    q = jnp.pad(q, ((0, 0), (0, lp - L), (0, 0), (0, 0), (0, 0)))
    kv_pad = ((0, 0), (bs, lp - L + bs), (0, 0), (0, 0))
    k = jnp.pad(k, kv_pad).reshape(n, nb + 2, bs, hk, dh)
    v = jnp.pad(v, kv_pad).reshape(n, nb + 2, bs, hk, dh)
    kw = jnp.concatenate([k[:, :-2], k[:, 1:-1], k[:, 2:]], axis=2)
    vw = jnp.concatenate([v[:, :-2], v[:, 1:-1], v[:, 2:]], axis=2)
    qb = q.reshape(n, nb, bs, hk, g, dh)
    qpos = jnp.arange(lp).reshape(nb, bs)
    kpos = (jnp.arange(nb)[:, None] - 1) * bs + jnp.arange(3 * bs)[None, :]
    mask = (jnp.abs(qpos[:, :, None] - kpos[:, None, :]) <= radius) & ((kpos >= 0) & (kpos < L))[:, None, :]
    s = jnp.einsum('nbqhgd,nbkhd->nbhgqk', qb, kw).astype(jnp.float32) * (dh ** -0.5)
    s = jnp.where(mask[None, :, None, None], s, NEG_INF)
    m = jnp.max(s, axis=-1)
    if sink is not None:
        m = jnp.maximum(m, sink[:, :, None])
    e = jnp.exp(s - m[..., None])
    den = jnp.sum(e, axis=-1)
    if sink is not None:
        den = den + jnp.exp(sink[:, :, None] - m)
    pr = (e / den[..., None]).astype(v.dtype)
    o = jnp.einsum('nbhgqk,nbkhd->nbqhgd', pr, vw).reshape(n, lp, hk, g, dh)[:, :L]
    lse = (m + jnp.log(den)).transpose(0, 1, 4, 2, 3).reshape(n, lp, hk, g)[:, :L]
    return o, lse


def dilated_attention(q, k, v):
    bsz, s, _, dh = q.shape
    outs, lses = [], []
    for gi, (window, dil) in enumerate(B_PAIRS):
        lo = gi * B_SLOTS
        sub = s // dil

        def to_res(t):
            t = t[:, :, lo:lo + B_SLOTS]
            return t.reshape(bsz, sub, dil, B_SLOTS, dh).transpose(0, 2, 1, 3, 4).reshape(bsz * dil, sub, B_SLOTS, dh)

        o, lse = banded_attention(to_res(q)[:, :, :, None], to_res(k), to_res(v), window // (2 * dil))
        outs.append(o[:, :, :, 0].reshape(bsz, dil, sub, B_SLOTS, dh).transpose(0, 2, 1, 3, 4).reshape(bsz, s, B_SLOTS, dh))
        lses.append(lse[..., 0].reshape(bsz, dil, sub, B_SLOTS).transpose(0, 2, 1, 3).reshape(bsz, s, B_SLOTS))
    wts = jax.nn.softmax(jnp.stack(lses, axis=0), axis=0)
    out = jnp.einsum('gbsh,gbshd->bshd', wts, jnp.stack(outs, axis=0).astype(jnp.float32))
    return out.astype(q.dtype)


def neighbourhood_attention(q, k, v, rpb):
    bsz, s, nh, dh = q.shape
    rows = s // GRID_W
    kh = min(NA_KH, rows)
    kw = NA_KW
    qg = q.reshape(bsz, rows, GRID_W, nh, dh)
    kg = k.reshape(bsz, rows, GRID_W, nh, dh)
    vg = v.reshape(bsz, rows, GRID_W, nh, dh)
    cols = jnp.arange(GRID_W)
    col_start = jnp.clip(cols - kw // 2, 0, GRID_W - kw)
    col_idx = col_start[:, None] + jnp.arange(kw)[None, :]
    dc = col_idx - cols[:, None] + (NA_KW - 1)
    scale = dh ** -0.5

    def row_fn(i):
        rs = jnp.clip(i - kh // 2, 0, rows - kh)
        k_win = lax.dynamic_slice_in_dim(kg, rs, kh, axis=1)[:, :, col_idx]
        v_win = lax.dynamic_slice_in_dim(vg, rs, kh, axis=1)[:, :, col_idx]
        q_row = lax.dynamic_index_in_dim(qg, i, axis=1, keepdims=False)
        sc = jnp.einsum('bjhd,bajkhd->bhjak', q_row, k_win).astype(jnp.float32) * scale
        dr = rs + jnp.arange(kh) - i + (NA_KH - 1)
        bias = rpb[:, dr][:, :, dc].transpose(0, 2, 1, 3).astype(jnp.float32)
        sc = (sc + bias[None]).reshape(bsz, nh, GRID_W, kh * kw)
        pr = jax.nn.softmax(sc, axis=-1).reshape(bsz, nh, GRID_W, kh, kw).astype(v.dtype)
        return jnp.einsum('bhjak,bajkhd->bjhd', pr, v_win)

    o = lax.map(row_fn, jnp.arange(rows))
    return o.transpose(1, 0, 2, 3, 4).reshape(bsz, s, nh, dh)


def ab_mixer(xn, w_in, sink, w_out, pos):
    bsz, s, _ = xn.shape
    splits = [A_Q, A_Q + A_KV, A_Q + 2 * A_KV, A_Q + 2 * A_KV + B_W, A_Q + 2 * A_KV + 2 * B_W]
    qa, ka, va, qb, kb, vb = jnp.split(xn @ w_in, splits, axis=-1)
    qa = rope(qa.reshape(bsz, s, A_HEADS, HEAD_DIM), pos).reshape(bsz, s, A_KV_HEADS, A_GROUP, HEAD_DIM)
    ka = rope(ka.reshape(bsz, s, A_KV_HEADS, HEAD_DIM), pos)
    va = va.reshape(bsz, s, A_KV_HEADS, HEAD_DIM)
    oa, _ = banded_attention(qa, ka, va, A_RADIUS, sink.astype(jnp.float32).reshape(A_KV_HEADS, A_GROUP))
    shp = (bsz, s, B_HEADS, HEAD_DIM)
    ob = dilated_attention(rope(qb.reshape(shp), pos), rope(kb.reshape(shp), pos), vb.reshape(shp))
    mixed = jnp.concatenate([oa.reshape(bsz, s, A_Q).astype(xn.dtype),
                             ob.reshape(bsz, s, B_SLOTS * HEAD_DIM).astype(xn.dtype)], axis=-1)
    return mixed @ w_out


def c_mixer(xn, w_in, rpb, w_out):
    bsz, s, _ = xn.shape
    q, k, v = jnp.split(xn @ w_in, 3, axis=-1)
    shp = (bsz, s, C_HEADS, HEAD_DIM)
    o = neighbourhood_attention(q.reshape(shp), k.reshape(shp), v.reshape(shp), rpb)
    return o.reshape(bsz, s, C_OUT) @ w_out


def setup_inputs(seed: int = 0) -> dict:
    key = jax.random.key(seed)
    ks = jax.random.split(key, 21)
    d = D_MODEL

    def nrm(k, shape, scale):
        return scale * jax.random.normal(k, shape, jnp.float32)

    def gain(k, shape):
        return 1.0 + 0.05 * jax.random.normal(k, shape, jnp.float32)

    return {
        'x': nrm(ks[0], (BATCH, SEQ, d), 1.0),
        'p': nrm(ks[1], (DEPTH, BATCH, SEQ, PLE_DIM), 1.0),
        'norm_ffn1': gain(ks[2], (DEPTH, d)),
        'ffn1_w_gate': nrm(ks[3], (DEPTH, d, D_FF), d ** -0.5),
        'ffn1_w_up': nrm(ks[4], (DEPTH, d, D_FF), d ** -0.5),
        'ffn1_w_down': nrm(ks[5], (DEPTH, D_FF, d), D_FF ** -0.5),
        'norm_mix': gain(ks[6], (DEPTH, d)),
        'w_in_ab': nrm(ks[7], (N_EVEN, d, AB_IN), d ** -0.5),
        'sink_a': nrm(ks[8], (N_EVEN, A_HEADS), 0.5),
        'w_out_ab': nrm(ks[9], (N_EVEN, AB_OUT, d), AB_OUT ** -0.5),
        'w_in_c': nrm(ks[10], (N_ODD, d, C_IN), d ** -0.5),
        'rpb_c': nrm(ks[11], (N_ODD, C_HEADS, 2 * NA_KH - 1, 2 * NA_KW - 1), 0.5),
        'w_out_c': nrm(ks[12], (N_ODD, C_OUT, d), C_OUT ** -0.5),
        'norm_ffn2': gain(ks[13], (DEPTH, d)),
        'ffn2_w_gate': nrm(ks[14], (DEPTH, d, D_FF), d ** -0.5),
        'ffn2_w_up': nrm(ks[15], (DEPTH, d, D_FF), d ** -0.5),
        'ffn2_w_down': nrm(ks[16], (DEPTH, D_FF, d), D_FF ** -0.5),
        'norm_ple': gain(ks[17], (DEPTH, d)),
        'w_ple_gate': nrm(ks[18], (DEPTH, d, d), d ** -0.5),
        'w_ple_proj': nrm(ks[19], (DEPTH, PLE_DIM, d), PLE_DIM ** -0.5),
        'norm_final': gain(ks[20], (d,)),
    }


def reference(x, p, norm_ffn1, ffn1_w_gate, ffn1_w_up, ffn1_w_down, norm_mix, w_in_ab, sink_a,
              w_out_ab, w_in_c, rpb_c, w_out_c, norm_ffn2, ffn2_w_gate, ffn2_w_up, ffn2_w_down,
              norm_ple, w_ple_gate, w_ple_proj, norm_final):
    s = x.shape[1]
    pos = jnp.arange(s, dtype=jnp.int32)
    h = x
    for i in range(DEPTH):
        h = h + 0.5 * swiglu(rmsnorm(h, norm_ffn1[i]), ffn1_w_gate[i], ffn1_w_up[i], ffn1_w_down[i])
        hn = rmsnorm(h, norm_mix[i])
        j = i // 2
        if i % 2 == 0:
            h = h + ab_mixer(hn, w_in_ab[j], sink_a[j], w_out_ab[j], pos)
        else:
            h = h + c_mixer(hn, w_in_c[j], rpb_c[j], w_out_c[j])
        h = h + 0.5 * swiglu(rmsnorm(h, norm_ffn2[i]), ffn2_w_gate[i], ffn2_w_up[i], ffn2_w_down[i])
        gate = jax.nn.sigmoid(rmsnorm(h, norm_ple[i]) @ w_ple_gate[i])
        h = h + gate * (p[i] @ w_ple_proj[i])
    return rmsnorm(h, norm_final)
```

```python
import numpy as np
import ml_dtypes
from contextlib import ExitStack
import concourse.bass as bass
import concourse.mybir as mybir
from concourse.bass_utils import run_bass_kernel_spmd

F32 = mybir.dt.float32
BF16 = mybir.dt.bfloat16
AF = mybir.ActivationFunctionType
ALU = mybir.AluOpType

D = 1024
NT = 2048
TT = 512
NTT = NT // TT
DFF = 2816
NF = DFF // 128
NC8 = D // 128
EPS = 1e-6


class Op:
    __slots__ = ("stream", "fn", "dma", "cc", "deps", "signal", "sem", "val", "idx", "waits", "know")

    def __init__(self, stream, fn, dma, cc=False):
        self.stream = stream
        self.fn = fn
        self.dma = dma
        self.cc = cc
        self.deps = []
        self.signal = False
        self.sem = None
        self.val = 0
        self.waits = None
        self.know = None


class Sched:
    STREAMS = ("tensor", "vector", "scalar", "gpsimd", "sync")
    SEM_LIMIT = 20000
    NDMA = 12

    def __init__(self, nc):
        self.nc = nc
        self.ops = []
        self.last_w = {}
        self.readers = {}

    def add(self, stream, fn, reads=(), writes=(), dma=False, cc=False):
        reads = list(reads) + [("arena",)]
        op = Op(stream, fn, dma or cc, cc)
        op.idx = len(self.ops)
        deps = set()
        for r in reads:
            w = self.last_w.get(r)
            if w is not None:
                deps.add(w)
        for w_ in writes:
            w = self.last_w.get(w_)
            if w is not None:
                deps.add(w)
            for rd in self.readers.get(w_, ()):
                deps.add(rd)
        for r in reads:
            self.readers.setdefault(r, []).append(op)
        for w_ in writes:
            self.last_w[w_] = op
            self.readers[w_] = []
        deps.discard(op)
        for d in deps:
            if d.stream == "tensor" and stream == "tensor" and not d.dma and not dma:
                continue
            op.deps.append(d)
            d.signal = True
        self.ops.append(op)
        return op

    def finalize(self, es, final_dma_ops=()):
        nc = self.nc
        for o in final_dma_ops:
            o.signal = True
        cur_sem = {}
        cur_cnt = {}
        dma_sems = {}
        dma_cnt = {}
        dma_prev = {}
        nsem = [0]

        def new_sem(tag):
            nsem[0] += 1
            return es.enter_context(nc.semaphore(f"s_{tag}_{nsem[0]}"))

        know = {s: {} for s in self.STREAMS}

        def merge(kn, other):
            for s_, v_ in other.items():
                if kn.get(s_, 0) < v_:
                    kn[s_] = v_

        for op in self.ops:
            st = op.stream
            kn = know[st]
            waits = {}
            alldeps = list(op.deps)
            slot = j = None
            if op.cc:
                pass
            elif op.dma:
                if st not in dma_sems:
                    dma_sems[st] = [new_sem("d" + st) for _ in range(self.NDMA)]
                    dma_cnt[st] = 0
                    dma_prev[st] = [None] * self.NDMA
                j = dma_cnt[st]
                slot = j % self.NDMA
                prev = dma_prev[st][slot]
                if prev is not None:
                    alldeps.append(prev)
            for d in alldeps:
                assert d.sem is not None, "dep must be earlier and signalling"
                if kn.get(d.sem, 0) >= d.val:
                    continue
                if waits.get(d.sem, 0) < d.val:
                    waits[d.sem] = d.val
            for d in alldeps:
                merge(kn, d.know)
            op.waits = list(waits.items())
            if op.cc:
                op.sem = new_sem("cc")
                op.val = 1
                op.signal = True
                op.know = dict(kn)
                op.know[op.sem] = 1
            elif op.dma:
                op.sem = dma_sems[st][slot]
                op.val = 16 * (j // self.NDMA + 1)
                dma_cnt[st] = j + 1
                dma_prev[st][slot] = op
                op.signal = True
                op.know = dict(kn)
                op.know[op.sem] = op.val
            elif op.signal:
                if st not in cur_sem or cur_cnt[st] >= self.SEM_LIMIT:
                    cur_sem[st] = new_sem(st)
                    cur_cnt[st] = 0
                cur_cnt[st] += 1
                op.sem = cur_sem[st]
                op.val = cur_cnt[st]
                kn[op.sem] = op.val
                op.know = dict(kn)
        self.nsem = nsem[0]

    def emit(self, block, final_waits=()):
        nc = self.nc
        by_stream = {s: [] for s in self.STREAMS}
        for op in self.ops:
            by_stream[op.stream].append(op)

        def run(eng, ops, tail=()):
            for op in ops:
                for s_, v_ in op.waits:
                    eng.wait_ge(s_, v_)
                ins = op.fn(eng)
                if op.signal:
                    ins.then_inc(op.sem, 16 if (op.dma and not op.cc) else 1)
            for o in tail:
                eng.wait_ge(o.sem, o.val)

        @block.tensor
        def _(e):
            run(e, by_stream["tensor"])

        @block.vector
        def _(e):
            run(e, by_stream["vector"])

        @block.scalar
        def _(e):
            run(e, by_stream["scalar"])

        @block.gpsimd
        def _(e):
            run(e, by_stream["gpsimd"])

        @block.sync
        def _(e):
            run(e, by_stream["sync"], tail=final_waits)


class Builder:
    def __init__(self, nc, es):
        self.nc = nc
        self.es = es
        self.S = Sched(nc)
        self.uid = 0
        self.ps2 = [nc.alloc_psum_tensor(f"ps{i}", [128, 1024], F32) for i in range(4)]
        self.banks = [self.ps2[i // 2][:, (i % 2) * 512:(i % 2 + 1) * 512] for i in range(8)]
        self.bank_rr = 0
        self.out_dmas = []

    def sb(self, name, shape, dtype):
        return self.nc.alloc_sbuf_tensor(name, shape, dtype)

    def dram_in(self, name, shape, dtype=F32):
        return self.nc.dram_tensor(name, list(shape), dtype, kind="ExternalInput").ap()

    def dram_out(self, name, shape, dtype=F32):
        return self.nc.dram_tensor(name, list(shape), dtype, kind="ExternalOutput").ap()

    def mm(self, out, lhsT, rhs, start, stop, reads, writes):
        return self.S.add("tensor", lambda e: e.matmul(out, lhsT, rhs, start=start, stop=stop),
                          reads, writes)

    def act(self, out, in_, func, reads, writes, scale=1.0, bias=None):
        if bias is None:
            return self.S.add("scalar", lambda e: e.activation(out, in_, func, scale=scale), reads, writes)
        return self.S.add("scalar", lambda e: e.activation(out, in_, func, bias=bias, scale=scale), reads, writes)

    def tt(self, eng, out, in0, in1, op, reads, writes):
        return self.S.add(eng, lambda e: e.tensor_tensor(out, in0, in1, op), reads, writes)

    def stt(self, eng, out, in0, scalar, in1, op0, op1, reads, writes):
        return self.S.add(eng, lambda e: e.scalar_tensor_tensor(out, in0, scalar, in1, op0, op1), reads, writes)

    def ts(self, eng, out, in0, s1, s2, op0, op1, reads, writes):
        if s2 is None:
            return self.S.add(eng, lambda e: e.tensor_scalar(out, in0, s1, None, op0), reads, writes)
        return self.S.add(eng, lambda e: e.tensor_scalar(out, in0, s1, s2, op0, op1), reads, writes)

    def copy(self, eng, out, in_, reads, writes):
        if eng == "scalar":
            return self.S.add(eng, lambda e: e.copy(out, in_), reads, writes)
        return self.S.add(eng, lambda e: e.tensor_copy(out, in_), reads, writes)

    def memset(self, eng, ap, val, writes):
        return self.S.add(eng, lambda e: e.memset(ap, val), (), writes)

    def dma(self, stream, out, in_, reads, writes):
        return self.S.add(stream, lambda e: e.dma_start(out, in_), reads, writes, dma=True)


def tsl(t):
    return slice(t * TT, (t + 1) * TT)


class Model(Builder):
    def setup_common(self):
        B = self
        self.h = B.sb("h", [128, NC8, NT], F32)
        self.xn = B.sb("xn", [128, NC8, NT], BF16)
        self.ones = B.sb("ones", [128, 128], BF16)
        B.memset("vector", self.ones[:], 1.0 / D, [("ones",)])
        self.sq = [B.sb(f"sq{i}", [128, NC8, TT], BF16) for i in range(1)]
        self.rstd = [B.sb(f"rstd{i}", [128, TT], F32) for i in range(2)]
        self.rtmp = [B.sb(f"rtmp{i}", [128, TT], F32) for i in range(2)]
        self.epsc = B.sb("epsc", [128, 1], F32)
        B.memset("vector", self.epsc[:], EPS, [("epsc",)])
        self.norm_cnt = 0
        self.gu_cnt = 0
        self.dn_cnt = 0
        self.sg = [B.sb(f"sg{i}", [128, TT], F32) for i in range(2)]
        self.ARENA = 44 * 1024
        self.arena = B.sb("arena", [128, self.ARENA], BF16)
        self.aoff = 0
        self.GS = 2
        self.wbuf_cnt = 0
        self.h1_cnt = 0
        self.cur_view = None

    def phase(self, name):
        self.S.add("vector", lambda e: e.memset(self.epsc[:, 0:1], EPS), (), [("arena",), ("epsc",)])
        self.aoff = 0
        self.cur_view = name

    def take(self, shape, dtype):
        n = int(np.prod(shape))
        ne = n * 2 if dtype == F32 else n
        start = (self.aoff + 15) // 16 * 16
        assert start + ne <= self.ARENA, ("arena overflow", self.cur_view, start + ne)
        self.aoff = start + ne
        v = self.arena[:, start:start + ne]
        if dtype == F32:
            v = v.bitcast(F32)
        if len(shape) == 2:
            v = v.rearrange("p (a b) -> p a b", a=shape[0], b=shape[1])
        elif len(shape) == 3:
            v = v.rearrange("p (a b c) -> p a b c", a=shape[0], b=shape[1], c=shape[2])
        return v

    def ffn_view(self):
        self.phase("ffn")
        self.wgu = [self.take((2, NC8, self.GS * 128), BF16) for i in range(2)]
        self.wd = [self.take((self.GS, D), BF16) for i in range(3)]
        self.h1 = [self.take((self.GS, NT), BF16) for i in range(3)]

    def load_gains(self, name, n):
        g_d = self.dram_in(name, [128, n * NC8])
        g = self.sb(name + "_sb", [128, n, NC8], F32)
        self.dma("sync", g[:], g_d.rearrange("p (n c) -> p n c", c=NC8), [], [(name,)])
        return g

    def rmsnorm(self, gsb, li, gname, out_f32_dram=None):
        B = self
        h, xn = self.h, self.xn
        for t in range(NTT):
            i = self.norm_cnt % 2
            self.norm_cnt += 1
            sq, rstd = self.sq[0], self.rstd[i]
            bank = self.banks[4 + (self.dn_cnt % 4)]
            bkey = ("bank", 4 + (self.dn_cnt % 4))
            self.dn_cnt += 1
            for c in range(NC8):
                B.act(sq[:, c, :], h[:, c, tsl(t)], AF.Square, [("h", c, t)], [("sq", c)])
            for c in range(NC8):
                B.mm(bank[:], self.ones[:], sq[:, c, :], c == 0, c == NC8 - 1,
                     [("ones",), ("sq", c)], [bkey])
            B.act(self.rtmp[i][:], bank[:], AF.Sqrt, [bkey], [("rtmp", i)], bias=self.epsc[:, 0:1])
            B.S.add("vector", lambda e, o=rstd, a=self.rtmp[i]: e.reciprocal(o[:], a[:]), [("rtmp", i)], [("rstd", i)])
            for c in range(NC8):
                eng = "vector"
                if out_f32_dram is None:
                    B.stt(eng, xn[:, c, tsl(t)], h[:, c, tsl(t)], gsb[:, li, c:c + 1], rstd[:],
                          ALU.mult, ALU.mult, [("h", c, t), ("rstd", i), (gname,)], [("xn", c, t)])
                else:
                    B.stt(eng, h[:, c, tsl(t)], h[:, c, tsl(t)], gsb[:, li, c:c + 1], rstd[:],
                          ALU.mult, ALU.mult, [("h", c, t), ("rstd", i), (gname,)], [("h", c, t)])
            if out_f32_dram is not None:
                o = B.dma("sync", out_f32_dram[:, :, tsl(t)], h[:, :, tsl(t)],
                          [("h", c, t) for c in range(NC8)], [("outdram", t)])
                self.out_dmas.append(o)

    def ffn(self, wg_d, wu_d, wd_d, tag):
        B = self
        GS = self.GS
        ng = NF // GS
        h, xn = self.h, self.xn
        self.ffn_view()
        wg_v = wg_d.rearrange("(k p) f -> p k f", p=128)
        wu_v = wu_d.rearrange("(k p) f -> p k f", p=128)
        wd_v = wd_d.rearrange("(j p) o -> p j o", p=128)
        state = {}

        def GU(gi):
            wb = self.wbuf_cnt % 2
            db = self.wbuf_cnt % 3
            hb = self.h1_cnt % 3
            state[gi] = (db, hb)
            self.wbuf_cnt += 1
            self.h1_cnt += 1
            fs = slice(gi * GS * 128, (gi + 1) * GS * 128)
            B.dma("gpsimd", self.wgu[wb][:, 0], wg_v[:, :, fs], [], [("wgu", wb, 0)])
            B.dma("gpsimd", self.wgu[wb][:, 1], wu_v[:, :, fs], [], [("wgu", wb, 1)])
            B.dma("gpsimd", self.wd[db], wd_v[:, gi * GS:(gi + 1) * GS, :], [], [("wd", db)])
            for fl in range(GS):
                for t in range(NTT):
                    p = self.gu_cnt % 2
                    self.gu_cnt += 1
                    bg, bu = self.banks[2 * p], self.banks[2 * p + 1]
                    kg, ku = ("bank", 2 * p), ("bank", 2 * p + 1)
                    for which, bank, bk in ((0, bg, kg), (1, bu, ku)):
                        for k in range(NC8):
                            B.mm(bank[:], self.wgu[wb][:, which, k, fl * 128:(fl + 1) * 128],
                                 xn[:, k, tsl(t)], k == 0, k == NC8 - 1,
                                 [("wgu", wb, which), ("xn", k, t)], [bk])
                    B.act(self.sg[p][:], bg[:], AF.Silu, [kg], [("sg", p)])
                    B.tt("vector", self.h1[hb][:, fl, tsl(t)], self.sg[p][:], bu[:], ALU.mult,
                         [("sg", p), ku], [("h1", hb, fl, t)])

        def DOWN(gi):
            wb, hb = state[gi]
            for oc in range(NC8):
                for t in range(NTT):
                    bi = 4 + (self.dn_cnt % 4)
                    self.dn_cnt += 1
                    bo, bk = self.banks[bi], ("bank", bi)
                    for fl in range(GS):
                        B.mm(bo[:], self.wd[wb][:, fl, oc * 128:(oc + 1) * 128],
                             self.h1[hb][:, fl, tsl(t)], fl == 0, fl == GS - 1,
                             [("wd", wb), ("h1", hb, fl, t)], [bk])
                    B.stt("vector", h[:, oc, tsl(t)], bo[:], 0.5, h[:, oc, tsl(t)], ALU.mult, ALU.add,
                          [bk, ("h", oc, t)], [("h", oc, t)])

        for gi in range(ng + 1):
            if gi < ng:
                GU(gi)
            if gi >= 1:
                DOWN(gi - 1)


def gains_layout(g):
    n = g.shape[0]
    return np.ascontiguousarray(g.reshape(n, NC8, 128).transpose(2, 0, 1).reshape(128, n * NC8))


A_CHUNK_HEADS = [(0, 3), (1, 4), (2, 5), (6, 9), (7, 10), (8, 11)]
B_DIL = (1, 4, 16)


def _mixer_methods():
    pass


class Mixer(Model):
    def setup_mixer_consts(self, with_rope):
        B = self
        cm_d = self.dram_in("cmask", [128, 640])
        self.cmask = B.sb("cmask_sb", [128, 640], BF16)
        B.dma("gpsimd", self.cmask[:], cm_d[:], [], [("cmask",)])
        if with_rope:
            ps_d = self.dram_in("pswap", [128, 128])
            self.pswap = B.sb("pswap_sb", [128, 128], BF16)
            B.dma("gpsimd", self.pswap[:], ps_d[:], [], [("pswap",)])
            self.ropecs_d = self.dram_in("ropecs", [128, 2, NT])

    def inproj(self, w_d, fm_chunks, vdils, qk_out, v_out):
        B = self
        xn = self.xn
        self.phase("inproj")
        wslots = [self.take((NC8, 256), BF16) for _ in range(4)]
        stage = [self.take((NT,), BF16) for _ in range(3)]
        qf = [self.take((TT,), F32) for _ in range(2)]
        qb = [self.take((TT,), BF16) for _ in range(2)]
        t2 = [self.take((TT,), F32) for _ in range(2)]
        vst = [self.take((4, 384), BF16) for _ in range(2)]
        for i in range(2):
            B.memset("vector", vst[i], 1.0, [("vst", i, j) for j in range(4)])
        if any(r for r, _ in fm_chunks):
            self.ropecs = self.take((2, NT), F32)
            B.dma("sync", self.ropecs, self.ropecs_d, [], [("ropecs",)])
        wv = w_d.rearrange("(k p) f -> p k f", p=128)
        nfm = len(fm_chunks)
        wcnt = 0
        cnt = 0
        for wg in range(nfm // 2):
            ws = wcnt % 4
            wcnt += 1
            B.dma("gpsimd", wslots[ws], wv[:, :, wg * 256:(wg + 1) * 256], [], [("ws", ws)])
            for fl in range(2):
                ci = wg * 2 + fl
                rope, d = fm_chunks[ci]
                si = ci % 3
                st = stage[si]
                for t in range(NTT):
                    p = cnt % 2
                    bi = cnt % 4
                    cnt += 1
                    bank, bk = self.banks[bi], ("bank", bi)
                    for k in range(NC8):
                        B.mm(bank, wslots[ws][:, k, fl * 128:(fl + 1) * 128], xn[:, k, tsl(t)],
                             k == 0, k == NC8 - 1, [("ws", ws), ("xn", k, t)], [bk])
                    if d == 1:
                        dst = st[:, tsl(t)]
                    else:
                        dst = st.rearrange("p (c t) -> p c t", c=d)[:, :, t * TT // d:(t + 1) * TT // d]

                    def nat(ap):
                        return ap if d == 1 else ap.rearrange("p (t c) -> p c t", c=d)
                    if not rope:
                        B.copy("scalar", dst, nat(bank), [bk], [("stage", si, t)])
                        continue
                    b2, bk2 = self.banks[4 + bi], ("bank", 4 + bi)
                    B.copy("scalar", qf[p], bank, [bk], [("qf", p)])
                    B.copy("scalar", qb[p], bank, [bk], [("qb", p)])
                    B.mm(b2, self.pswap[:], qb[p], True, True, [("pswap",), ("qb", p)], [bk2])
                    B.tt("vector", qf[p], qf[p], self.ropecs[:, 0, tsl(t)], ALU.mult,
                         [("qf", p), ("ropecs",)], [("qf", p)])
                    B.tt("vector", t2[p], b2, self.ropecs[:, 1, tsl(t)], ALU.mult,
                         [bk2, ("ropecs",)], [("t2", p)])
                    B.tt("vector", dst, nat(qf[p]), nat(t2[p]), ALU.add,
                         [("qf", p), ("t2", p)], [("stage", si, t)])
                dst_ap = qk_out(ci) if callable(qk_out) else qk_out[ci]
                B.dma("sync", dst_ap, st if not callable(qk_out) else self._stage_view(st, d),
                      [("stage", si, t) for t in range(NTT)], [("qk_out", ci)])
        ncfm = nfm * 128
        vcnt = 0
        for vg, d in enumerate(vdils):
            ws = wcnt % 4
            wcnt += 1
            B.dma("gpsimd", wslots[ws], wv[:, :, ncfm + vg * 256:ncfm + (vg + 1) * 256], [], [("ws", ws)])
            T = NT // d
            nb = T // 128
            for c in range(d):
                for tb in range(nb):
                    blk = c * nb + tb
                    vi = (blk // 4) % 2
                    bi = cnt % 4
                    cnt += 1
                    bank, bk = self.banks[bi], ("bank", bi)
                    t0 = c + d * tb * 128
                    for k in range(NC8):
                        if d == 1:
                            lhsT = xn[:, k, t0:t0 + 128]
                        else:
                            lhsT = xn[:, k, t0:t0 + 127 * d + 1:d]
                        B.mm(bank[:, 0:256], lhsT, wslots[ws][:, k, :], k == 0, k == NC8 - 1,
                             [("ws", ws)] + [("xn", k, t) for t in range(NTT)], [bk])
                    dstv = vst[vi][:, blk % 4, :].rearrange("p (a b c) -> p a b c", a=2, b=3, c=64)[:, :, 0:3:2, :]
                    srcv = bank[:, 0:256].rearrange("p (a b c) -> p a b c", a=2, b=2, c=64)
                    B.copy("scalar", dstv, srcv, [bk], [("vst", vi, blk % 4)])
                    if blk % 4 == 3:
                        r0 = (blk - 3) * 128
                        if callable(v_out):
                            dst_ap = v_out(vg, d, blk - 3)
                        else:
                            dst_ap = v_out[vg, r0:r0 + 512, :].rearrange("(j p) x -> p j x", p=128)
                        B.dma("sync", dst_ap, vst[vi],
                              [("vst", vi, j) for j in range(4)], [("v_out", vg, blk // 4)])

    @staticmethod
    def _stage_view(st, d):
        return st if d == 1 else st.rearrange("p (c t) -> p c t", c=d)

    def attn_setup(self):
        self.pt = [self.take((768,), BF16) for _ in range(3)]
        self.rc = [self.take((TT,), F32) for _ in range(2)]
        self.ucnt = 0
        self.ocnt = 0
        self.pcnt = 0

    def attn_unit(self, q_ap, ktiles, vtiles, mask_ap, o_ap, okey, rd):
        B = self
        n = len(ktiles)
        si = self.ucnt % 2
        pi = self.ucnt % 3
        self.ucnt += 1
        S, sk = self.ps2[si], ("S", si)
        pt, pk = self.pt[pi], ("pt", pi)
        for i, kT in enumerate(ktiles):
            B.mm(S[:, i * 128:(i + 1) * 128], kT, q_ap, True, True, rd, [sk])
        B.act(pt[:, 0:n * 128], S[:, 0:n * 128], AF.Exp, [sk], [pk], scale=0.125)
        B.tt("vector", pt[:, 0:n * 128], pt[:, 0:n * 128], mask_ap, ALU.mult, [pk] + rd, [pk])
        for i, va in enumerate(vtiles):
            B.mm(o_ap, va, pt[:, i * 128:(i + 1) * 128], i == 0, i == n - 1, [pk] + rd, [okey])

    def obank(self):
        j = self.ocnt % 4
        self.ocnt += 1
        return self.banks[4 + j], ("bank", 4 + j)

    def normalize(self, src_o, src_den, par, dst, rd, wr, sink_col=None):
        B = self
        oh = slice(par * 64, par * 64 + 64)
        dh = slice((1 - par) * 64, (1 - par) * 64 + 64)
        p = self.pcnt % 2
        self.pcnt += 1
        rc = self.rc[p]
        if sink_col is not None:
            B.ts("vector", rc[oh], src_den[dh], self.esink[dh, sink_col:sink_col + 1], None, ALU.add, None,
                 rd + [("esink",)], [("rc", p)])
            B.S.add("vector", lambda e: e.reciprocal(rc[oh], rc[oh]), [("rc", p)], [("rc", p)])
        else:
            B.S.add("vector", lambda e: e.reciprocal(rc[oh], src_den[dh]), rd, [("rc", p)])
        B.tt("vector", dst[oh], src_o[oh], rc[oh], ALU.mult, rd + [("rc", p)], wr)

    def load_group(self, kh_d, vh_d, q_srcs, klen, ntile):
        B = self
        kt = [self.take((klen,), BF16) for _ in range(2)]
        vt = self.take((ntile, 384), BF16)
        qt = [self.take((NT,), BF16) for _ in q_srcs]
        for i in range(2):
            B.dma("sync", kt[i], kh_d[i], self.dram_rd, [("kt", i)])
        B.dma("sync", vt, vh_d.rearrange("(n p) x -> p n x", p=128), self.dram_rd, [("vt",)])
        self.vmask(vt[:, 0:1, :], vt[:, ntile - 1:ntile, :])
        for i, qs in enumerate(q_srcs):
            B.dma("sync", qt[i], qs, self.dram_rd, [("qt", i)])
        return kt, vt, qt

    dram_rd = ()
    valid = None

    def vmask(self, lo, hi):
        if self.valid is None:
            return
        B = self
        B.ts("vector", lo, lo, self.valid[lo.base_partition():lo.base_partition() + lo.shape[0], 0:1], None,
             ALU.mult, None, [("vt",), ("valid",)], [("vt",)])
        B.ts("vector", hi, hi, self.valid[hi.base_partition():hi.base_partition() + hi.shape[0], 1:2], None,
             ALU.mult, None, [("vt",), ("valid",)], [("vt",)])

    def attn_ab(self, qk_d, kA_d, vA_d, kB_d, vB_d, sink_d):
        B = self
        mixed = self.xn
        self.phase("attnA")
        self.attn_setup()
        self.esink = self.take((12,), F32)
        B.dma("sync", self.esink, sink_d, [], [("esink",)])
        B.act(self.esink, self.esink, AF.Exp, [("esink",)], [("esink",)])
        kt, vt, qt = self.load_group(kA_d, vA_d, [qk_d[i] for i in range(6)], NT + 256, 18)
        maskA = self.cmask[:, 0:384]
        for qc in range(6):
            kc = qc // 3
            for par in range(2):
                head = A_CHUNK_HEADS[qc][par]
                hs = slice(par * 64, par * 64 + 64)
                vs = slice(kc * 192 + par * 64, kc * 192 + par * 64 + 128)
                rd = [("kt", kc), ("vt",), ("qt", qc), ("cmask",)]
                for t in range(NTT):
                    ob, ok = self.obank()
                    for u in range(4):
                        qblk = t * 4 + u
                        self.attn_unit(qt[qc][hs, qblk * 128:(qblk + 1) * 128],
                                       [kt[kc][hs, (qblk + i) * 128:(qblk + i + 1) * 128] for i in range(3)],
                                       [vt[:, qblk + i, vs] for i in range(3)],
                                       maskA, ob[:, u * 128:(u + 1) * 128], ok, rd)
                    self.normalize(ob, ob, par, mixed[:, qc, tsl(t)], [ok], [("xn", qc, t)], sink_col=head)
        for g, d in enumerate(B_DIL):
            T = NT // d
            nb = T // 128
            self.phase("attnB%d" % g)
            self.attn_setup()
            if g == 0:
                pass
            acc = [[self.take((NT,), F32) for _ in range(2)] for _ in range(2)]
            if g == 0:
                self.accB = acc
            klen, ntile = d * (T + 128), d * (nb + 1)
            ktb = self.take((klen,), BF16)
            vtb = self.take((ntile, 192), BF16)
            qtb = self.take((NT,), BF16)
            maskB = self.cmask[:, 384:640]
            for sp in range(2):
                B.dma("sync", ktb, kB_d[g][sp], self.dram_rd, [("kt", 0)])
                B.dma("sync", vtb, vB_d[g].rearrange("(n p) x -> p n x", p=128)[:, :, sp * 192:(sp + 1) * 192],
                      self.dram_rd, [("vt",)])
                vv = vtb.rearrange("p (c n) x -> p c n x", c=d)
                self.vmask(vv[0:64, :, 0, :], vv[64:128, :, nb, :])
                B.dma("sync", qtb, qk_d[8 + 2 * g + sp], self.dram_rd, [("qt", 0)])
                kt, vt, qt = [ktb, ktb], vtb, [qtb, qtb]
                for par in range(2):
                    hs = slice(par * 64, par * 64 + 64)
                    vs = slice(par * 64, par * 64 + 128)
                    rd = [("kt", 0), ("vt",), ("qt", 0), ("cmask",)]
                    a = acc[sp][par]
                    units = [(c, qb_) for c in range(d) for qb_ in range(nb)]
                    for u0 in range(0, len(units), 4):
                        ob, ok = self.obank()
                        for u in range(4):
                            c, qb_ = units[u0 + u]
                            kbase = c * (T + 128) + qb_ * 128
                            vbase = c * (nb + 1) + qb_
                            self.attn_unit(qt[sp][hs, c * T + qb_ * 128:c * T + (qb_ + 1) * 128],
                                           [kt[sp][hs, kbase + i * 128:kbase + (i + 1) * 128] for i in range(2)],
                                           [vt[:, vbase + i, vs] for i in range(2)],
                                           maskB, ob[:, u * 128:(u + 1) * 128], ok, rd)
                        if d == 1:
                            dst = a[:, u0 * 128:(u0 + 4) * 128]
                            src = ob
                        elif d == 4:
                            c = units[u0][0]
                            dst = a.rearrange("p (t c) -> p c t", c=4)[:, c, :]
                            src = ob
                        else:
                            c0 = units[u0][0]
                            dst = a.rearrange("p (t c) -> p c t", c=16)[:, c0:c0 + 4, :]
                            src = ob.rearrange("p (j t) -> p j t", j=4)
                        akey = ("acc", sp, par)
                        if g == 0:
                            B.copy("scalar", dst, src, [ok], [akey])
                        else:
                            B.tt("vector", dst, src, dst, ALU.add, [ok, akey], [akey])
            if g == 2:
                for sp in range(2):
                    for par in range(2):
                        a = acc[sp][par]
                        for t in range(NTT):
                            self.normalize(a[:, tsl(t)], a[:, tsl(t)], par, mixed[:, 6 + sp, tsl(t)],
                                           [("acc", sp, par)], [("xn", 6 + sp, t)])

    def outproj(self, w_d, scale=1.0):
        B = self
        self.phase("outproj")
        wsq = self.take((NC8, D), BF16)
        B.dma("gpsimd", wsq, w_d.rearrange("(k p) o -> p k o", p=128), [], [("wsq",)])
        for oc in range(NC8):
            for t in range(NTT):
                bi = 4 + (self.dn_cnt % 4)
                self.dn_cnt += 1
                bo, bk = self.banks[bi], ("bank", bi)
                for k in range(NC8):
                    B.mm(bo, wsq[:, k, oc * 128:(oc + 1) * 128], self.xn[:, k, tsl(t)], k == 0, k == NC8 - 1,
                         [("wsq",), ("xn", k, t)], [bk])
                B.stt("vector", self.h[:, oc, tsl(t)], bo, scale, self.h[:, oc, tsl(t)], ALU.mult, ALU.add,
                      [bk, ("h", oc, t)], [("h", oc, t)])

    def ple(self, wg_d, wp_d, pT_d):
        B = self
        self.phase("ple")
        wsq = self.take((NC8, D), BF16)
        wpp = self.take((2, D), BF16)
        pT = self.take((2, NT), BF16)
        B.dma("gpsimd", wsq, wg_d.rearrange("(k p) o -> p k o", p=128), [], [("wsq",)])
        B.dma("gpsimd", wpp, wp_d.rearrange("(k p) o -> p k o", p=128), [], [("wpp",)])
        B.dma("gpsimd", pT, pT_d.rearrange("(k p) t -> p k t", p=128), [], [("pT",)])
        cnt = 0
        for oc in range(NC8):
            for t in range(NTT):
                p = cnt % 2
                cnt += 1
                bg, kg = self.banks[2 * p], ("bank", 2 * p)
                bp, kp = self.banks[2 * p + 1], ("bank", 2 * p + 1)
                for k in range(NC8):
                    B.mm(bg, wsq[:, k, oc * 128:(oc + 1) * 128], self.xn[:, k, tsl(t)], k == 0, k == NC8 - 1,
                         [("wsq",), ("xn", k, t)], [kg])
                for k in range(2):
                    B.mm(bp, wpp[:, k, oc * 128:(oc + 1) * 128], pT[:, k, tsl(t)], k == 0, k == 1,
                         [("wpp",), ("pT",)], [kp])
                B.act(self.sg[p][:], bg, AF.Sigmoid, [kg], [("sg", p)])
                B.tt("vector", self.sg[p][:], self.sg[p][:], bp, ALU.mult, [("sg", p), kp], [("sg", p)])
                B.tt("vector", self.h[:, oc, tsl(t)], self.h[:, oc, tsl(t)], self.sg[p][:], ALU.add,
                     [("sg", p), ("h", oc, t)], [("h", oc, t)])

    C_CLS = {0: (5, 6, 0), 1: (11, 5, 0), 14: (16, 5, 0), 15: (21, 6, -1)}

    def attn_c(self, qk_d, kC_d, vC_d, bias_d, qm_d):
        B = self
        mixed = self.xn
        self.phase("attnC")
        self.attn_setup()
        qm = self.take((27 * 128,), BF16)
        B.dma("gpsimd", qm, qm_d, [], [("qm",)])
        kt = [self.take((NT + 512,), BF16) for _ in range(2)]
        vt = self.take((20, 384), BF16)
        qt = [self.take((NT,), BF16) for _ in range(2)]
        bst = [self.take((896,), F32) for _ in range(2)]
        ed = [self.take((896,), BF16) for _ in range(2)]
        ecls = [self.take((27 * 128,), BF16) for _ in range(2)]
        for g4 in range(4):
            for i in range(2):
                B.dma("sync", kt[i], kC_d[g4, i], self.dram_rd, [("kt", i)])
                B.dma("sync", qt[i], qk_d[2 * g4 + i], self.dram_rd, [("qt", i)])
            B.dma("sync", vt, vC_d[g4].rearrange("(n p) x -> p n x", p=128), self.dram_rd, [("vt",)])
            self.vmask(vt[:, 0:2, :], vt[:, 18:20, :])
            for ci in range(2):
                for par in range(2):
                    head = 4 * g4 + 2 * ci + par
                    e = head % 2
                    B.dma("sync", bst[e], bias_d[head], [], [("bst", e)])
                    B.act(ed[e], bst[e], AF.Exp, [("bst", e)], [("ed", e)])
                    for off, n, i0 in ((0, 5, 1), (5, 6, 1), (11, 5, 1), (16, 5, 1), (21, 6, 0)):
                        B.tt("vector", ecls[e][:, off * 128:(off + n) * 128], ed[e][:, i0 * 128:(i0 + n) * 128],
                             qm[:, off * 128:(off + n) * 128], ALU.mult, [("ed", e), ("qm",)], [("ecls", e, off)])
                    hs = slice(par * 64, par * 64 + 64)
                    vs = slice(ci * 192 + par * 64, ci * 192 + par * 64 + 128)
                    for t in range(NTT):
                        ob, ok = self.obank()
                        for u in range(4):
                            m = t * 4 + u
                            off, n, o0 = self.C_CLS.get(m, (0, 5, 0))
                            rd = [("kt", ci), ("vt",), ("qt", ci), ("ecls", e, off)]
                            j0 = m + o0
                            self.attn_unit(qt[ci][hs, m * 128:(m + 1) * 128],
                                           [kt[ci][hs, (j0 + i) * 128:(j0 + i + 1) * 128] for i in range(n)],
                                           [vt[:, j0 + i, vs] for i in range(n)],
                                           ecls[e][:, off * 128:(off + n) * 128], ob[:, u * 128:(u + 1) * 128], ok, rd)
                        self.normalize(ob, ob, par, mixed[:, 2 * g4 + ci, tsl(t)], [ok], [("xn", 2 * g4 + ci, t)])


def build_segment(kind, part):
    nc = bass.Bass("TRN2", target_bir_lowering=False)
    es = ExitStack()
    M = Mixer(nc, es)
    hT_in = M.dram_in("hT_in", [D, NT])
    hT_out = M.dram_out("hT_out", [D, NT])
    M.setup_common()
    gains = M.load_gains("gains", 3)
    hv = hT_in.rearrange("(c p) t -> p c t", p=128)
    for c in range(NC8):
        M.dma("sync", M.h[:, c, :], hv[:, c, :], [], [("h", c, t) for t in range(NTT)])
    nfm = 20 if kind == "ab" else 16
    nvg = 4
    if part == "pre":
        wg = M.dram_in("wg", [D, DFF]); wu = M.dram_in("wu", [D, DFF]); wd = M.dram_in("wd", [DFF, D])
        w_in = M.dram_in("w_in", [D, nfm * 128 + nvg * 256])
        qk_out = M.dram_out("qk_out", [nfm, 128, NT], BF16)
        v_out = M.dram_out("v_out", [nvg, NT, 384], BF16)
        M.setup_mixer_consts(kind == "ab")
        M.rmsnorm(gains, 0, "gains")
        M.ffn(wg, wu, wd, "f1")
        M.rmsnorm(gains, 1, "gains")
        if kind == "ab":
            fm = [(True, 1)] * 8 + [(True, B_DIL[g]) for g in range(3) for _ in range(2)] * 2
            M.inproj(w_in, fm, [1, 1, 4, 16], qk_out, v_out)
        else:
            M.inproj(w_in, [(False, 1)] * 16, [1, 1, 1, 1], qk_out, v_out)
        outs = []
    else:
        qk_in = M.dram_in("qk_in", [nfm, 128, NT], BF16)
        w_out = M.dram_in("w_out", [D, D])
        wg = M.dram_in("wg", [D, DFF]); wu = M.dram_in("wu", [D, DFF]); wd = M.dram_in("wd", [DFF, D])
        wpg = M.dram_in("wpg", [D, D]); wpp = M.dram_in("wpp", [256, D]); pT = M.dram_in("pT", [256, NT])
        M.setup_mixer_consts(False)
        if kind == "ab":
            kA = M.dram_in("kA", [2, 128, NT + 256], BF16)
            vA = M.dram_in("vA", [NT + 256, 384], BF16)
            kB = [M.dram_in(f"kB{g}", [2, 128, NT + 128 * d], BF16) for g, d in enumerate(B_DIL)]
            vB = [M.dram_in(f"vB{g}", [NT + 128 * d, 384], BF16) for g, d in enumerate(B_DIL)]
            sink = M.dram_in("sink", [128, 12])
            M.attn_ab(qk_in, kA, vA, kB, vB, sink)
        else:
            kC = M.dram_in("kC", [4, 2, 128, NT + 512], BF16)
            vC = M.dram_in("vC", [4, NT + 512, 384], BF16)
            bias = M.dram_in("bias", [16, 128, 896])
            qm = M.dram_in("qm", [128, 27 * 128])
            M.attn_c(qk_in, kC, vC, bias, qm)
        M.outproj(w_out)
        M.rmsnorm(gains, 0, "gains")
        M.ffn(wg, wu, wd, "f2")
        M.rmsnorm(gains, 1, "gains")
        M.ple(wpg, wpp, pT)
    ov = hT_out.rearrange("(c p) t -> p c t", p=128)
    for t in range(NTT):
        o = M.dma("sync", ov[:, :, tsl(t)], M.h[:, :, tsl(t)], [("h", c, t) for c in range(NC8)], [("hout", t)])
        M.out_dmas.append(o)
    if part == "post" and kind == "c":
        fin = M.dram_out("finT", [D, NT])
        M.rmsnorm(gains, 2, "gains", out_f32_dram=fin.rearrange("(c p) t -> p c t", p=128))
    M.S.finalize(es, M.out_dmas)
    with nc.Block() as block:
        M.S.emit(block, final_waits=M.out_dmas)
    return nc, es


BF = ml_dtypes.bfloat16
A_ORDER = [0, 3, 1, 4, 2, 5, 6, 9, 7, 10, 8, 11]
NCORES = 8


def perm_w_in_ab(w):
    qa = w[:, 0:768].reshape(D, 12, 64)[:, A_ORDER].reshape(D, 768)
    return np.ascontiguousarray(np.concatenate(
        [qa, w[:, 768:1024], w[:, 1280:2048], w[:, 2048:2816], w[:, 1024:1280], w[:, 2816:3584]], axis=1))


def perm_w_out_ab(w):
    a = w[0:768].reshape(12, 64, D)[A_ORDER].reshape(768, D)
    return np.ascontiguousarray(np.concatenate([a, w[768:1024]], axis=0))


def const_tables():
    k = np.arange(128)[:, None]
    q = np.arange(128)[None, :]
    L = (k >= q).astype(np.float32)
    U = (k <= q).astype(np.float32)
    cmask = np.concatenate([L, np.ones((128, 128), np.float32), U, L, U], axis=1)
    pswap = (np.arange(128)[:, None] == (np.arange(128)[None, :] ^ 32)).astype(np.float32)
    return np.ascontiguousarray(cmask), np.ascontiguousarray(pswap)


def rope_table(core):
    pos = ((core % 4) * NT + np.arange(NT)).astype(np.float32)
    dd = np.arange(128) % 64
    inv = (10000.0 ** (-(dd % 32).astype(np.float32) / 32)).astype(np.float32)
    ang = pos[None, :] * inv[:, None]
    sign = np.where(dd < 32, -1.0, 1.0).astype(np.float32)[:, None]
    return np.ascontiguousarray(np.stack([np.cos(ang), np.sin(ang) * sign], axis=1).astype(np.float32))


def nbrs(core):
    pos = core % 4
    return (core - 1 if pos > 0 else None), (core + 1 if pos < 3 else None)


def halo_cols(arrs, core, hw):
    own = arrs[core]
    pv, nx = nbrs(core)
    z = np.zeros(own.shape[:-1] + (hw,), own.dtype)
    left = arrs[pv][..., -hw:] if pv is not None else z
    right = arrs[nx][..., :hw] if nx is not None else z
    return np.ascontiguousarray(np.concatenate([left, own, right], axis=-1))


def halo_rows(arrs, core, hw, axis):
    own = arrs[core]
    pv, nx = nbrs(core)
    zshape = list(own.shape)
    zshape[axis] = hw
    z = np.zeros(zshape, own.dtype)
    sl_l = [slice(None)] * own.ndim
    sl_l[axis] = slice(-hw, None)
    sl_r = [slice(None)] * own.ndim
    sl_r[axis] = slice(0, hw)
    left = arrs[pv][tuple(sl_l)] if pv is not None else z
    right = arrs[nx][tuple(sl_r)] if nx is not None else z
    return np.ascontiguousarray(np.concatenate([left, own, right], axis=axis))


def exchange_ab(qk, v):
    res = []
    kAs = [qk[c][6:8] for c in range(NCORES)]
    vAs = [v[c][0] for c in range(NCORES)]
    for c in range(NCORES):
        m = {"kA": halo_cols(kAs, c, 128), "vA": halo_rows(vAs, c, 128, 0)}
        res.append(m)
    for g, d in enumerate(B_DIL):
        T = NT // d
        kBs = [qk[c][14 + 2 * g:16 + 2 * g].reshape(2, 128, d, T) for c in range(NCORES)]
        vBs = [v[c][1 + g].reshape(d, T, 384) for c in range(NCORES)]
        for c in range(NCORES):
            res[c][f"kB{g}"] = halo_cols(kBs, c, 64).reshape(2, 128, d * (T + 128))
            res[c][f"vB{g}"] = halo_rows(vBs, c, 64, 1).reshape(d * (T + 128), 384)
    return res


def exchange_c(qk, v):
    res = []
    kCs = [qk[c][8:16].reshape(4, 2, 128, NT) for c in range(NCORES)]
    vCs = [v[c] for c in range(NCORES)]
    for c in range(NCORES):
        res.append({"kC": halo_cols(kCs, c, 256), "vC": halo_rows(vCs, c, 256, 1)})
    return res


def c_bias_table(rpb):
    a = np.arange(2)[:, None, None, None, None]
    jp = np.arange(64)[None, :, None, None, None]
    di = np.arange(7)[None, None, :, None, None]
    b = np.arange(2)[None, None, None, :, None]
    j = np.arange(64)[None, None, None, None, :]
    dr = (2 * di - 6) + a - b + 7
    dc = jp - j + 15
    cs = np.clip(j - 8, 0, 48)
    valid = (dr >= 0) & (dr <= 14) & (jp >= cs) & (jp < cs + 16)
    valid = np.broadcast_to(valid, (2, 64, 7, 2, 64))
    drc = np.broadcast_to(np.clip(dr, 0, 14), valid.shape)
    dcc = np.broadcast_to(np.clip(dc, 0, 30), valid.shape)
    g = rpb[:, drc, dcc]
    g = np.where(valid[None], g, np.float32(-30000.0)).astype(np.float32)
    return np.ascontiguousarray(g.reshape(16, 128, 896))


def c_qmask(core):
    R0 = (core % 4) * 32
    out = np.zeros((2, 64, 27, 2, 64), np.float32)
    cls = [(0, 5, 0, 4), (5, 6, 0, 0), (11, 5, 0, 1), (16, 5, 0, 14), (21, 6, -1, 15)]
    for off, n, o0, m in cls:
        for i in range(n):
            for a in range(2):
                for b in range(2):
                    if off == 0:
                        ok = 0 <= 2 * (o0 + i) + a - b <= 7
                    else:
                        r = R0 + 2 * m + b
                        kr = R0 + 2 * (m + o0 + i) - 4 + a
                        rs = min(max(r - 4, 0), 120)
                        ok = (0 <= kr < 128) and (rs <= kr < rs + 8)
                    if ok:
                        out[a, :, off + i, b, :] = 1.0
    return np.ascontiguousarray(out.reshape(128, 27 * 128))


_PROG_CACHE = {}


def get_prog(kind, part):
    key = (kind, part)
    if key not in _PROG_CACHE:
        _PROG_CACHE[key] = build_segment(kind, part)
    return _PROG_CACHE[key][0]


def kernel_impl_unfused(x, p, norm_ffn1, ffn1_w_gate, ffn1_w_up, ffn1_w_down, norm_mix, w_in_ab, sink_a,
           w_out_ab, w_in_c, rpb_c, w_out_c, norm_ffn2, ffn2_w_gate, ffn2_w_up, ffn2_w_down,
           norm_ple, w_ple_gate, w_ple_proj, norm_final, _debug=None):
    f32 = lambda a: np.ascontiguousarray(np.asarray(a, dtype=np.float32))
    x = f32(x); p = f32(p)
    cores = list(range(NCORES))
    hT = [np.ascontiguousarray(x[c // 4, (c % 4) * NT:(c % 4 + 1) * NT, :].T) for c in cores]
    cmask, pswap = const_tables()
    ropes = [rope_table(c) for c in cores]
    fin = None
    for li in range(4):
        kind = "ab" if li % 2 == 0 else "c"
        j = li // 2
        g_pre = gains_layout(np.stack([f32(norm_ffn1)[li], f32(norm_mix)[li], f32(norm_mix)[li]]))
        w_in = perm_w_in_ab(f32(w_in_ab)[j]) if kind == "ab" else f32(w_in_c)[j]
        wg, wu, wd = f32(ffn1_w_gate)[li], f32(ffn1_w_up)[li], f32(ffn1_w_down)[li]
        in_maps = []
        for c in cores:
            m = {"hT_in": hT[c], "gains": g_pre, "wg": wg, "wu": wu, "wd": wd, "w_in": w_in, "cmask": cmask}
            if kind == "ab":
                m["pswap"] = pswap
                m["ropecs"] = ropes[c]
            in_maps.append(m)
        res = run_bass_kernel_spmd(get_prog(kind, "pre"), in_maps, core_ids=cores).results
        hT = [res[c]["hT_out"] for c in cores]
        qk = [res[c]["qk_out"] for c in cores]
        v = [res[c]["v_out"] for c in cores]
        if _debug is not None:
            _debug(f"pre{li}", hT)
        ex = exchange_ab(qk, v) if kind == "ab" else exchange_c(qk, v)
        g_post = gains_layout(np.stack([f32(norm_ffn2)[li], f32(norm_ple)[li], f32(norm_final)]))
        wg, wu, wd = f32(ffn2_w_gate)[li], f32(ffn2_w_up)[li], f32(ffn2_w_down)[li]
        w_out = perm_w_out_ab(f32(w_out_ab)[j]) if kind == "ab" else f32(w_out_c)[j]
        if kind == "c":
            bias = c_bias_table(f32(rpb_c)[j])
        else:
            sink = np.ascontiguousarray(np.broadcast_to(f32(sink_a)[j][None, :], (128, 12)))
        in_maps = []
        for c in cores:
            pT = np.ascontiguousarray(p[li, c // 4, (c % 4) * NT:(c % 4 + 1) * NT, :].T)
            m = {"hT_in": hT[c], "gains": g_post, "wg": wg, "wu": wu, "wd": wd, "qk_in": qk[c], "w_out": w_out,
                 "wpg": f32(w_ple_gate)[li], "wpp": f32(w_ple_proj)[li], "pT": pT, "cmask": cmask}
            m.update(ex[c])
            if kind == "c":
                m["bias"] = bias
                m["qm"] = c_qmask(c)
            else:
                m["sink"] = sink
            in_maps.append(m)
        res = run_bass_kernel_spmd(get_prog(kind, "post"), in_maps, core_ids=cores).results
        hT = [res[c]["hT_out"] for c in cores]
        if kind == "c":
            fin = [res[c]["finT"] for c in cores]
        if _debug is not None:
            _debug(f"post{li}", hT)
    out = np.empty((2, 4 * NT, D), np.float32)
    for c in cores:
        out[c // 4, (c % 4) * NT:(c % 4 + 1) * NT, :] = fin[c].T
    return out


NR_X = 55680


def xb_layout(kind):
    lay = {}
    r = 0
    if kind == "ab":
        items = [("kA", 2 * 128 * 18)] + [(f"kB{g}", 2 * 128 * (NT + 128 * d) // 128) for g, d in enumerate(B_DIL)]
        items += [("vA", (NT + 256) * 3)] + [(f"vB{g}", (NT + 128 * d) * 3) for g, d in enumerate(B_DIL)]
    else:
        items = [("kC", 8 * 128 * 20), ("vC", 4 * (NT + 512) * 3)]
    for n, nr in items:
        lay[n] = (r, nr)
        r += nr
    assert r <= NR_X
    return lay


def xb_views(kind, base):
    lay = xb_layout(kind)
    v = {}
    for n, (r0, nr) in lay.items():
        ap = base(r0, nr)
        if n == "kA":
            v[n] = ap.rearrange("(c p a) b -> c p (a b)", c=2, p=128)
        elif n.startswith("kB"):
            v[n] = ap.rearrange("(c p a) b -> c p (a b)", c=2, p=128)
        elif n == "kC":
            v[n] = ap.rearrange("(g c p a) b -> g c p (a b)", g=4, c=2, p=128)
        elif n == "vC":
            v[n] = ap.rearrange("(g r a) b -> g r (a b)", g=4, a=3)
        else:
            v[n] = ap.rearrange("(r a) b -> r (a b)", a=3)
    return v


def build_fused(nlayers=4, dbg_h=False, no_cc=False):
    nc = bass.Bass("TRN2", target_bir_lowering=False)
    es = ExitStack()
    M = Mixer(nc, es)
    xT = M.dram_in("xT", [D, NT])
    finT = M.dram_out("finT", [D, NT])
    M.setup_common()
    gains = M.load_gains("gains", 17)
    valid_d = M.dram_in("valid", [128, 2])
    M.valid = M.sb("valid_sb", [128, 2], F32)
    M.dma("sync", M.valid[:], valid_d[:], [], [("valid",)])
    hv = xT.rearrange("(c p) t -> p c t", p=128)
    for c in range(NC8):
        M.dma("sync", M.h[:, c, :], hv[:, c, :], [], [("h", c, t) for t in range(NTT)])
    M.setup_mixer_consts(True)
    M._rank = {}
    for li in range(nlayers):
        kind = "ab" if li % 2 == 0 else "c"
        nfm = 20 if kind == "ab" else 16
        L = f"_{li}"
        wg1 = M.dram_in("wg1" + L, [D, DFF]); wu1 = M.dram_in("wu1" + L, [D, DFF]); wd1 = M.dram_in("wd1" + L, [DFF, D])
        wg2 = M.dram_in("wg2" + L, [D, DFF]); wu2 = M.dram_in("wu2" + L, [D, DFF]); wd2 = M.dram_in("wd2" + L, [DFF, D])
        w_in = M.dram_in("w_in" + L, [D, nfm * 128 + 1024])
        w_out = M.dram_in("w_out" + L, [D, D])
        wpg = M.dram_in("wpg" + L, [D, D]); wpp = M.dram_in("wpp" + L, [256, D]); pT = M.dram_in("pT" + L, [256, NT])
        q_dram = nc.dram_tensor("q_dram" + L, [nfm, 128, NT], BF16).ap()
        xb = nc.dram_tensor("xb" + L, [NR_X, 128], BF16)
        xv = xb_views(kind, lambda r0, nr: xb.ap()[r0:r0 + nr, :])
        lay = xb_layout(kind)

        M.rmsnorm(gains, 4 * li + 0, "gains")
        M.ffn(wg1, wu1, wd1, "f1")
        M.rmsnorm(gains, 4 * li + 1, "gains")
        if kind == "ab":
            def qk_dst(ci, xv=xv, q_dram=q_dram):
                if 6 <= ci < 8:
                    return xv["kA"][ci - 6][:, 128:128 + NT]
                if ci >= 14:
                    g, sp = (ci - 14) // 2, (ci - 14) % 2
                    d = B_DIL[g]
                    T = NT // d
                    kv = xv[f"kB{g}"][sp]
                    if d == 1:
                        return kv[:, 64:64 + T]
                    return kv.rearrange("p (c t) -> p c t", c=d)[:, :, 64:64 + T]
                return q_dram[ci]

            def v_dst(vg, d, blk0, xv=xv):
                if vg == 0:
                    return xv["vA"][128 + blk0 * 128:128 + blk0 * 128 + 512, :].rearrange("(j p) x -> p j x", p=128)
                g = vg - 1
                T = NT // d
                nb = T // 128
                if nb >= 4:
                    c, tb0 = blk0 // nb, blk0 % nb
                    r = c * (T + 128) + 64 + tb0 * 128
                    return xv[f"vB{g}"][r:r + 512, :].rearrange("(j p) x -> p j x", p=128)
                return xv[f"vB{g}"].rearrange("(c t) x -> c t x", c=d)[blk0:blk0 + 4, 64:192, :].rearrange(
                    "j p x -> p j x")
            fm = [(True, 1)] * 8 + [(True, B_DIL[g]) for g in range(3) for _ in range(2)] * 2
            M.inproj(w_in, fm, [1, 1, 4, 16], qk_dst, v_dst)
        else:
            def qk_dst(ci, xv=xv, q_dram=q_dram):
                if ci >= 8:
                    return xv["kC"][(ci - 8) // 2, (ci - 8) % 2][:, 256:256 + NT]
                return q_dram[ci]

            def v_dst(vg, d, blk0, xv=xv):
                return xv["vC"][vg][256 + blk0 * 128:256 + blk0 * 128 + 512, :].rearrange("(j p) x -> p j x", p=128)
            M.inproj(w_in, [(False, 1)] * 16, [1, 1, 1, 1], qk_dst, v_dst)

        wr_keys = [("qk_out", ci) for ci in range(nfm)] + [("v_out", vg, j) for vg in range(4) for j in range(4)]
        NES = 14720
        edge_send = nc.dram_tensor("edge_send" + L, [2 * NES, 64], BF16)
        gath_e = nc.dram_tensor("gath_e" + L, [NCORES * 2 * NES, 64], BF16)
        edge_nb = nc.dram_tensor("edge_nb" + L, [2 * NES, 64], BF16)
        fills = []
        sends = []
        eoff = [0]
        pending = []

        def edge_view(t, side, name, r0):
            if name == "kA":
                nr, pat, kw = 512, "(c p a) b -> c p (a b)", dict(c=2, p=128)
            elif name.startswith("kB"):
                d = B_DIL[int(name[2])]
                nr, pat, kw = 256 * d, "(c p k) b -> c p k b", dict(c=2, p=128)
            elif name == "kC":
                nr, pat, kw = 4096, "(g c p a) b -> g c p (a b)", dict(g=4, c=2, p=128)
            elif name == "vA":
                nr, pat, kw = 768, "(r a) b -> r (a b)", dict(a=6)
            elif name.startswith("vB"):
                d = B_DIL[int(name[2])]
                nr, pat, kw = 384 * d, "(k t a) b -> k t (a b)", dict(k=d, a=6)
            else:
                nr, pat, kw = 6144, "(g r a) b -> g r (a b)", dict(g=4, a=6)
            base = side * NES + r0
            return t.ap()[base:base + nr, :].rearrange(pat, **kw), nr

        def fill(name, sel, side):
            own_region, halo_region = sel(xv[name], side, False)
            if side == 0:
                pending.append((name, sel))
            r0 = sum(edge_view(edge_send, 0, n_, 0)[1] for n_, _ in pending[:[n_ for n_, _ in pending].index(name)])
            sv, _ = edge_view(edge_send, side, name, r0)
            nv, _ = edge_view(edge_nb, side, name, r0)
            k1 = ("esend", li, name, side)
            M.dma("sync", sv, own_region, wr_keys, [k1])
            sends.append(k1)
            fills.append((name, side, halo_region, nv))

        def sel_cols(hw, own):
            def f(v, side, is_src):
                if side == 0:
                    return v[:, :, own:own + hw], v[:, :, 0:hw]
                return v[:, :, hw:2 * hw], v[:, :, hw + own:hw + own + hw]
            return f

        def sel_rows(hw, own):
            def f(v, side, is_src):
                if side == 0:
                    return v[own:own + hw, :], v[0:hw, :]
                return v[hw:2 * hw, :], v[hw + own:hw + own + hw, :]
            return f

        if kind == "ab":
            for side in range(2):
                fill("kA", sel_cols(128, NT), side)
                fill("vA", sel_rows(128, NT), side)
                for g, d in enumerate(B_DIL):
                    T = NT // d

                    def selk(v, side_, is_src, d=d, T=T):
                        vv = v.rearrange("c p (k t) -> c p k t", k=d)
                        if side_ == 0:
                            return vv[:, :, :, T:T + 64], vv[:, :, :, 0:64]
                        return vv[:, :, :, 64:128], vv[:, :, :, T + 64:T + 128]

                    def selv(v, side_, is_src, d=d, T=T):
                        vv = v.rearrange("(k t) x -> k t x", k=d)
                        if side_ == 0:
                            return vv[:, T:T + 64, :], vv[:, 0:64, :]
                        return vv[:, 64:128, :], vv[:, T + 64:T + 128, :]
                    fill(f"kB{g}", selk, side)
                    fill(f"vB{g}", selv, side)
        else:
            for side in range(2):
                def selkc(v, side_, is_src):
                    if side_ == 0:
                        return v[:, :, :, NT:NT + 256], v[:, :, :, 0:256]
                    return v[:, :, :, 256:512], v[:, :, :, 256 + NT:512 + NT]

                def selvc(v, side_, is_src):
                    if side_ == 0:
                        return v[:, NT:NT + 256, :], v[:, 0:256, :]
                    return v[:, 256:512, :], v[:, 256 + NT:512 + NT, :]
                fill("kC", selkc, side)
                fill("vC", selvc, side)
        if not no_cc:
            M.S.add("gpsimd", lambda e, edge_send=edge_send, gath_e=gath_e: e.collective_compute(
                "AllGather", ALU.bypass, replica_groups=[list(range(NCORES))],
                ins=[edge_send.ap()[:, :]], outs=[gath_e.ap()[:, :]]), sends, [("gath", li)], cc=True)

        def dyn(side, gath_e=gath_e, edge_nb=edge_nb, NES=NES):
            def fn(e):
                if "pid" not in M._rank:
                    M._rank["pid"] = e.partition_id()
                pid = M._rank["pid"]
                rank = (pid + 7) % NCORES if side == 0 else (pid + 1) % NCORES
                src = gath_e.ap()[bass.ds(rank * (2 * NES) + side * NES, NES), :]
                return e.dma_start(out=edge_nb.ap()[side * NES:(side + 1) * NES, :], in_=src)
            return fn
        for side in range(2):
            M.S.add("gpsimd", dyn(side), [("gath", li)], [("enb", li, side)], dma=True)
        fkeys = []
        for name, side, halo_region, nv in fills:
            k2 = ("fill", li, name, side)
            M.dma("sync", halo_region, nv, [("enb", li, side)], [k2])
            fkeys.append(k2)
        M.dram_rd = fkeys + wr_keys

        if kind == "ab":
            sink = M.dram_in("sink" + L, [128, 12])
            M.attn_ab(q_dram, xv["kA"], xv["vA"], [xv[f"kB{g}"] for g in range(3)],
                      [xv[f"vB{g}"] for g in range(3)], sink)
        else:
            bias = M.dram_in("bias" + L, [16, 128, 896])
            if li == 1:
                M.qm_d = M.dram_in("qm", [128, 27 * 128])
            M.attn_c(q_dram, xv["kC"], xv["vC"], bias, M.qm_d)
        M.outproj(w_out)
        M.rmsnorm(gains, 4 * li + 2, "gains")
        M.ffn(wg2, wu2, wd2, "f2")
        M.rmsnorm(gains, 4 * li + 3, "gains")
        M.ple(wpg, wpp, pT)
    if dbg_h:
        ov = finT.rearrange("(c p) t -> p c t", p=128)
        for t in range(NTT):
            o = M.dma("sync", ov[:, :, tsl(t)], M.h[:, :, tsl(t)], [("h", c, t) for c in range(NC8)], [("hout", t)])
            M.out_dmas.append(o)
    else:
        M.rmsnorm(gains, 16, "gains", out_f32_dram=finT.rearrange("(c p) t -> p c t", p=128))
    M.S.finalize(es, M.out_dmas)
    with nc.Block() as block:
        M.S.emit(block, final_waits=M.out_dmas)
    return nc, es


def kernel_unfused(*a, **k):
    return kernel_impl_unfused(*a, **k)


_FUSED = {}


def kernel(x, p, norm_ffn1, ffn1_w_gate, ffn1_w_up, ffn1_w_down, norm_mix, w_in_ab, sink_a,
           w_out_ab, w_in_c, rpb_c, w_out_c, norm_ffn2, ffn2_w_gate, ffn2_w_up, ffn2_w_down,
           norm_ple, w_ple_gate, w_ple_proj, norm_final, _nlayers=4, _dbg_h=False, _no_cc=False):
    f32 = lambda a: np.ascontiguousarray(np.asarray(a, dtype=np.float32))
    x = f32(x); p = f32(p)
    cores = list(range(NCORES))
    if "nc" not in _FUSED:
        _FUSED["nc"], _FUSED["es"] = build_fused(_nlayers, _dbg_h, _no_cc)
    cmask, pswap = const_tables()
    glist = []
    for li in range(4):
        glist += [f32(norm_ffn1)[li], f32(norm_mix)[li], f32(norm_ffn2)[li], f32(norm_ple)[li]]
    glist.append(f32(norm_final))
    shared = {"gains": gains_layout(np.stack(glist)), "cmask": cmask, "pswap": pswap}
    for li in range(_nlayers):
        L = f"_{li}"
        j = li // 2
        shared["wg1" + L] = f32(ffn1_w_gate)[li]; shared["wu1" + L] = f32(ffn1_w_up)[li]; shared["wd1" + L] = f32(ffn1_w_down)[li]
        shared["wg2" + L] = f32(ffn2_w_gate)[li]; shared["wu2" + L] = f32(ffn2_w_up)[li]; shared["wd2" + L] = f32(ffn2_w_down)[li]
        shared["wpg" + L] = f32(w_ple_gate)[li]; shared["wpp" + L] = f32(w_ple_proj)[li]
        if li % 2 == 0:
            shared["w_in" + L] = perm_w_in_ab(f32(w_in_ab)[j])
            shared["w_out" + L] = perm_w_out_ab(f32(w_out_ab)[j])
            shared["sink" + L] = np.ascontiguousarray(np.broadcast_to(f32(sink_a)[j][None, :], (128, 12)))
        else:
            shared["w_in" + L] = f32(w_in_c)[j]
            shared["w_out" + L] = f32(w_out_c)[j]
            shared["bias" + L] = c_bias_table(f32(rpb_c)[j])
    in_maps = []
    for c in cores:
        b, q = c // 4, c % 4
        m = dict(shared)
        m["xT"] = np.ascontiguousarray(x[b, q * NT:(q + 1) * NT, :].T)
        m["ropecs"] = rope_table(c)
        if _nlayers > 1:
            m["qm"] = c_qmask(c)
        m["valid"] = np.ascontiguousarray(np.broadcast_to(
            np.array([[1.0 if q > 0 else 0.0, 1.0 if q < 3 else 0.0]], np.float32), (128, 2)))
        for li in range(_nlayers):
            m[f"pT_{li}"] = np.ascontiguousarray(p[li, b, q * NT:(q + 1) * NT, :].T)
        in_maps.append(m)
    res = run_bass_kernel_spmd(_FUSED["nc"], in_maps, core_ids=cores).results
    out = np.empty((2, 4 * NT, D), np.float32)
    for c in cores:
        out[c // 4, (c % 4) * NT:(c % 4 + 1) * NT, :] = res[c]["finT"].T
    return out
```

```python
import numpy as np
import ml_dtypes
from contextlib import ExitStack
import concourse.bass as bass
import concourse.mybir as mybir
from concourse.bass_utils import run_bass_kernel_spmd

F32 = mybir.dt.float32
BF16 = mybir.dt.bfloat16
AF = mybir.ActivationFunctionType
ALU = mybir.AluOpType

D = 1024
NT = 2048
TT = 512
NTT = NT // TT
DFF = 2816
NF = DFF // 128
NC8 = D // 128
EPS = 1e-6


class Op:
    __slots__ = ("stream", "fn", "dma", "cc", "deps", "signal", "sem", "val", "idx", "waits", "know")

    def __init__(self, stream, fn, dma, cc=False):
        self.stream = stream
        self.fn = fn
        self.dma = dma
        self.cc = cc
        self.deps = []
        self.signal = False
        self.sem = None
        self.val = 0
        self.waits = None
        self.know = None


class Sched:
    STREAMS = ("tensor", "vector", "scalar", "gpsimd", "sync")
    SEM_LIMIT = 20000
    NDMA = 12

    def __init__(self, nc):
        self.nc = nc
        self.ops = []
        self.last_w = {}
        self.readers = {}

    def add(self, stream, fn, reads=(), writes=(), dma=False, cc=False):
        reads = list(reads) + [("arena",)]
        op = Op(stream, fn, dma or cc, cc)
        op.idx = len(self.ops)
        deps = set()
        for r in reads:
            w = self.last_w.get(r)
            if w is not None:
                deps.add(w)
        for w_ in writes:
            w = self.last_w.get(w_)
            if w is not None:
                deps.add(w)
            for rd in self.readers.get(w_, ()):
                deps.add(rd)
        for r in reads:
            self.readers.setdefault(r, []).append(op)
        for w_ in writes:
            self.last_w[w_] = op
            self.readers[w_] = []
        deps.discard(op)
        for d in deps:
            if d.stream == "tensor" and stream == "tensor" and not d.dma and not dma:
                continue
            op.deps.append(d)
            d.signal = True
        self.ops.append(op)
        return op

    def finalize(self, es, final_dma_ops=()):
        nc = self.nc
        for o in final_dma_ops:
            o.signal = True
        cur_sem = {}
        cur_cnt = {}
        dma_sems = {}
        dma_cnt = {}
        dma_prev = {}
        nsem = [0]

        def new_sem(tag):
            nsem[0] += 1
            return es.enter_context(nc.semaphore(f"s_{tag}_{nsem[0]}"))

        know = {s: {} for s in self.STREAMS}

        def merge(kn, other):
            for s_, v_ in other.items():
                if kn.get(s_, 0) < v_:
                    kn[s_] = v_

        for op in self.ops:
            st = op.stream
            kn = know[st]
            waits = {}
            alldeps = list(op.deps)
            slot = j = None
            if op.cc:
                pass
            elif op.dma:
                if st not in dma_sems:
                    dma_sems[st] = [new_sem("d" + st) for _ in range(self.NDMA)]
                    dma_cnt[st] = 0
                    dma_prev[st] = [None] * self.NDMA
                j = dma_cnt[st]
                slot = j % self.NDMA
                prev = dma_prev[st][slot]
                if prev is not None:
                    alldeps.append(prev)
            for d in alldeps:
                assert d.sem is not None, "dep must be earlier and signalling"
                if kn.get(d.sem, 0) >= d.val:
                    continue
                if waits.get(d.sem, 0) < d.val:
                    waits[d.sem] = d.val
            for d in alldeps:
                merge(kn, d.know)
            op.waits = list(waits.items())
            if op.cc:
                op.sem = new_sem("cc")
                op.val = 1
                op.signal = True
                op.know = dict(kn)
                op.know[op.sem] = 1
            elif op.dma:
                op.sem = dma_sems[st][slot]
                op.val = 16 * (j // self.NDMA + 1)
                dma_cnt[st] = j + 1
                dma_prev[st][slot] = op
                op.signal = True
                op.know = dict(kn)
                op.know[op.sem] = op.val
            elif op.signal:
                if st not in cur_sem or cur_cnt[st] >= self.SEM_LIMIT:
                    cur_sem[st] = new_sem(st)
                    cur_cnt[st] = 0
                cur_cnt[st] += 1
                op.sem = cur_sem[st]
                op.val = cur_cnt[st]
                kn[op.sem] = op.val
                op.know = dict(kn)
        self.nsem = nsem[0]

    def emit(self, block, final_waits=()):
        nc = self.nc
        by_stream = {s: [] for s in self.STREAMS}
        for op in self.ops:
            by_stream[op.stream].append(op)

        def run(eng, ops, tail=()):
            for op in ops:
                for s_, v_ in op.waits:
                    eng.wait_ge(s_, v_)
                ins = op.fn(eng)
                if op.signal:
                    ins.then_inc(op.sem, 16 if (op.dma and not op.cc) else 1)
            for o in tail:
                eng.wait_ge(o.sem, o.val)

        @block.tensor
        def _(e):
            run(e, by_stream["tensor"])

        @block.vector
        def _(e):
            run(e, by_stream["vector"])

        @block.scalar
        def _(e):
            run(e, by_stream["scalar"])

        @block.gpsimd
        def _(e):
            run(e, by_stream["gpsimd"])

        @block.sync
        def _(e):
            run(e, by_stream["sync"], tail=final_waits)


class Builder:
    def __init__(self, nc, es):
        self.nc = nc
        self.es = es
        self.S = Sched(nc)
        self.uid = 0
        self.ps2 = [nc.alloc_psum_tensor(f"ps{i}", [128, 1024], F32) for i in range(4)]
        self.banks = [self.ps2[i // 2][:, (i % 2) * 512:(i % 2 + 1) * 512] for i in range(8)]
        self.bank_rr = 0
        self.out_dmas = []

    def sb(self, name, shape, dtype):
        return self.nc.alloc_sbuf_tensor(name, shape, dtype)

    def dram_in(self, name, shape, dtype=F32):
        return self.nc.dram_tensor(name, list(shape), dtype, kind="ExternalInput").ap()

    def dram_out(self, name, shape, dtype=F32):
        return self.nc.dram_tensor(name, list(shape), dtype, kind="ExternalOutput").ap()

    def mm(self, out, lhsT, rhs, start, stop, reads, writes):
        return self.S.add("tensor", lambda e: e.matmul(out, lhsT, rhs, start=start, stop=stop),
                          reads, writes)

    def act(self, out, in_, func, reads, writes, scale=1.0, bias=None):
        if bias is None:
            return self.S.add("scalar", lambda e: e.activation(out, in_, func, scale=scale), reads, writes)
        return self.S.add("scalar", lambda e: e.activation(out, in_, func, bias=bias, scale=scale), reads, writes)

    def tt(self, eng, out, in0, in1, op, reads, writes):
        return self.S.add(eng, lambda e: e.tensor_tensor(out, in0, in1, op), reads, writes)

    def stt(self, eng, out, in0, scalar, in1, op0, op1, reads, writes):
        return self.S.add(eng, lambda e: e.scalar_tensor_tensor(out, in0, scalar, in1, op0, op1), reads, writes)

    def ts(self, eng, out, in0, s1, s2, op0, op1, reads, writes):
        if s2 is None:
            return self.S.add(eng, lambda e: e.tensor_scalar(out, in0, s1, None, op0), reads, writes)
        return self.S.add(eng, lambda e: e.tensor_scalar(out, in0, s1, s2, op0, op1), reads, writes)

    def copy(self, eng, out, in_, reads, writes):
        if eng == "scalar":
            return self.S.add(eng, lambda e: e.copy(out, in_), reads, writes)
        return self.S.add(eng, lambda e: e.tensor_copy(out, in_), reads, writes)

    def memset(self, eng, ap, val, writes):
        return self.S.add(eng, lambda e: e.memset(ap, val), (), writes)

    def dma(self, stream, out, in_, reads, writes):
        return self.S.add(stream, lambda e: e.dma_start(out, in_), reads, writes, dma=True)


def tsl(t):
    return slice(t * TT, (t + 1) * TT)


class Model(Builder):
    def setup_common(self):
        B = self
        self.h = B.sb("h", [128, NC8, NT], F32)
        self.xn = B.sb("xn", [128, NC8, NT], BF16)
        self.ones = B.sb("ones", [128, 128], BF16)
        B.memset("vector", self.ones[:], 1.0 / D, [("ones",)])
        self.sq = [B.sb(f"sq{i}", [128, NC8, TT], BF16) for i in range(1)]
        self.rstd = [B.sb(f"rstd{i}", [128, TT], F32) for i in range(2)]
        self.rtmp = [B.sb(f"rtmp{i}", [128, TT], F32) for i in range(2)]
        self.epsc = B.sb("epsc", [128, 1], F32)
        B.memset("vector", self.epsc[:], EPS, [("epsc",)])
        self.norm_cnt = 0
        self.gu_cnt = 0
        self.dn_cnt = 0
        self.sg = [B.sb(f"sg{i}", [128, TT], F32) for i in range(2)]
        self.ARENA = 44 * 1024
        self.arena = B.sb("arena", [128, self.ARENA], BF16)
        self.aoff = 0
        self.GS = 2
        self.wbuf_cnt = 0
        self.h1_cnt = 0
        self.cur_view = None

    def phase(self, name):
        self.S.add("vector", lambda e: e.memset(self.epsc[:, 0:1], EPS), (), [("arena",), ("epsc",)])
        self.aoff = 0
        self.cur_view = name

    def take(self, shape, dtype):
        n = int(np.prod(shape))
        ne = n * 2 if dtype == F32 else n
        start = (self.aoff + 15) // 16 * 16
        assert start + ne <= self.ARENA, ("arena overflow", self.cur_view, start + ne)
        self.aoff = start + ne
        v = self.arena[:, start:start + ne]
        if dtype == F32:
            v = v.bitcast(F32)
        if len(shape) == 2:
            v = v.rearrange("p (a b) -> p a b", a=shape[0], b=shape[1])
        elif len(shape) == 3:
            v = v.rearrange("p (a b c) -> p a b c", a=shape[0], b=shape[1], c=shape[2])
        return v

    def ffn_view(self):
        self.phase("ffn")
        self.wgu = [self.take((2, NC8, self.GS * 128), BF16) for i in range(2)]
        self.wd = [self.take((self.GS, D), BF16) for i in range(3)]
        self.h1 = [self.take((self.GS, NT), BF16) for i in range(3)]

    def load_gains(self, name, n):
        g_d = self.dram_in(name, [128, n * NC8])
        g = self.sb(name + "_sb", [128, n, NC8], F32)
        self.dma("sync", g[:], g_d.rearrange("p (n c) -> p n c", c=NC8), [], [(name,)])
        return g

    def rmsnorm(self, gsb, li, gname, out_f32_dram=None):
        B = self
        h, xn = self.h, self.xn
        for t in range(NTT):
            i = self.norm_cnt % 2
            self.norm_cnt += 1
            sq, rstd = self.sq[0], self.rstd[i]
            bank = self.banks[4 + (self.dn_cnt % 4)]
            bkey = ("bank", 4 + (self.dn_cnt % 4))
            self.dn_cnt += 1
            for c in range(NC8):
                B.act(sq[:, c, :], h[:, c, tsl(t)], AF.Square, [("h", c, t)], [("sq", c)])
            for c in range(NC8):
                B.mm(bank[:], self.ones[:], sq[:, c, :], c == 0, c == NC8 - 1,
                     [("ones",), ("sq", c)], [bkey])
            B.act(self.rtmp[i][:], bank[:], AF.Sqrt, [bkey], [("rtmp", i)], bias=self.epsc[:, 0:1])
            B.S.add("vector", lambda e, o=rstd, a=self.rtmp[i]: e.reciprocal(o[:], a[:]), [("rtmp", i)], [("rstd", i)])
            for c in range(NC8):
                eng = "vector"
                if out_f32_dram is None:
                    B.stt(eng, xn[:, c, tsl(t)], h[:, c, tsl(t)], gsb[:, li, c:c + 1], rstd[:],
                          ALU.mult, ALU.mult, [("h", c, t), ("rstd", i), (gname,)], [("xn", c, t)])
                else:
                    B.stt(eng, h[:, c, tsl(t)], h[:, c, tsl(t)], gsb[:, li, c:c + 1], rstd[:],
                          ALU.mult, ALU.mult, [("h", c, t), ("rstd", i), (gname,)], [("h", c, t)])
            if out_f32_dram is not None:
                o = B.dma("sync", out_f32_dram[:, :, tsl(t)], h[:, :, tsl(t)],
                          [("h", c, t) for c in range(NC8)], [("outdram", t)])
                self.out_dmas.append(o)

    def ffn(self, wg_d, wu_d, wd_d, tag):
        B = self
        GS = self.GS
        ng = NF // GS
        h, xn = self.h, self.xn
        self.ffn_view()
        wg_v = wg_d.rearrange("(k p) f -> p k f", p=128)
        wu_v = wu_d.rearrange("(k p) f -> p k f", p=128)
        wd_v = wd_d.rearrange("(j p) o -> p j o", p=128)
        state = {}

        def GU(gi):
            wb = self.wbuf_cnt % 2
            db = self.wbuf_cnt % 3
            hb = self.h1_cnt % 3
            state[gi] = (db, hb)
            self.wbuf_cnt += 1
            self.h1_cnt += 1
            fs = slice(gi * GS * 128, (gi + 1) * GS * 128)
            B.dma("gpsimd", self.wgu[wb][:, 0], wg_v[:, :, fs], [], [("wgu", wb, 0)])
            B.dma("gpsimd", self.wgu[wb][:, 1], wu_v[:, :, fs], [], [("wgu", wb, 1)])
            B.dma("gpsimd", self.wd[db], wd_v[:, gi * GS:(gi + 1) * GS, :], [], [("wd", db)])
            for fl in range(GS):
                for t in range(NTT):
                    p = self.gu_cnt % 2
                    self.gu_cnt += 1
                    bg, bu = self.banks[2 * p], self.banks[2 * p + 1]
                    kg, ku = ("bank", 2 * p), ("bank", 2 * p + 1)
                    for which, bank, bk in ((0, bg, kg), (1, bu, ku)):
                        for k in range(NC8):
                            B.mm(bank[:], self.wgu[wb][:, which, k, fl * 128:(fl + 1) * 128],
                                 xn[:, k, tsl(t)], k == 0, k == NC8 - 1,
                                 [("wgu", wb, which), ("xn", k, t)], [bk])
                    B.act(self.sg[p][:], bg[:], AF.Silu, [kg], [("sg", p)])
                    B.tt("vector", self.h1[hb][:, fl, tsl(t)], self.sg[p][:], bu[:], ALU.mult,
                         [("sg", p), ku], [("h1", hb, fl, t)])

        def DOWN(gi):
            wb, hb = state[gi]
            for oc in range(NC8):
                for t in range(NTT):
                    bi = 4 + (self.dn_cnt % 4)
                    self.dn_cnt += 1
                    bo, bk = self.banks[bi], ("bank", bi)
                    for fl in range(GS):
                        B.mm(bo[:], self.wd[wb][:, fl, oc * 128:(oc + 1) * 128],
                             self.h1[hb][:, fl, tsl(t)], fl == 0, fl == GS - 1,
                             [("wd", wb), ("h1", hb, fl, t)], [bk])
                    B.stt("vector", h[:, oc, tsl(t)], bo[:], 0.5, h[:, oc, tsl(t)], ALU.mult, ALU.add,
                          [bk, ("h", oc, t)], [("h", oc, t)])

        for gi in range(ng + 1):
            if gi < ng:
                GU(gi)
            if gi >= 1:
                DOWN(gi - 1)


def gains_layout(g):
    n = g.shape[0]
    return np.ascontiguousarray(g.reshape(n, NC8, 128).transpose(2, 0, 1).reshape(128, n * NC8))


A_CHUNK_HEADS = [(0, 3), (1, 4), (2, 5), (6, 9), (7, 10), (8, 11)]
B_DIL = (1, 4, 16)


def _mixer_methods():
    pass


class Mixer(Model):
    def setup_mixer_consts(self, with_rope):
        B = self
        cm_d = self.dram_in("cmask", [128, 640])
        self.cmask = B.sb("cmask_sb", [128, 640], BF16)
        B.dma("gpsimd", self.cmask[:], cm_d[:], [], [("cmask",)])
        if with_rope:
            ps_d = self.dram_in("pswap", [128, 128])
            self.pswap = B.sb("pswap_sb", [128, 128], BF16)
            B.dma("gpsimd", self.pswap[:], ps_d[:], [], [("pswap",)])
            self.ropecs_d = self.dram_in("ropecs", [128, 2, NT])

    def inproj(self, w_d, fm_chunks, vdils, qk_out, v_out, first_groups=None, mid=None):
        B = self
        xn = self.xn
        self.phase("inproj")
        wslots = [self.take((NC8, 256), BF16) for _ in range(4)]
        stage = [self.take((NT,), BF16) for _ in range(3)]
        qf = [self.take((TT,), F32) for _ in range(3)]
        qb = [self.take((TT,), BF16) for _ in range(3)]
        t2 = [self.take((TT,), F32) for _ in range(3)]
        rcnt = [0]
        pend = []
        vst = [self.take((4, 384), BF16) for _ in range(2)]
        for i in range(2):
            B.memset("vector", vst[i], 1.0, [("vst", i, j) for j in range(4)])
        if any(r for r, _ in fm_chunks):
            self.ropecs = self.take((2, NT), F32)
            B.dma("sync", self.ropecs, self.ropecs_d, [], [("ropecs",)])
        wv = w_d.rearrange("(k p) f -> p k f", p=128)
        nfm = len(fm_chunks)
        ncfm = nfm * 128
        cntb = [0]
        first_groups = list(first_groups) if first_groups is not None else []
        rest = [g_ for g_ in range(nfm // 2) if g_ not in first_groups]
        items = [("fm", g_) for g_ in first_groups] + [("v", vg) for vg in range(len(vdils))] + \
                [("mid", None)] + [("fm", g_) for g_ in rest]
        loads = [it for it in items if it[0] != "mid"]
        slot_of = {}

        def issue_load(j):
            if j >= len(loads):
                return
            kind_, gi = loads[j]
            ws = j % 4
            slot_of[loads[j]] = ws
            c0 = gi * 256 if kind_ == "fm" else ncfm + gi * 256
            B.dma("gpsimd", wslots[ws], wv[:, :, c0:c0 + 256], [], [("ws", ws)])

        def do_fm(wg):
            ws = slot_of[("fm", wg)]
            for fl in range(2):
                ci = wg * 2 + fl
                rope, d = fm_chunks[ci]
                si = ci % 3
                st = stage[si]
                for t in range(NTT):
                    bi = cntb[0] % 4
                    cntb[0] += 1
                    bank, bk = self.banks[bi], ("bank", bi)
                    for k in range(NC8):
                        B.mm(bank, wslots[ws][:, k, fl * 128:(fl + 1) * 128], xn[:, k, tsl(t)],
                             k == 0, k == NC8 - 1, [("ws", ws), ("xn", k, t)], [bk])
                    if d == 1:
                        dst = st[:, tsl(t)]
                    else:
                        dst = st.rearrange("p (c t) -> p c t", c=d)[:, :, t * TT // d:(t + 1) * TT // d]

                    def nat(ap, d=d):
                        return ap if d == 1 else ap.rearrange("p (t c) -> p c t", c=d)
                    if not rope:
                        B.copy("scalar", dst, nat(bank), [bk], [("stage", si, t)])
                        continue
                    b2, bk2 = self.banks[4 + bi], ("bank", 4 + bi)
                    p = rcnt[0] % 3
                    rcnt[0] += 1
                    B.copy("scalar", qf[p], bank, [bk], [("qf", p)])
                    B.copy("scalar", qb[p], bank, [bk], [("qb", p)])

                    def tail(p=p, b2=b2, bk2=bk2, t=t, dst=dst, nat=nat, si=si):
                        B.mm(b2, self.pswap[:], qb[p], True, True, [("pswap",), ("qb", p)], [bk2])
                        B.tt("vector", qf[p], qf[p], self.ropecs[:, 0, tsl(t)], ALU.mult,
                             [("qf", p), ("ropecs",)], [("qf", p)])
                        B.tt("vector", t2[p], b2, self.ropecs[:, 1, tsl(t)], ALU.mult,
                             [bk2, ("ropecs",)], [("t2", p)])
                        B.tt("vector", dst, nat(qf[p]), nat(t2[p]), ALU.add,
                             [("qf", p), ("t2", p)], [("stage", si, t)])
                    if pend:
                        pend.pop(0)()
                    pend.append(tail)
                while pend:
                    pend.pop(0)()
                dst_ap = qk_out(ci) if callable(qk_out) else qk_out[ci]
                B.dma("sync", dst_ap, st if not callable(qk_out) else self._stage_view(st, d),
                      [("stage", si, t) for t in range(NTT)], [("qk_out", ci)])

        def do_v(vg):
            ws = slot_of[("v", vg)]
            d = vdils[vg]
            T = NT // d
            nb = T // 128
            for c in range(d):
                for tb in range(nb):
                    blk = c * nb + tb
                    vi = (blk // 4) % 2
                    bi = cntb[0] % 4
                    cntb[0] += 1
                    bank, bk = self.banks[bi], ("bank", bi)
                    t0 = c + d * tb * 128
                    for k in range(NC8):
                        if d == 1:
                            lhsT = xn[:, k, t0:t0 + 128]
                        else:
                            lhsT = xn[:, k, t0:t0 + 127 * d + 1:d]
                        B.mm(bank[:, 0:256], lhsT, wslots[ws][:, k, :], k == 0, k == NC8 - 1,
                             [("ws", ws)] + [("xn", k, t) for t in range(NTT)], [bk])
                    dstv = vst[vi][:, blk % 4, :].rearrange("p (a b c) -> p a b c", a=2, b=3, c=64)[:, :, 0:3:2, :]
                    srcv = bank[:, 0:256].rearrange("p (a b c) -> p a b c", a=2, b=2, c=64)
                    B.copy("scalar", dstv, srcv, [bk], [("vst", vi, blk % 4)])
                    if blk % 4 == 3:
                        r0 = (blk - 3) * 128
                        if callable(v_out):
                            dst_ap = v_out(vg, d, blk - 3)
                        else:
                            dst_ap = v_out[vg, r0:r0 + 512, :].rearrange("(j p) x -> p j x", p=128)
                        B.dma("sync", dst_ap, vst[vi],
                              [("vst", vi, j) for j in range(4)], [("v_out", vg, blk // 4)])

        issue_load(0)
        issue_load(1)
        issue_load(2)
        j = 0
        for it in items:
            if it[0] == "mid":
                if mid is not None:
                    mid()
                continue
            issue_load(j + 3)
            if it[0] == "fm":
                do_fm(it[1])
            else:
                do_v(it[1])
            j += 1

    @staticmethod
    def _stage_view(st, d):
        return st if d == 1 else st.rearrange("p (c t) -> p c t", c=d)

    def attn_setup(self, s_banks=1):
        self.pt = [self.take((768,), BF16) for _ in range(5)]
        self.rc = [self.take((TT,), F32) for _ in range(3)]
        self.ucnt = 0
        self.ocnt = 0
        self.pcnt = 0
        if s_banks == 1:
            self.sbufs = [(self.banks[i], ("bank", i)) for i in range(4)]
            self.obanks = [4, 5, 6, 7]
            self.LA = 3
        else:
            self.sbufs = [(self.ps2[i], ("S2", i)) for i in range(3)]
            self.obanks = [6, 7]
            self.LA = 2
        self.pend = []

    def attn_unit(self, q_ap, ktiles, vtiles, mask_ap, o_ap, okey, rd, post=None):
        B = self
        n = len(ktiles)
        si = self.ucnt % len(self.sbufs)
        pi = self.ucnt % 5
        self.ucnt += 1
        S, sk = self.sbufs[si]
        sk2 = [sk] if sk[0] == "bank" else [("bank", 2 * sk[1]), ("bank", 2 * sk[1] + 1)]
        pt, pk = self.pt[pi], ("pt", pi)
        for i, kT in enumerate(ktiles):
            B.mm(S[:, i * 128:(i + 1) * 128], kT, q_ap, True, True, rd, sk2)
        B.act(pt[:, 0:n * 128], S[:, 0:n * 128], AF.Exp, sk2, [pk], scale=0.125)
        meng = "gpsimd" if (self.ucnt % 2 == 0) else "vector"
        B.tt(meng, pt[:, 0:n * 128], pt[:, 0:n * 128], mask_ap, ALU.mult, [pk] + rd, [pk])

        def pv():
            for i, va in enumerate(vtiles):
                B.mm(o_ap, va, pt[:, i * 128:(i + 1) * 128], i == 0, i == n - 1, [pk] + rd, [okey])
            if post is not None:
                post()
        self.pend.append(pv)
        if len(self.pend) > self.LA:
            self.pend.pop(0)()

    def attn_flush(self):
        while self.pend:
            self.pend.pop(0)()

    def obank(self):
        j = self.obanks[self.ocnt % len(self.obanks)]
        self.ocnt += 1
        return self.banks[j], ("bank", j)

    def normalize(self, src_o, src_den, par, dst, rd, wr, sink_col=None):
        B = self
        oh = slice(par * 64, par * 64 + 64)
        dh = slice((1 - par) * 64, (1 - par) * 64 + 64)
        p = self.pcnt % 3
        self.pcnt += 1
        rc = self.rc[p]
        if sink_col is not None:
            B.act(rc[oh], src_den[dh], AF.Identity, rd + [("esink",)], [("rc", p)],
                  bias=self.esink[dh, sink_col:sink_col + 1])
        else:
            B.copy("scalar", rc[oh], src_den[dh], rd, [("rc", p)])
        B.S.add("vector", lambda e: e.reciprocal(rc[oh], rc[oh]), [("rc", p)], [("rc", p)])
        B.tt("vector", dst[oh], src_o[oh], rc[oh], ALU.mult, rd + [("rc", p)], wr)

    def load_group(self, kh_d, vh_d, q_srcs, klen, ntile):
        B = self
        kt = [self.take((klen,), BF16) for _ in range(2)]
        vt = self.take((ntile, 384), BF16)
        qt = [self.take((NT,), BF16) for _ in q_srcs]
        for i in range(2):
            B.dma("sync", kt[i], kh_d[i], self.dram_rd, [("kt", i)])
        B.dma("sync", vt, vh_d.rearrange("(n p) x -> p n x", p=128), self.dram_rd, [("vt",)])
        self.vmask(vt[:, 0:1, :], vt[:, ntile - 1:ntile, :])
        for i, qs in enumerate(q_srcs):
            B.dma("sync", qt[i], qs, self.dram_rd, [("qt", i)])
        return kt, vt, qt

    dram_rd = ()
    valid = None

    def vmask(self, lo, hi):
        if self.valid is None:
            return
        B = self
        B.ts("vector", lo, lo, self.valid[lo.base_partition():lo.base_partition() + lo.shape[0], 0:1], None,
             ALU.mult, None, [("vt",), ("valid",)], [("vt",)])
        B.ts("vector", hi, hi, self.valid[hi.base_partition():hi.base_partition() + hi.shape[0], 1:2], None,
             ALU.mult, None, [("vt",), ("valid",)], [("vt",)])

    def attn_ab(self, qk_d, kA_d, vA_d, kB_d, vB_d, sink_d):
        B = self
        mixed = self.xn
        self.phase("attnA")
        self.attn_setup()
        self.esink = self.take((12,), F32)
        B.dma("sync", self.esink, sink_d, [], [("esink",)])
        B.act(self.esink, self.esink, AF.Exp, [("esink",)], [("esink",)])
        kt, vt, qt = self.load_group(kA_d, vA_d, [qk_d[i] for i in range(6)], NT + 256, 18)
        maskA = self.cmask[:, 0:384]
        for qc in range(6):
            kc = qc // 3
            for par in range(2):
                head = A_CHUNK_HEADS[qc][par]
                hs = slice(par * 64, par * 64 + 64)
                vs = slice(kc * 192 + par * 64, kc * 192 + par * 64 + 128)
                rd = [("kt", kc), ("vt",), ("qt", qc), ("cmask",)]
                for t in range(NTT):
                    ob, ok = self.obank()
                    for u in range(4):
                        qblk = t * 4 + u
                        post = None
                        if u == 3:
                            post = (lambda ob=ob, ok=ok, par=par, qc=qc, t=t, head=head: self.normalize(
                                ob, ob, par, mixed[:, qc, tsl(t)], [ok], [("xn", qc, t)], sink_col=head))
                        self.attn_unit(qt[qc][hs, qblk * 128:(qblk + 1) * 128],
                                       [kt[kc][hs, (qblk + i) * 128:(qblk + i + 1) * 128] for i in range(3)],
                                       [vt[:, qblk + i, vs] for i in range(3)],
                                       maskA, ob[:, u * 128:(u + 1) * 128], ok, rd, post)
        self.attn_flush()
        for g, d in enumerate(B_DIL):
            T = NT // d
            nb = T // 128
            self.phase("attnB%d" % g)
            self.attn_setup()
            if g == 0:
                pass
            acc = [[self.take((NT,), F32) for _ in range(2)] for _ in range(2)]
            if g == 0:
                self.accB = acc
            klen, ntile = d * (T + 128), d * (nb + 1)
            ktb = self.take((klen,), BF16)
            vtb = self.take((ntile, 192), BF16)
            qtb = self.take((NT,), BF16)
            maskB = self.cmask[:, 384:640]
            for sp in range(2):
                B.dma("sync", ktb, kB_d[g][sp], self.dram_rd, [("kt", 0)])
                B.dma("sync", vtb, vB_d[g].rearrange("(n p) x -> p n x", p=128)[:, :, sp * 192:(sp + 1) * 192],
                      self.dram_rd, [("vt",)])
                vv = vtb.rearrange("p (c n) x -> p c n x", c=d)
                self.vmask(vv[0:64, :, 0, :], vv[64:128, :, nb, :])
                B.dma("sync", qtb, qk_d[8 + 2 * g + sp], self.dram_rd, [("qt", 0)])
                kt, vt, qt = [ktb, ktb], vtb, [qtb, qtb]
                for par in range(2):
                    hs = slice(par * 64, par * 64 + 64)
                    vs = slice(par * 64, par * 64 + 128)
                    rd = [("kt", 0), ("vt",), ("qt", 0), ("cmask",)]
                    a = acc[sp][par]
                    units = [(c, qb_) for c in range(d) for qb_ in range(nb)]
                    for u0 in range(0, len(units), 4):
                        ob, ok = self.obank()
                        if d == 1:
                            dst = a[:, u0 * 128:(u0 + 4) * 128]
                            src = ob
                        elif d == 4:
                            c = units[u0][0]
                            dst = a.rearrange("p (t c) -> p c t", c=4)[:, c, :]
                            src = ob
                        else:
                            c0 = units[u0][0]
                            dst = a.rearrange("p (t c) -> p c t", c=16)[:, c0:c0 + 4, :]
                            src = ob.rearrange("p (j t) -> p j t", j=4)
                        akey = ("acc", sp, par)

                        def post(dst=dst, src=src, ok=ok, akey=akey, g=g):
                            if g == 0:
                                B.copy("scalar", dst, src, [ok], [akey])
                            else:
                                B.tt("vector", dst, src, dst, ALU.add, [ok, akey], [akey])
                        for u in range(4):
                            c, qb_ = units[u0 + u]
                            kbase = c * (T + 128) + qb_ * 128
                            vbase = c * (nb + 1) + qb_
                            self.attn_unit(qt[sp][hs, c * T + qb_ * 128:c * T + (qb_ + 1) * 128],
                                           [kt[sp][hs, kbase + i * 128:kbase + (i + 1) * 128] for i in range(2)],
                                           [vt[:, vbase + i, vs] for i in range(2)],
                                           maskB, ob[:, u * 128:(u + 1) * 128], ok, rd, post if u == 3 else None)
                self.attn_flush()
            if g == 2:
                for sp in range(2):
                    for par in range(2):
                        a = acc[sp][par]
                        for t in range(NTT):
                            self.normalize(a[:, tsl(t)], a[:, tsl(t)], par, mixed[:, 6 + sp, tsl(t)],
                                           [("acc", sp, par)], [("xn", 6 + sp, t)])

    def outproj(self, w_d, scale=1.0):
        B = self
        self.phase("outproj")
        wsq = self.take((NC8, D), BF16)
        B.dma("gpsimd", wsq, w_d.rearrange("(k p) o -> p k o", p=128), [], [("wsq",)])
        for oc in range(NC8):
            for t in range(NTT):
                bi = 4 + (self.dn_cnt % 4)
                self.dn_cnt += 1
                bo, bk = self.banks[bi], ("bank", bi)
                for k in range(NC8):
                    B.mm(bo, wsq[:, k, oc * 128:(oc + 1) * 128], self.xn[:, k, tsl(t)], k == 0, k == NC8 - 1,
                         [("wsq",), ("xn", k, t)], [bk])
                B.stt("vector", self.h[:, oc, tsl(t)], bo, scale, self.h[:, oc, tsl(t)], ALU.mult, ALU.add,
                      [bk, ("h", oc, t)], [("h", oc, t)])

    def ple(self, wg_d, wp_d, pT_d):
        B = self
        self.phase("ple")
        wsq = self.take((NC8, D), BF16)
        wpp = self.take((2, D), BF16)
        pT = self.take((2, NT), BF16)
        B.dma("gpsimd", wsq, wg_d.rearrange("(k p) o -> p k o", p=128), [], [("wsq",)])
        B.dma("gpsimd", wpp, wp_d.rearrange("(k p) o -> p k o", p=128), [], [("wpp",)])
        B.dma("gpsimd", pT, pT_d.rearrange("(k p) t -> p k t", p=128), [], [("pT",)])
        cnt = 0
        for oc in range(NC8):
            for t in range(NTT):
                p = cnt % 2
                cnt += 1
                bg, kg = self.banks[2 * p], ("bank", 2 * p)
                bp, kp = self.banks[2 * p + 1], ("bank", 2 * p + 1)
                for k in range(NC8):
                    B.mm(bg, wsq[:, k, oc * 128:(oc + 1) * 128], self.xn[:, k, tsl(t)], k == 0, k == NC8 - 1,
                         [("wsq",), ("xn", k, t)], [kg])
                for k in range(2):
                    B.mm(bp, wpp[:, k, oc * 128:(oc + 1) * 128], pT[:, k, tsl(t)], k == 0, k == 1,
                         [("wpp",), ("pT",)], [kp])
                B.act(self.sg[p][:], bg, AF.Sigmoid, [kg], [("sg", p)])
                B.tt("vector", self.sg[p][:], self.sg[p][:], bp, ALU.mult, [("sg", p), kp], [("sg", p)])
                B.tt("vector", self.h[:, oc, tsl(t)], self.h[:, oc, tsl(t)], self.sg[p][:], ALU.add,
                     [("sg", p), ("h", oc, t)], [("h", oc, t)])

    C_CLS = {0: (5, 6, 0), 1: (11, 5, 0), 14: (16, 5, 0), 15: (21, 6, -1)}

    def attn_c(self, qk_d, kC_d, vC_d, bias_d, qm_d):
        B = self
        mixed = self.xn
        self.phase("attnC")
        self.attn_setup(s_banks=2)
        qm = self.take((27 * 128,), BF16)
        B.dma("gpsimd", qm, qm_d, [], [("qm",)])
        kt = [self.take((NT + 512,), BF16) for _ in range(2)]
        vt = self.take((20, 384), BF16)
        qt = [self.take((NT,), BF16) for _ in range(2)]
        bst = [self.take((896,), F32) for _ in range(2)]
        ed = [self.take((896,), BF16) for _ in range(2)]
        ecls = [self.take((27 * 128,), BF16) for _ in range(2)]
        for g4 in range(4):
            for i in range(2):
                B.dma("sync", kt[i], kC_d[g4, i], self.dram_rd, [("kt", i)])
                B.dma("sync", qt[i], qk_d[2 * g4 + i], self.dram_rd, [("qt", i)])
            B.dma("sync", vt, vC_d[g4].rearrange("(n p) x -> p n x", p=128), self.dram_rd, [("vt",)])
            self.vmask(vt[:, 0:2, :], vt[:, 18:20, :])
            for ci in range(2):
                for par in range(2):
                    head = 4 * g4 + 2 * ci + par
                    e = head % 2
                    B.dma("sync", bst[e], bias_d[head], [], [("bst", e)])
                    B.act(ed[e], bst[e], AF.Exp, [("bst", e)], [("ed", e)])
                    for off, n, i0 in ((0, 5, 1), (5, 6, 1), (11, 5, 1), (16, 5, 1), (21, 6, 0)):
                        B.tt("vector", ecls[e][:, off * 128:(off + n) * 128], ed[e][:, i0 * 128:(i0 + n) * 128],
                             qm[:, off * 128:(off + n) * 128], ALU.mult, [("ed", e), ("qm",)], [("ecls", e, off)])
                    hs = slice(par * 64, par * 64 + 64)
                    vs = slice(ci * 192 + par * 64, ci * 192 + par * 64 + 128)
                    for t in range(NTT):
                        ob, ok = self.obank()
                        for u in range(4):
                            m = t * 4 + u
                            off, n, o0 = self.C_CLS.get(m, (0, 5, 0))
                            rd = [("kt", ci), ("vt",), ("qt", ci), ("ecls", e, off)]
                            j0 = m + o0
                            post = None
                            if u == 3:
                                post = (lambda ob=ob, ok=ok, par=par, cc_=2 * g4 + ci, t=t: self.normalize(
                                    ob, ob, par, mixed[:, cc_, tsl(t)], [ok], [("xn", cc_, t)]))
                            self.attn_unit(qt[ci][hs, m * 128:(m + 1) * 128],
                                           [kt[ci][hs, (j0 + i) * 128:(j0 + i + 1) * 128] for i in range(n)],
                                           [vt[:, j0 + i, vs] for i in range(n)],
                                           ecls[e][:, off * 128:(off + n) * 128], ob[:, u * 128:(u + 1) * 128], ok, rd,
                                           post)
            self.attn_flush()


def build_segment(kind, part):
    nc = bass.Bass("TRN2", target_bir_lowering=False)
    es = ExitStack()
    M = Mixer(nc, es)
    hT_in = M.dram_in("hT_in", [D, NT])
    hT_out = M.dram_out("hT_out", [D, NT])
    M.setup_common()
    gains = M.load_gains("gains", 3)
    hv = hT_in.rearrange("(c p) t -> p c t", p=128)
    for c in range(NC8):
        M.dma("sync", M.h[:, c, :], hv[:, c, :], [], [("h", c, t) for t in range(NTT)])
    nfm = 20 if kind == "ab" else 16
    nvg = 4
    if part == "pre":
        wg = M.dram_in("wg", [D, DFF]); wu = M.dram_in("wu", [D, DFF]); wd = M.dram_in("wd", [DFF, D])
        w_in = M.dram_in("w_in", [D, nfm * 128 + nvg * 256])
        qk_out = M.dram_out("qk_out", [nfm, 128, NT], BF16)
        v_out = M.dram_out("v_out", [nvg, NT, 384], BF16)
        M.setup_mixer_consts(kind == "ab")
        M.rmsnorm(gains, 0, "gains")
        M.ffn(wg, wu, wd, "f1")
        M.rmsnorm(gains, 1, "gains")
        if kind == "ab":
            fm = [(True, 1)] * 8 + [(True, B_DIL[g]) for g in range(3) for _ in range(2)] * 2
            M.inproj(w_in, fm, [1, 1, 4, 16], qk_out, v_out)
        else:
            M.inproj(w_in, [(False, 1)] * 16, [1, 1, 1, 1], qk_out, v_out)
        outs = []
    else:
        qk_in = M.dram_in("qk_in", [nfm, 128, NT], BF16)
        w_out = M.dram_in("w_out", [D, D])
        wg = M.dram_in("wg", [D, DFF]); wu = M.dram_in("wu", [D, DFF]); wd = M.dram_in("wd", [DFF, D])
        wpg = M.dram_in("wpg", [D, D]); wpp = M.dram_in("wpp", [256, D]); pT = M.dram_in("pT", [256, NT])
        M.setup_mixer_consts(False)
        if kind == "ab":
            kA = M.dram_in("kA", [2, 128, NT + 256], BF16)
            vA = M.dram_in("vA", [NT + 256, 384], BF16)
            kB = [M.dram_in(f"kB{g}", [2, 128, NT + 128 * d], BF16) for g, d in enumerate(B_DIL)]
            vB = [M.dram_in(f"vB{g}", [NT + 128 * d, 384], BF16) for g, d in enumerate(B_DIL)]
            sink = M.dram_in("sink", [128, 12])
            M.attn_ab(qk_in, kA, vA, kB, vB, sink)
        else:
            kC = M.dram_in("kC", [4, 2, 128, NT + 512], BF16)
            vC = M.dram_in("vC", [4, NT + 512, 384], BF16)
            bias = M.dram_in("bias", [16, 128, 896])
            qm = M.dram_in("qm", [128, 27 * 128])
            M.attn_c(qk_in, kC, vC, bias, qm)
        M.outproj(w_out)
        M.rmsnorm(gains, 0, "gains")
        M.ffn(wg, wu, wd, "f2")
        M.rmsnorm(gains, 1, "gains")
        M.ple(wpg, wpp, pT)
    ov = hT_out.rearrange("(c p) t -> p c t", p=128)
    for t in range(NTT):
        o = M.dma("sync", ov[:, :, tsl(t)], M.h[:, :, tsl(t)], [("h", c, t) for c in range(NC8)], [("hout", t)])
        M.out_dmas.append(o)
    if part == "post" and kind == "c":
        fin = M.dram_out("finT", [D, NT])
        M.rmsnorm(gains, 2, "gains", out_f32_dram=fin.rearrange("(c p) t -> p c t", p=128))
    M.S.finalize(es, M.out_dmas)
    with nc.Block() as block:
        M.S.emit(block, final_waits=M.out_dmas)
    return nc, es


BF = ml_dtypes.bfloat16
A_ORDER = [0, 3, 1, 4, 2, 5, 6, 9, 7, 10, 8, 11]
NCORES = 8


def perm_w_in_ab(w):
    qa = w[:, 0:768].reshape(D, 12, 64)[:, A_ORDER].reshape(D, 768)
    return np.ascontiguousarray(np.concatenate(
        [qa, w[:, 768:1024], w[:, 1280:2048], w[:, 2048:2816], w[:, 1024:1280], w[:, 2816:3584]], axis=1))


def perm_w_out_ab(w):
    a = w[0:768].reshape(12, 64, D)[A_ORDER].reshape(768, D)
    return np.ascontiguousarray(np.concatenate([a, w[768:1024]], axis=0))


def const_tables():
    k = np.arange(128)[:, None]
    q = np.arange(128)[None, :]
    L = (k >= q).astype(np.float32)
    U = (k <= q).astype(np.float32)
    cmask = np.concatenate([L, np.ones((128, 128), np.float32), U, L, U], axis=1)
    pswap = (np.arange(128)[:, None] == (np.arange(128)[None, :] ^ 32)).astype(np.float32)
    return np.ascontiguousarray(cmask), np.ascontiguousarray(pswap)


def rope_table(core):
    pos = ((core % 4) * NT + np.arange(NT)).astype(np.float32)
    dd = np.arange(128) % 64
    inv = (10000.0 ** (-(dd % 32).astype(np.float32) / 32)).astype(np.float32)
    ang = pos[None, :] * inv[:, None]
    sign = np.where(dd < 32, -1.0, 1.0).astype(np.float32)[:, None]
    return np.ascontiguousarray(np.stack([np.cos(ang), np.sin(ang) * sign], axis=1).astype(np.float32))


def nbrs(core):
    pos = core % 4
    return (core - 1 if pos > 0 else None), (core + 1 if pos < 3 else None)


def halo_cols(arrs, core, hw):
    own = arrs[core]
    pv, nx = nbrs(core)
    z = np.zeros(own.shape[:-1] + (hw,), own.dtype)
    left = arrs[pv][..., -hw:] if pv is not None else z
    right = arrs[nx][..., :hw] if nx is not None else z
    return np.ascontiguousarray(np.concatenate([left, own, right], axis=-1))


def halo_rows(arrs, core, hw, axis):
    own = arrs[core]
    pv, nx = nbrs(core)
    zshape = list(own.shape)
    zshape[axis] = hw
    z = np.zeros(zshape, own.dtype)
    sl_l = [slice(None)] * own.ndim
    sl_l[axis] = slice(-hw, None)
    sl_r = [slice(None)] * own.ndim
    sl_r[axis] = slice(0, hw)
    left = arrs[pv][tuple(sl_l)] if pv is not None else z
    right = arrs[nx][tuple(sl_r)] if nx is not None else z
    return np.ascontiguousarray(np.concatenate([left, own, right], axis=axis))


def exchange_ab(qk, v):
    res = []
    kAs = [qk[c][6:8] for c in range(NCORES)]
    vAs = [v[c][0] for c in range(NCORES)]
    for c in range(NCORES):
        m = {"kA": halo_cols(kAs, c, 128), "vA": halo_rows(vAs, c, 128, 0)}
        res.append(m)
    for g, d in enumerate(B_DIL):
        T = NT // d
        kBs = [qk[c][14 + 2 * g:16 + 2 * g].reshape(2, 128, d, T) for c in range(NCORES)]
        vBs = [v[c][1 + g].reshape(d, T, 384) for c in range(NCORES)]
        for c in range(NCORES):
            res[c][f"kB{g}"] = halo_cols(kBs, c, 64).reshape(2, 128, d * (T + 128))
            res[c][f"vB{g}"] = halo_rows(vBs, c, 64, 1).reshape(d * (T + 128), 384)
    return res


def exchange_c(qk, v):
    res = []
    kCs = [qk[c][8:16].reshape(4, 2, 128, NT) for c in range(NCORES)]
    vCs = [v[c] for c in range(NCORES)]
    for c in range(NCORES):
        res.append({"kC": halo_cols(kCs, c, 256), "vC": halo_rows(vCs, c, 256, 1)})
    return res


def c_bias_table(rpb):
    a = np.arange(2)[:, None, None, None, None]
    jp = np.arange(64)[None, :, None, None, None]
    di = np.arange(7)[None, None, :, None, None]
    b = np.arange(2)[None, None, None, :, None]
    j = np.arange(64)[None, None, None, None, :]
    dr = (2 * di - 6) + a - b + 7
    dc = jp - j + 15
    cs = np.clip(j - 8, 0, 48)
    valid = (dr >= 0) & (dr <= 14) & (jp >= cs) & (jp < cs + 16)
    valid = np.broadcast_to(valid, (2, 64, 7, 2, 64))
    drc = np.broadcast_to(np.clip(dr, 0, 14), valid.shape)
    dcc = np.broadcast_to(np.clip(dc, 0, 30), valid.shape)
    g = rpb[:, drc, dcc]
    g = np.where(valid[None], g, np.float32(-30000.0)).astype(np.float32)
    return np.ascontiguousarray(g.reshape(16, 128, 896))


def c_qmask(core):
    R0 = (core % 4) * 32
    out = np.zeros((2, 64, 27, 2, 64), np.float32)
    cls = [(0, 5, 0, 4), (5, 6, 0, 0), (11, 5, 0, 1), (16, 5, 0, 14), (21, 6, -1, 15)]
    for off, n, o0, m in cls:
        for i in range(n):
            for a in range(2):
                for b in range(2):
                    if off == 0:
                        ok = 0 <= 2 * (o0 + i) + a - b <= 7
                    else:
                        r = R0 + 2 * m + b
                        kr = R0 + 2 * (m + o0 + i) - 4 + a
                        rs = min(max(r - 4, 0), 120)
                        ok = (0 <= kr < 128) and (rs <= kr < rs + 8)
                    if ok:
                        out[a, :, off + i, b, :] = 1.0
    return np.ascontiguousarray(out.reshape(128, 27 * 128))


_PROG_CACHE = {}


def get_prog(kind, part):
    key = (kind, part)
    if key not in _PROG_CACHE:
        _PROG_CACHE[key] = build_segment(kind, part)
    return _PROG_CACHE[key][0]


def kernel_impl_unfused(x, p, norm_ffn1, ffn1_w_gate, ffn1_w_up, ffn1_w_down, norm_mix, w_in_ab, sink_a,
           w_out_ab, w_in_c, rpb_c, w_out_c, norm_ffn2, ffn2_w_gate, ffn2_w_up, ffn2_w_down,
           norm_ple, w_ple_gate, w_ple_proj, norm_final, _debug=None):
    f32 = lambda a: np.ascontiguousarray(np.asarray(a, dtype=np.float32))
    x = f32(x); p = f32(p)
    cores = list(range(NCORES))
    hT = [np.ascontiguousarray(x[c // 4, (c % 4) * NT:(c % 4 + 1) * NT, :].T) for c in cores]
    cmask, pswap = const_tables()
    ropes = [rope_table(c) for c in cores]
    fin = None
    for li in range(4):
        kind = "ab" if li % 2 == 0 else "c"
        j = li // 2
        g_pre = gains_layout(np.stack([f32(norm_ffn1)[li], f32(norm_mix)[li], f32(norm_mix)[li]]))
        w_in = perm_w_in_ab(f32(w_in_ab)[j]) if kind == "ab" else f32(w_in_c)[j]
        wg, wu, wd = f32(ffn1_w_gate)[li], f32(ffn1_w_up)[li], f32(ffn1_w_down)[li]
        in_maps = []
        for c in cores:
            m = {"hT_in": hT[c], "gains": g_pre, "wg": wg, "wu": wu, "wd": wd, "w_in": w_in, "cmask": cmask}
            if kind == "ab":
                m["pswap"] = pswap
                m["ropecs"] = ropes[c]
            in_maps.append(m)
        res = run_bass_kernel_spmd(get_prog(kind, "pre"), in_maps, core_ids=cores).results
        hT = [res[c]["hT_out"] for c in cores]
        qk = [res[c]["qk_out"] for c in cores]
        v = [res[c]["v_out"] for c in cores]
        if _debug is not None:
            _debug(f"pre{li}", hT)
        ex = exchange_ab(qk, v) if kind == "ab" else exchange_c(qk, v)
        g_post = gains_layout(np.stack([f32(norm_ffn2)[li], f32(norm_ple)[li], f32(norm_final)]))
        wg, wu, wd = f32(ffn2_w_gate)[li], f32(ffn2_w_up)[li], f32(ffn2_w_down)[li]
        w_out = perm_w_out_ab(f32(w_out_ab)[j]) if kind == "ab" else f32(w_out_c)[j]
        if kind == "c":
            bias = c_bias_table(f32(rpb_c)[j])
        else:
            sink = np.ascontiguousarray(np.broadcast_to(f32(sink_a)[j][None, :], (128, 12)))
        in_maps = []
        for c in cores:
            pT = np.ascontiguousarray(p[li, c // 4, (c % 4) * NT:(c % 4 + 1) * NT, :].T)
            m = {"hT_in": hT[c], "gains": g_post, "wg": wg, "wu": wu, "wd": wd, "qk_in": qk[c], "w_out": w_out,
                 "wpg": f32(w_ple_gate)[li], "wpp": f32(w_ple_proj)[li], "pT": pT, "cmask": cmask}
            m.update(ex[c])
            if kind == "c":
                m["bias"] = bias
                m["qm"] = c_qmask(c)
            else:
                m["sink"] = sink
            in_maps.append(m)
        res = run_bass_kernel_spmd(get_prog(kind, "post"), in_maps, core_ids=cores).results
        hT = [res[c]["hT_out"] for c in cores]
        if kind == "c":
            fin = [res[c]["finT"] for c in cores]
        if _debug is not None:
            _debug(f"post{li}", hT)
    out = np.empty((2, 4 * NT, D), np.float32)
    for c in cores:
        out[c // 4, (c % 4) * NT:(c % 4 + 1) * NT, :] = fin[c].T
    return out


NR_X = 55680


def xb_layout(kind):
    lay = {}
    r = 0
    if kind == "ab":
        items = [("kA", 2 * 128 * 18)] + [(f"kB{g}", 2 * 128 * (NT + 128 * d) // 128) for g, d in enumerate(B_DIL)]
        items += [("vA", (NT + 256) * 3)] + [(f"vB{g}", (NT + 128 * d) * 3) for g, d in enumerate(B_DIL)]
    else:
        items = [("kC", 8 * 128 * 20), ("vC", 4 * (NT + 512) * 3)]
    for n, nr in items:
        lay[n] = (r, nr)
        r += nr
    assert r <= NR_X
    return lay


def xb_views(kind, base):
    lay = xb_layout(kind)
    v = {}
    for n, (r0, nr) in lay.items():
        ap = base(r0, nr)
        if n == "kA":
            v[n] = ap.rearrange("(c p a) b -> c p (a b)", c=2, p=128)
        elif n.startswith("kB"):
            v[n] = ap.rearrange("(c p a) b -> c p (a b)", c=2, p=128)
        elif n == "kC":
            v[n] = ap.rearrange("(g c p a) b -> g c p (a b)", g=4, c=2, p=128)
        elif n == "vC":
            v[n] = ap.rearrange("(g r a) b -> g r (a b)", g=4, a=3)
        else:
            v[n] = ap.rearrange("(r a) b -> r (a b)", a=3)
    return v


def build_fused(nlayers=4, dbg_h=False, no_cc=False):
    nc = bass.Bass("TRN2", target_bir_lowering=False)
    es = ExitStack()
    M = Mixer(nc, es)
    xT = M.dram_in("xT", [D, NT])
    finT = M.dram_out("finT", [D, NT])
    M.setup_common()
    gains = M.load_gains("gains", 17)
    valid_d = M.dram_in("valid", [128, 2])
    M.valid = M.sb("valid_sb", [128, 2], F32)
    M.dma("sync", M.valid[:], valid_d[:], [], [("valid",)])
    hv = xT.rearrange("(c p) t -> p c t", p=128)
    for c in range(NC8):
        M.dma("sync", M.h[:, c, :], hv[:, c, :], [], [("h", c, t) for t in range(NTT)])
    M.setup_mixer_consts(True)
    M._rank = {}
    for li in range(nlayers):
        kind = "ab" if li % 2 == 0 else "c"
        nfm = 20 if kind == "ab" else 16
        L = f"_{li}"
        wg1 = M.dram_in("wg1" + L, [D, DFF]); wu1 = M.dram_in("wu1" + L, [D, DFF]); wd1 = M.dram_in("wd1" + L, [DFF, D])
        wg2 = M.dram_in("wg2" + L, [D, DFF]); wu2 = M.dram_in("wu2" + L, [D, DFF]); wd2 = M.dram_in("wd2" + L, [DFF, D])
        w_in = M.dram_in("w_in" + L, [D, nfm * 128 + 1024])
        w_out = M.dram_in("w_out" + L, [D, D])
        wpg = M.dram_in("wpg" + L, [D, D]); wpp = M.dram_in("wpp" + L, [256, D]); pT = M.dram_in("pT" + L, [256, NT])
        q_dram = nc.dram_tensor("q_dram" + L, [nfm, 128, NT], BF16).ap()
        xb = nc.dram_tensor("xb" + L, [NR_X, 128], BF16)
        xv = xb_views(kind, lambda r0, nr: xb.ap()[r0:r0 + nr, :])
        lay = xb_layout(kind)

        def do_exchange(li=li, kind=kind, nfm=nfm, L=L, xv=xv, lay=lay):
            wr_keys = [("qk_out", ci) for ci in range(nfm)] + [("v_out", vg, j) for vg in range(4) for j in range(4)]
            NES = 14720
            edge_send = nc.dram_tensor("edge_send" + L, [2 * NES, 64], BF16)
            gath_e = nc.dram_tensor("gath_e" + L, [NCORES * 2 * NES, 64], BF16)
            edge_nb = nc.dram_tensor("edge_nb" + L, [2 * NES, 64], BF16)
            fills = []
            sends = []
            eoff = [0]
            pending = []

            def edge_view(t, side, name, r0):
                if name == "kA":
                    nr, pat, kw = 512, "(c p a) b -> c p (a b)", dict(c=2, p=128)
                elif name.startswith("kB"):
                    d = B_DIL[int(name[2])]
                    nr, pat, kw = 256 * d, "(c p k) b -> c p k b", dict(c=2, p=128)
                elif name == "kC":
                    nr, pat, kw = 4096, "(g c p a) b -> g c p (a b)", dict(g=4, c=2, p=128)
                elif name == "vA":
                    nr, pat, kw = 768, "(r a) b -> r (a b)", dict(a=6)
                elif name.startswith("vB"):
                    d = B_DIL[int(name[2])]
                    nr, pat, kw = 384 * d, "(k t a) b -> k t (a b)", dict(k=d, a=6)
                else:
                    nr, pat, kw = 6144, "(g r a) b -> g r (a b)", dict(g=4, a=6)
                base = side * NES + r0
                return t.ap()[base:base + nr, :].rearrange(pat, **kw), nr

            def fill(name, sel, side):
                own_region, halo_region = sel(xv[name], side, False)
                if side == 0:
                    pending.append((name, sel))
                r0 = sum(edge_view(edge_send, 0, n_, 0)[1] for n_, _ in pending[:[n_ for n_, _ in pending].index(name)])
                sv, _ = edge_view(edge_send, side, name, r0)
                nv, _ = edge_view(edge_nb, side, name, r0)
                k1 = ("esend", li, name, side)
                M.dma("sync", sv, own_region, wr_keys, [k1])
                sends.append(k1)
                fills.append((name, side, halo_region, nv))

            def sel_cols(hw, own):
                def f(v, side, is_src):
                    if side == 0:
                        return v[:, :, own:own + hw], v[:, :, 0:hw]
                    return v[:, :, hw:2 * hw], v[:, :, hw + own:hw + own + hw]
                return f

            def sel_rows(hw, own):
                def f(v, side, is_src):
                    if side == 0:
                        return v[own:own + hw, :], v[0:hw, :]
                    return v[hw:2 * hw, :], v[hw + own:hw + own + hw, :]
                return f

            if kind == "ab":
                for side in range(2):
                    fill("kA", sel_cols(128, NT), side)
                    fill("vA", sel_rows(128, NT), side)
                    for g, d in enumerate(B_DIL):
                        T = NT // d

                        def selk(v, side_, is_src, d=d, T=T):
                            vv = v.rearrange("c p (k t) -> c p k t", k=d)
                            if side_ == 0:
                                return vv[:, :, :, T:T + 64], vv[:, :, :, 0:64]
                            return vv[:, :, :, 64:128], vv[:, :, :, T + 64:T + 128]

                        def selv(v, side_, is_src, d=d, T=T):
                            vv = v.rearrange("(k t) x -> k t x", k=d)
                            if side_ == 0:
                                return vv[:, T:T + 64, :], vv[:, 0:64, :]
                            return vv[:, 64:128, :], vv[:, T + 64:T + 128, :]
                        fill(f"kB{g}", selk, side)
                        fill(f"vB{g}", selv, side)
            else:
                for side in range(2):
                    def selkc(v, side_, is_src):
                        if side_ == 0:
                            return v[:, :, :, NT:NT + 256], v[:, :, :, 0:256]
                        return v[:, :, :, 256:512], v[:, :, :, 256 + NT:512 + NT]

                    def selvc(v, side_, is_src):
                        if side_ == 0:
                            return v[:, NT:NT + 256, :], v[:, 0:256, :]
                        return v[:, 256:512, :], v[:, 256 + NT:512 + NT, :]
                    fill("kC", selkc, side)
                    fill("vC", selvc, side)
            if not no_cc:
                M.S.add("gpsimd", lambda e, edge_send=edge_send, gath_e=gath_e: e.collective_compute(
                    "AllGather", ALU.bypass, replica_groups=[list(range(NCORES))],
                    ins=[edge_send.ap()[:, :]], outs=[gath_e.ap()[:, :]]), sends, [("gath", li)], cc=True)

            def dyn(side, gath_e=gath_e, edge_nb=edge_nb, NES=NES):
                def fn(e):
                    if "pid_sync" not in M._rank:
                        M._rank["pid_sync"] = e.partition_id()
                    pid = M._rank["pid_sync"]
                    rank = (pid + 7) % NCORES if side == 0 else (pid + 1) % NCORES
                    src = gath_e.ap()[bass.ds(rank * (2 * NES) + side * NES, NES), :]
                    return e.dma_start(out=edge_nb.ap()[side * NES:(side + 1) * NES, :], in_=src)
                return fn
            for side in range(2):
                M.S.add("sync", dyn(side), [("gath", li)], [("enb", li, side)], dma=True)
            fkeys = []
            for name, side, halo_region, nv in fills:
                k2 = ("fill", li, name, side)
                M.dma("sync", halo_region, nv, [("enb", li, side)], [k2])
                fkeys.append(k2)
            M.dram_rd = fkeys + wr_keys

        M.rmsnorm(gains, 4 * li + 0, "gains")
        M.ffn(wg1, wu1, wd1, "f1")
        M.rmsnorm(gains, 4 * li + 1, "gains")
        if kind == "ab":
            def qk_dst(ci, xv=xv, q_dram=q_dram):
                if 6 <= ci < 8:
                    return xv["kA"][ci - 6][:, 128:128 + NT]
                if ci >= 14:
                    g, sp = (ci - 14) // 2, (ci - 14) % 2
                    d = B_DIL[g]
                    T = NT // d
                    kv = xv[f"kB{g}"][sp]
                    if d == 1:
                        return kv[:, 64:64 + T]
                    return kv.rearrange("p (c t) -> p c t", c=d)[:, :, 64:64 + T]
                return q_dram[ci]

            def v_dst(vg, d, blk0, xv=xv):
                if vg == 0:
                    return xv["vA"][128 + blk0 * 128:128 + blk0 * 128 + 512, :].rearrange("(j p) x -> p j x", p=128)
                g = vg - 1
                T = NT // d
                nb = T // 128
                if nb >= 4:
                    c, tb0 = blk0 // nb, blk0 % nb
                    r = c * (T + 128) + 64 + tb0 * 128
                    return xv[f"vB{g}"][r:r + 512, :].rearrange("(j p) x -> p j x", p=128)
                return xv[f"vB{g}"].rearrange("(c t) x -> c t x", c=d)[blk0:blk0 + 4, 64:192, :].rearrange(
                    "j p x -> p j x")
            fm = [(True, 1)] * 8 + [(True, B_DIL[g]) for g in range(3) for _ in range(2)] * 2
            M.inproj(w_in, fm, [1, 1, 4, 16], qk_dst, v_dst, first_groups=[3, 7, 8, 9], mid=do_exchange)
        else:
            def qk_dst(ci, xv=xv, q_dram=q_dram):
                if ci >= 8:
                    return xv["kC"][(ci - 8) // 2, (ci - 8) % 2][:, 256:256 + NT]
                return q_dram[ci]

            def v_dst(vg, d, blk0, xv=xv):
                return xv["vC"][vg][256 + blk0 * 128:256 + blk0 * 128 + 512, :].rearrange("(j p) x -> p j x", p=128)
            M.inproj(w_in, [(False, 1)] * 16, [1, 1, 1, 1], qk_dst, v_dst, first_groups=[4, 5, 6, 7], mid=do_exchange)


        if kind == "ab":
            sink = M.dram_in("sink" + L, [128, 12])
            M.attn_ab(q_dram, xv["kA"], xv["vA"], [xv[f"kB{g}"] for g in range(3)],
                      [xv[f"vB{g}"] for g in range(3)], sink)
        else:
            bias = M.dram_in("bias" + L, [16, 128, 896])
            if li == 1:
                M.qm_d = M.dram_in("qm", [128, 27 * 128])
            M.attn_c(q_dram, xv["kC"], xv["vC"], bias, M.qm_d)
        M.outproj(w_out)
        M.rmsnorm(gains, 4 * li + 2, "gains")
        M.ffn(wg2, wu2, wd2, "f2")
        M.rmsnorm(gains, 4 * li + 3, "gains")
        M.ple(wpg, wpp, pT)
    if dbg_h:
        ov = finT.rearrange("(c p) t -> p c t", p=128)
        for t in range(NTT):
            o = M.dma("sync", ov[:, :, tsl(t)], M.h[:, :, tsl(t)], [("h", c, t) for c in range(NC8)], [("hout", t)])
            M.out_dmas.append(o)
    else:
        M.rmsnorm(gains, 16, "gains", out_f32_dram=finT.rearrange("(c p) t -> p c t", p=128))
    M.S.finalize(es, M.out_dmas)
    with nc.Block() as block:
        M.S.emit(block, final_waits=M.out_dmas)
    return nc, es


def kernel_unfused(*a, **k):
    return kernel_impl_unfused(*a, **k)


_FUSED = {}


def kernel(x, p, norm_ffn1, ffn1_w_gate, ffn1_w_up, ffn1_w_down, norm_mix, w_in_ab, sink_a,
           w_out_ab, w_in_c, rpb_c, w_out_c, norm_ffn2, ffn2_w_gate, ffn2_w_up, ffn2_w_down,
           norm_ple, w_ple_gate, w_ple_proj, norm_final, _nlayers=4, _dbg_h=False, _no_cc=False):
    f32 = lambda a: np.ascontiguousarray(np.asarray(a, dtype=np.float32))
    x = f32(x); p = f32(p)
    cores = list(range(NCORES))
    if "nc" not in _FUSED:
        _FUSED["nc"], _FUSED["es"] = build_fused(_nlayers, _dbg_h, _no_cc)
    cmask, pswap = const_tables()
    glist = []
    for li in range(4):
        glist += [f32(norm_ffn1)[li], f32(norm_mix)[li], f32(norm_ffn2)[li], f32(norm_ple)[li]]
    glist.append(f32(norm_final))
    shared = {"gains": gains_layout(np.stack(glist)), "cmask": cmask, "pswap": pswap}
    for li in range(_nlayers):
        L = f"_{li}"
        j = li // 2
        shared["wg1" + L] = f32(ffn1_w_gate)[li]; shared["wu1" + L] = f32(ffn1_w_up)[li]; shared["wd1" + L] = f32(ffn1_w_down)[li]
        shared["wg2" + L] = f32(ffn2_w_gate)[li]; shared["wu2" + L] = f32(ffn2_w_up)[li]; shared["wd2" + L] = f32(ffn2_w_down)[li]
        shared["wpg" + L] = f32(w_ple_gate)[li]; shared["wpp" + L] = f32(w_ple_proj)[li]
        if li % 2 == 0:
            shared["w_in" + L] = perm_w_in_ab(f32(w_in_ab)[j])
            shared["w_out" + L] = perm_w_out_ab(f32(w_out_ab)[j])
            shared["sink" + L] = np.ascontiguousarray(np.broadcast_to(f32(sink_a)[j][None, :], (128, 12)))
        else:
            shared["w_in" + L] = f32(w_in_c)[j]
            shared["w_out" + L] = f32(w_out_c)[j]
            shared["bias" + L] = c_bias_table(f32(rpb_c)[j])
    in_maps = []
    for c in cores:
        b, q = c // 4, c % 4
        m = dict(shared)
        m["xT"] = np.ascontiguousarray(x[b, q * NT:(q + 1) * NT, :].T)
        m["ropecs"] = rope_table(c)
        if _nlayers > 1:
            m["qm"] = c_qmask(c)
        m["valid"] = np.ascontiguousarray(np.broadcast_to(
            np.array([[1.0 if q > 0 else 0.0, 1.0 if q < 3 else 0.0]], np.float32), (128, 2)))
        for li in range(_nlayers):
            m[f"pT_{li}"] = np.ascontiguousarray(p[li, b, q * NT:(q + 1) * NT, :].T)
        in_maps.append(m)
    res = run_bass_kernel_spmd(_FUSED["nc"], in_maps, core_ids=cores).results
    out = np.empty((2, 4 * NT, D), np.float32)
    for c in cores:
        out[c // 4, (c % 4) * NT:(c % 4 + 1) * NT, :] = res[c]["finT"].T
    return out
```

```python
import numpy as np
import ml_dtypes
from contextlib import ExitStack
import concourse.bass as bass
import concourse.mybir as mybir
from concourse.bass_utils import run_bass_kernel_spmd

F32 = mybir.dt.float32
BF16 = mybir.dt.bfloat16
AF = mybir.ActivationFunctionType
ALU = mybir.AluOpType

D = 1024
NT = 2048
TT = 512
NTT = NT // TT
DFF = 2816
NF = DFF // 128
NC8 = D // 128
EPS = 1e-6


class Op:
    __slots__ = ("stream", "fn", "dma", "cc", "deps", "signal", "sem", "val", "idx", "waits", "know")

    def __init__(self, stream, fn, dma, cc=False):
        self.stream = stream
        self.fn = fn
        self.dma = dma
        self.cc = cc
        self.deps = []
        self.signal = False
        self.sem = None
        self.val = 0
        self.waits = None
        self.know = None


class Sched:
    STREAMS = ("tensor", "vector", "scalar", "gpsimd", "sync")
    SEM_LIMIT = 20000
    NDMA = 12

    def __init__(self, nc):
        self.nc = nc
        self.ops = []
        self.last_w = {}
        self.readers = {}

    def add(self, stream, fn, reads=(), writes=(), dma=False, cc=False):
        reads = list(reads) + [("arena",)]
        op = Op(stream, fn, dma or cc, cc)
        op.idx = len(self.ops)
        deps = set()
        for r in reads:
            w = self.last_w.get(r)
            if w is not None:
                deps.add(w)
        for w_ in writes:
            w = self.last_w.get(w_)
            if w is not None:
                deps.add(w)
            for rd in self.readers.get(w_, ()):
                deps.add(rd)
        for r in reads:
            self.readers.setdefault(r, []).append(op)
        for w_ in writes:
            self.last_w[w_] = op
            self.readers[w_] = []
        deps.discard(op)
        for d in deps:
            if d.stream == "tensor" and stream == "tensor" and not d.dma and not dma:
                continue
            op.deps.append(d)
            d.signal = True
        self.ops.append(op)
        return op

    def finalize(self, es, final_dma_ops=()):
        nc = self.nc
        for o in final_dma_ops:
            o.signal = True
        cur_sem = {}
        cur_cnt = {}
        dma_sems = {}
        dma_cnt = {}
        dma_prev = {}
        nsem = [0]

        def new_sem(tag):
            nsem[0] += 1
            return es.enter_context(nc.semaphore(f"s_{tag}_{nsem[0]}"))

        know = {s: {} for s in self.STREAMS}

        def merge(kn, other):
            for s_, v_ in other.items():
                if kn.get(s_, 0) < v_:
                    kn[s_] = v_

        for op in self.ops:
            st = op.stream
            kn = know[st]
            waits = {}
            alldeps = list(op.deps)
            slot = j = None
            if op.cc:
                pass
            elif op.dma:
                if st not in dma_sems:
                    dma_sems[st] = [new_sem("d" + st) for _ in range(self.NDMA)]
                    dma_cnt[st] = 0
                    dma_prev[st] = [None] * self.NDMA
                j = dma_cnt[st]
                slot = j % self.NDMA
                prev = dma_prev[st][slot]
                if prev is not None:
                    alldeps.append(prev)
            for d in alldeps:
                assert d.sem is not None, "dep must be earlier and signalling"
                if kn.get(d.sem, 0) >= d.val:
                    continue
                if waits.get(d.sem, 0) < d.val:
                    waits[d.sem] = d.val
            for d in alldeps:
                merge(kn, d.know)
            op.waits = list(waits.items())
            if op.cc:
                op.sem = new_sem("cc")
                op.val = 1
                op.signal = True
                op.know = dict(kn)
                op.know[op.sem] = 1
            elif op.dma:
                op.sem = dma_sems[st][slot]
                op.val = 16 * (j // self.NDMA + 1)
                dma_cnt[st] = j + 1
                dma_prev[st][slot] = op
                op.signal = True
                op.know = dict(kn)
                op.know[op.sem] = op.val
            elif op.signal:
                if st not in cur_sem or cur_cnt[st] >= self.SEM_LIMIT:
                    cur_sem[st] = new_sem(st)
                    cur_cnt[st] = 0
                cur_cnt[st] += 1
                op.sem = cur_sem[st]
                op.val = cur_cnt[st]
                kn[op.sem] = op.val
                op.know = dict(kn)
        self.nsem = nsem[0]

    def emit(self, block, final_waits=()):
        nc = self.nc
        by_stream = {s: [] for s in self.STREAMS}
        for op in self.ops:
            by_stream[op.stream].append(op)

        def run(eng, ops, tail=()):
            for op in ops:
                for s_, v_ in op.waits:
                    eng.wait_ge(s_, v_)
                ins = op.fn(eng)
                if op.signal:
                    ins.then_inc(op.sem, 16 if (op.dma and not op.cc) else 1)
            for o in tail:
                eng.wait_ge(o.sem, o.val)

        @block.tensor
        def _(e):
            run(e, by_stream["tensor"])

        @block.vector
        def _(e):
            run(e, by_stream["vector"])

        @block.scalar
        def _(e):
            run(e, by_stream["scalar"])

        @block.gpsimd
        def _(e):
            run(e, by_stream["gpsimd"])

        @block.sync
        def _(e):
            run(e, by_stream["sync"], tail=final_waits)


class Builder:
    def __init__(self, nc, es):
        self.nc = nc
        self.es = es
        self.S = Sched(nc)
        self.uid = 0
        self.ps2 = [nc.alloc_psum_tensor(f"ps{i}", [128, 1024], F32) for i in range(4)]
        self.banks = [self.ps2[i // 2][:, (i % 2) * 512:(i % 2 + 1) * 512] for i in range(8)]
        self.bank_rr = 0
        self.out_dmas = []

    def sb(self, name, shape, dtype):
        return self.nc.alloc_sbuf_tensor(name, shape, dtype)

    def dram_in(self, name, shape, dtype=F32):
        return self.nc.dram_tensor(name, list(shape), dtype, kind="ExternalInput").ap()

    def dram_out(self, name, shape, dtype=F32):
        return self.nc.dram_tensor(name, list(shape), dtype, kind="ExternalOutput").ap()

    def mm(self, out, lhsT, rhs, start, stop, reads, writes):
        return self.S.add("tensor", lambda e: e.matmul(out, lhsT, rhs, start=start, stop=stop),
                          reads, writes)

    def act(self, out, in_, func, reads, writes, scale=1.0, bias=None):
        if bias is None:
            return self.S.add("scalar", lambda e: e.activation(out, in_, func, scale=scale), reads, writes)
        return self.S.add("scalar", lambda e: e.activation(out, in_, func, bias=bias, scale=scale), reads, writes)

    def tt(self, eng, out, in0, in1, op, reads, writes):
        return self.S.add(eng, lambda e: e.tensor_tensor(out, in0, in1, op), reads, writes)

    def stt(self, eng, out, in0, scalar, in1, op0, op1, reads, writes):
        return self.S.add(eng, lambda e: e.scalar_tensor_tensor(out, in0, scalar, in1, op0, op1), reads, writes)

    def ts(self, eng, out, in0, s1, s2, op0, op1, reads, writes):
        if s2 is None:
            return self.S.add(eng, lambda e: e.tensor_scalar(out, in0, s1, None, op0), reads, writes)
        return self.S.add(eng, lambda e: e.tensor_scalar(out, in0, s1, s2, op0, op1), reads, writes)

    def copy(self, eng, out, in_, reads, writes):
        if eng == "scalar":
            return self.S.add(eng, lambda e: e.copy(out, in_), reads, writes)
        return self.S.add(eng, lambda e: e.tensor_copy(out, in_), reads, writes)

    def memset(self, eng, ap, val, writes):
        return self.S.add(eng, lambda e: e.memset(ap, val), (), writes)

    def dma(self, stream, out, in_, reads, writes):
        return self.S.add(stream, lambda e: e.dma_start(out, in_), reads, writes, dma=True)


def tsl(t):
    return slice(t * TT, (t + 1) * TT)


class Model(Builder):
    def setup_common(self):
        B = self
        self.h = B.sb("h", [128, NC8, NT], F32)
        self.xn = B.sb("xn", [128, NC8, NT], BF16)
        self.ones = B.sb("ones", [128, 128], BF16)
        B.memset("vector", self.ones[:], 1.0 / D, [("ones",)])
        self.sq = [B.sb(f"sq{i}", [128, NC8, TT], BF16) for i in range(1)]
        self.rstd = [B.sb(f"rstd{i}", [128, TT], F32) for i in range(2)]
        self.rtmp = [B.sb(f"rtmp{i}", [128, TT], F32) for i in range(2)]
        self.epsc = B.sb("epsc", [128, 1], F32)
        B.memset("vector", self.epsc[:], EPS, [("epsc",)])
        self.norm_cnt = 0
        self.gu_cnt = 0
        self.dn_cnt = 0
        self.sg = [B.sb(f"sg{i}", [128, TT], F32) for i in range(2)]
        self.ARENA = 44 * 1024
        self.arena = B.sb("arena", [128, self.ARENA], BF16)
        self.aoff = 0
        self.GS = 2
        self.wbuf_cnt = 0
        self.h1_cnt = 0
        self.cur_view = None

    def phase(self, name):
        self.S.add("vector", lambda e: e.memset(self.epsc[:, 0:1], EPS), (), [("arena",), ("epsc",)])
        self.aoff = 0
        self.cur_view = name

    def take(self, shape, dtype):
        n = int(np.prod(shape))
        ne = n * 2 if dtype == F32 else n
        start = (self.aoff + 15) // 16 * 16
        assert start + ne <= self.ARENA, ("arena overflow", self.cur_view, start + ne)
        self.aoff = start + ne
        v = self.arena[:, start:start + ne]
        if dtype == F32:
            v = v.bitcast(F32)
        if len(shape) == 2:
            v = v.rearrange("p (a b) -> p a b", a=shape[0], b=shape[1])
        elif len(shape) == 3:
            v = v.rearrange("p (a b c) -> p a b c", a=shape[0], b=shape[1], c=shape[2])
        return v

    def ffn_view(self):
        self.phase("ffn")
        self.wgu = [self.take((2, NC8, self.GS * 128), BF16) for i in range(2)]
        self.wd = [self.take((self.GS, D), BF16) for i in range(3)]
        self.h1 = [self.take((self.GS, NT), BF16) for i in range(3)]

    def load_gains(self, name, n):
        g_d = self.dram_in(name, [128, n * NC8])
        g = self.sb(name + "_sb", [128, n, NC8], F32)
        self.dma("sync", g[:], g_d.rearrange("p (n c) -> p n c", c=NC8), [], [(name,)])
        return g

    def rmsnorm(self, gsb, li, gname, out_f32_dram=None):
        B = self
        h, xn = self.h, self.xn
        for t in range(NTT):
            i = self.norm_cnt % 2
            self.norm_cnt += 1
            sq, rstd = self.sq[0], self.rstd[i]
            bank = self.banks[4 + (self.dn_cnt % 4)]
            bkey = ("bank", 4 + (self.dn_cnt % 4))
            self.dn_cnt += 1
            for c in range(NC8):
                B.act(sq[:, c, :], h[:, c, tsl(t)], AF.Square, [("h", c, t)], [("sq", c)])
            for c in range(NC8):
                B.mm(bank[:], self.ones[:], sq[:, c, :], c == 0, c == NC8 - 1,
                     [("ones",), ("sq", c)], [bkey])
            B.act(self.rtmp[i][:], bank[:], AF.Sqrt, [bkey], [("rtmp", i)], bias=self.epsc[:, 0:1])
            B.S.add("vector", lambda e, o=rstd, a=self.rtmp[i]: e.reciprocal(o[:], a[:]), [("rtmp", i)], [("rstd", i)])
            for c in range(NC8):
                eng = "vector"
                if out_f32_dram is None:
                    B.stt(eng, xn[:, c, tsl(t)], h[:, c, tsl(t)], gsb[:, li, c:c + 1], rstd[:],
                          ALU.mult, ALU.mult, [("h", c, t), ("rstd", i), (gname,)], [("xn", c, t)])
                else:
                    B.stt(eng, h[:, c, tsl(t)], h[:, c, tsl(t)], gsb[:, li, c:c + 1], rstd[:],
                          ALU.mult, ALU.mult, [("h", c, t), ("rstd", i), (gname,)], [("h", c, t)])
            if out_f32_dram is not None:
                o = B.dma("sync", out_f32_dram[:, :, tsl(t)], h[:, :, tsl(t)],
                          [("h", c, t) for c in range(NC8)], [("outdram", t)])
                self.out_dmas.append(o)

    def ffn(self, wg_d, wu_d, wd_d, tag):
        B = self
        GS = self.GS
        ng = NF // GS
        h, xn = self.h, self.xn
        self.ffn_view()
        wg_v = wg_d.rearrange("(k p) f -> p k f", p=128)
        wu_v = wu_d.rearrange("(k p) f -> p k f", p=128)
        wd_v = wd_d.rearrange("(j p) o -> p j o", p=128)
        state = {}

        def GU(gi):
            wb = self.wbuf_cnt % 2
            db = self.wbuf_cnt % 3
            hb = self.h1_cnt % 3
            state[gi] = (db, hb)
            self.wbuf_cnt += 1
            self.h1_cnt += 1
            fs = slice(gi * GS * 128, (gi + 1) * GS * 128)
            B.dma("gpsimd", self.wgu[wb][:, 0], wg_v[:, :, fs], [], [("wgu", wb, 0)])
            B.dma("gpsimd", self.wgu[wb][:, 1], wu_v[:, :, fs], [], [("wgu", wb, 1)])
            B.dma("gpsimd", self.wd[db], wd_v[:, gi * GS:(gi + 1) * GS, :], [], [("wd", db)])
            for fl in range(GS):
                for t in range(NTT):
                    p = self.gu_cnt % 2
                    self.gu_cnt += 1
                    bg, bu = self.banks[2 * p], self.banks[2 * p + 1]
                    kg, ku = ("bank", 2 * p), ("bank", 2 * p + 1)
                    for which, bank, bk in ((0, bg, kg), (1, bu, ku)):
                        for k in range(NC8):
                            B.mm(bank[:], self.wgu[wb][:, which, k, fl * 128:(fl + 1) * 128],
                                 xn[:, k, tsl(t)], k == 0, k == NC8 - 1,
                                 [("wgu", wb, which), ("xn", k, t)], [bk])
                    B.act(self.sg[p][:], bg[:], AF.Silu, [kg], [("sg", p)])
                    B.tt("vector", self.h1[hb][:, fl, tsl(t)], self.sg[p][:], bu[:], ALU.mult,
                         [("sg", p), ku], [("h1", hb, fl, t)])

        def DOWN(gi):
            wb, hb = state[gi]
            for oc in range(NC8):
                for t in range(NTT):
                    bi = 4 + (self.dn_cnt % 4)
                    self.dn_cnt += 1
                    bo, bk = self.banks[bi], ("bank", bi)
                    for fl in range(GS):
                        B.mm(bo[:], self.wd[wb][:, fl, oc * 128:(oc + 1) * 128],
                             self.h1[hb][:, fl, tsl(t)], fl == 0, fl == GS - 1,
                             [("wd", wb), ("h1", hb, fl, t)], [bk])
                    B.stt("vector", h[:, oc, tsl(t)], bo[:], 0.5, h[:, oc, tsl(t)], ALU.mult, ALU.add,
                          [bk, ("h", oc, t)], [("h", oc, t)])

        for gi in range(ng + 1):
            if gi < ng:
                GU(gi)
            if gi >= 1:
                DOWN(gi - 1)


def gains_layout(g):
    n = g.shape[0]
    return np.ascontiguousarray(g.reshape(n, NC8, 128).transpose(2, 0, 1).reshape(128, n * NC8))


A_CHUNK_HEADS = [(0, 3), (1, 4), (2, 5), (6, 9), (7, 10), (8, 11)]
B_DIL = (1, 4, 16)


def _mixer_methods():
    pass


class Mixer(Model):
    def setup_mixer_consts(self, with_rope):
        B = self
        cm_d = self.dram_in("cmask", [128, 640])
        self.cmask = B.sb("cmask_sb", [128, 640], BF16)
        B.dma("gpsimd", self.cmask[:], cm_d[:], [], [("cmask",)])
        if with_rope:
            ps_d = self.dram_in("pswap", [128, 128])
            self.pswap = B.sb("pswap_sb", [128, 128], BF16)
            B.dma("gpsimd", self.pswap[:], ps_d[:], [], [("pswap",)])
            self.ropecs_d = self.dram_in("ropecs", [128, 2, NT])

    def inproj(self, w_d, fm_chunks, vdils, qk_out, v_out, first_groups=None, mid=None):
        B = self
        xn = self.xn
        self.phase("inproj")
        wslots = [self.take((NC8, 256), BF16) for _ in range(4)]
        stage = [self.take((NT,), BF16) for _ in range(3)]
        qf = [self.take((TT,), F32) for _ in range(3)]
        qb = [self.take((TT,), BF16) for _ in range(3)]
        t2 = [self.take((TT,), F32) for _ in range(3)]
        rcnt = [0]
        pend = []
        vst = [self.take((4, 384), BF16) for _ in range(2)]
        for i in range(2):
            B.memset("vector", vst[i], 1.0, [("vst", i, j) for j in range(4)])
        if any(r for r, _ in fm_chunks):
            self.ropecs = self.take((2, NT), F32)
            B.dma("sync", self.ropecs, self.ropecs_d, [], [("ropecs",)])
        wv = w_d.rearrange("(k p) f -> p k f", p=128)
        nfm = len(fm_chunks)
        ncfm = nfm * 128
        cntb = [0]
        first_groups = list(first_groups) if first_groups is not None else []
        rest = [g_ for g_ in range(nfm // 2) if g_ not in first_groups]
        items = [("fm", g_) for g_ in first_groups] + [("v", vg) for vg in range(len(vdils))] + \
                [("mid", None)] + [("fm", g_) for g_ in rest]
        loads = [it for it in items if it[0] != "mid"]
        slot_of = {}

        def issue_load(j):
            if j >= len(loads):
                return
            kind_, gi = loads[j]
            ws = j % 4
            slot_of[loads[j]] = ws
            c0 = gi * 256 if kind_ == "fm" else ncfm + gi * 256
            B.dma("gpsimd", wslots[ws], wv[:, :, c0:c0 + 256], [], [("ws", ws)])

        def do_fm(wg):
            ws = slot_of[("fm", wg)]
            for fl in range(2):
                ci = wg * 2 + fl
                rope, d = fm_chunks[ci]
                si = ci % 3
                st = stage[si]
                for t in range(NTT):
                    bi = cntb[0] % 4
                    cntb[0] += 1
                    bank, bk = self.banks[bi], ("bank", bi)
                    for k in range(NC8):
                        B.mm(bank, wslots[ws][:, k, fl * 128:(fl + 1) * 128], xn[:, k, tsl(t)],
                             k == 0, k == NC8 - 1, [("ws", ws), ("xn", k, t)], [bk])
                    if d == 1:
                        dst = st[:, tsl(t)]
                    else:
                        dst = st.rearrange("p (c t) -> p c t", c=d)[:, :, t * TT // d:(t + 1) * TT // d]

                    def nat(ap, d=d):
                        return ap if d == 1 else ap.rearrange("p (t c) -> p c t", c=d)
                    if not rope:
                        B.copy("scalar", dst, nat(bank), [bk], [("stage", si, t)])
                        continue
                    b2, bk2 = self.banks[4 + bi], ("bank", 4 + bi)
                    p = rcnt[0] % 3
                    rcnt[0] += 1
                    B.copy("scalar", qf[p], bank, [bk], [("qf", p)])
                    B.copy("scalar", qb[p], bank, [bk], [("qb", p)])

                    def tail(p=p, b2=b2, bk2=bk2, t=t, dst=dst, nat=nat, si=si):
                        B.mm(b2, self.pswap[:], qb[p], True, True, [("pswap",), ("qb", p)], [bk2])
                        B.tt("vector", qf[p], qf[p], self.ropecs[:, 0, tsl(t)], ALU.mult,
                             [("qf", p), ("ropecs",)], [("qf", p)])
                        B.tt("vector", t2[p], b2, self.ropecs[:, 1, tsl(t)], ALU.mult,
                             [bk2, ("ropecs",)], [("t2", p)])
                        B.tt("vector", dst, nat(qf[p]), nat(t2[p]), ALU.add,
                             [("qf", p), ("t2", p)], [("stage", si, t)])
                    if pend:
                        pend.pop(0)()
                    pend.append(tail)
                while pend:
                    pend.pop(0)()
                dst_ap = qk_out(ci) if callable(qk_out) else qk_out[ci]
                B.dma("sync", dst_ap, st if not callable(qk_out) else self._stage_view(st, d),
                      [("stage", si, t) for t in range(NTT)], [("qk_out", ci)])

        def do_v(vg):
            ws = slot_of[("v", vg)]
            d = vdils[vg]
            T = NT // d
            nb = T // 128
            for c in range(d):
                for tb in range(nb):
                    blk = c * nb + tb
                    vi = (blk // 4) % 2
                    bi = cntb[0] % 4
                    cntb[0] += 1
                    bank, bk = self.banks[bi], ("bank", bi)
                    t0 = c + d * tb * 128
                    for k in range(NC8):
                        if d == 1:
                            lhsT = xn[:, k, t0:t0 + 128]
                        else:
                            lhsT = xn[:, k, t0:t0 + 127 * d + 1:d]
                        B.mm(bank[:, 0:256], lhsT, wslots[ws][:, k, :], k == 0, k == NC8 - 1,
                             [("ws", ws)] + [("xn", k, t) for t in range(NTT)], [bk])
                    dstv = vst[vi][:, blk % 4, :].rearrange("p (a b c) -> p a b c", a=2, b=3, c=64)[:, :, 0:3:2, :]
                    srcv = bank[:, 0:256].rearrange("p (a b c) -> p a b c", a=2, b=2, c=64)
                    B.copy("scalar", dstv, srcv, [bk], [("vst", vi, blk % 4)])
                    if blk % 4 == 3:
                        r0 = (blk - 3) * 128
                        if callable(v_out):
                            dst_ap = v_out(vg, d, blk - 3)
                        else:
                            dst_ap = v_out[vg, r0:r0 + 512, :].rearrange("(j p) x -> p j x", p=128)
                        B.dma("sync", dst_ap, vst[vi],
                              [("vst", vi, j) for j in range(4)], [("v_out", vg, blk // 4)])

        issue_load(0)
        issue_load(1)
        issue_load(2)
        j = 0
        for it in items:
            if it[0] == "mid":
                if mid is not None:
                    mid()
                continue
            issue_load(j + 3)
            if it[0] == "fm":
                do_fm(it[1])
            else:
                do_v(it[1])
            j += 1

    @staticmethod
    def _stage_view(st, d):
        return st if d == 1 else st.rearrange("p (c t) -> p c t", c=d)

    def attn_setup(self, s_banks=1):
        self.pt = [self.take((768,), BF16) for _ in range(5)]
        self.rc = [self.take((TT,), F32) for _ in range(3)]
        self.ucnt = 0
        self.ocnt = 0
        self.pcnt = 0
        if s_banks == 1:
            self.sbufs = [(self.banks[i], ("bank", i)) for i in range(4)]
            self.obanks = [4, 5, 6, 7]
            self.LA = 3
        else:
            self.sbufs = [(self.ps2[i], ("S2", i)) for i in range(3)]
            self.obanks = [6, 7]
            self.LA = 2
        self.pend = []

    def attn_unit(self, q_ap, ktiles, vtiles, mask_ap, o_ap, okey, rd, post=None):
        B = self
        n = len(ktiles)
        si = self.ucnt % len(self.sbufs)
        pi = self.ucnt % 5
        self.ucnt += 1
        S, sk = self.sbufs[si]
        sk2 = [sk] if sk[0] == "bank" else [("bank", 2 * sk[1]), ("bank", 2 * sk[1] + 1)]
        pt, pk = self.pt[pi], ("pt", pi)
        for i, kT in enumerate(ktiles):
            B.mm(S[:, i * 128:(i + 1) * 128], kT, q_ap, True, True, rd, sk2)
        B.act(pt[:, 0:n * 128], S[:, 0:n * 128], AF.Exp, sk2, [pk], scale=0.125)
        meng = "gpsimd" if (self.ucnt % 2 == 0) else "vector"
        B.tt(meng, pt[:, 0:n * 128], pt[:, 0:n * 128], mask_ap, ALU.mult, [pk] + rd, [pk])

        def pv():
            for i, va in enumerate(vtiles):
                B.mm(o_ap, va, pt[:, i * 128:(i + 1) * 128], i == 0, i == n - 1, [pk] + rd, [okey])
            if post is not None:
                post()
        self.pend.append(pv)
        if len(self.pend) > self.LA:
            self.pend.pop(0)()

    def attn_flush(self):
        while self.pend:
            self.pend.pop(0)()

    def obank(self):
        j = self.obanks[self.ocnt % len(self.obanks)]
        self.ocnt += 1
        return self.banks[j], ("bank", j)

    def normalize(self, src_o, src_den, par, dst, rd, wr, sink_col=None):
        B = self
        oh = slice(par * 64, par * 64 + 64)
        dh = slice((1 - par) * 64, (1 - par) * 64 + 64)
        p = self.pcnt % 3
        self.pcnt += 1
        rc = self.rc[p]
        if sink_col is not None:
            B.act(rc[oh], src_den[dh], AF.Identity, rd + [("esink",)], [("rc", p)],
                  bias=self.esink[dh, sink_col:sink_col + 1])
        else:
            B.copy("scalar", rc[oh], src_den[dh], rd, [("rc", p)])
        B.S.add("vector", lambda e: e.reciprocal(rc[oh], rc[oh]), [("rc", p)], [("rc", p)])
        B.tt("vector", dst[oh], src_o[oh], rc[oh], ALU.mult, rd + [("rc", p)], wr)

    def load_group(self, kh_d, vh_d, q_srcs, klen, ntile):
        B = self
        kt = [self.take((klen,), BF16) for _ in range(2)]
        vt = self.take((ntile, 384), BF16)
        qt = [self.take((NT,), BF16) for _ in q_srcs]
        for i in range(2):
            B.dma("sync", kt[i], kh_d[i], self.dram_rd, [("kt", i)])
        B.dma("sync", vt, vh_d.rearrange("(n p) x -> p n x", p=128), self.dram_rd, [("vt",)])
        self.vmask(vt[:, 0:1, :], vt[:, ntile - 1:ntile, :])
        for i, qs in enumerate(q_srcs):
            B.dma("sync", qt[i], qs, self.dram_rd, [("qt", i)])
        return kt, vt, qt

    dram_rd = ()
    valid = None
    exch_tail = None

    def vmask(self, lo, hi):
        if self.valid is None:
            return
        B = self
        B.ts("vector", lo, lo, self.valid[lo.base_partition():lo.base_partition() + lo.shape[0], 0:1], None,
             ALU.mult, None, [("vt",), ("valid",)], [("vt",)])
        B.ts("vector", hi, hi, self.valid[hi.base_partition():hi.base_partition() + hi.shape[0], 1:2], None,
             ALU.mult, None, [("vt",), ("valid",)], [("vt",)])

    def attn_ab(self, qk_d, kA_d, vA_d, kB_d, vB_d, sink_d):
        B = self
        mixed = self.xn
        self.phase("attnA")
        self.attn_setup()
        self.esink = self.take((12,), F32)
        B.dma("sync", self.esink, sink_d, [], [("esink",)])
        B.act(self.esink, self.esink, AF.Exp, [("esink",)], [("esink",)])
        kt = [self.take((NT + 256,), BF16) for _ in range(2)]
        vt = self.take((18, 384), BF16)
        qt = [self.take((NT,), BF16) for _ in range(6)]
        for i in range(2):
            B.dma("sync", kt[i][:, 128:128 + NT], kA_d[i][:, 128:128 + NT], self.dram_rd, [("kt", i, "own")])
        B.dma("sync", vt[:, 1:17, :], vA_d[128:128 + NT, :].rearrange("(n p) x -> p n x", p=128),
              self.dram_rd, [("vt", "own")])
        for i in range(6):
            B.dma("sync", qt[i], qk_d[i], self.dram_rd, [("qt", i)])
        if self.exch_tail is not None:
            self.exch_tail()
        for i in range(2):
            B.dma("sync", kt[i][:, 0:128], kA_d[i][:, 0:128], self.dram_rd, [("kt", i, "h0")])
            B.dma("sync", kt[i][:, 128 + NT:256 + NT], kA_d[i][:, 128 + NT:256 + NT], self.dram_rd, [("kt", i, "h1")])
        B.dma("sync", vt[:, 0, :], vA_d[0:128, :], self.dram_rd, [("vt", "h0")])
        B.dma("sync", vt[:, 17, :], vA_d[128 + NT:256 + NT, :], self.dram_rd, [("vt", "h1")])
        if self.valid is not None:
            B.ts("vector", vt[:, 0, :], vt[:, 0, :], self.valid[:, 0:1], None, ALU.mult, None,
                 [("vt", "h0"), ("valid",)], [("vt", "h0")])
            B.ts("vector", vt[:, 17, :], vt[:, 17, :], self.valid[:, 1:2], None, ALU.mult, None,
                 [("vt", "h1"), ("valid",)], [("vt", "h1")])
        maskA = self.cmask[:, 0:384]
        for t in (1, 2, 0, 3):
            for qc in range(6):
                kc = qc // 3
                for par in range(2):
                    head = A_CHUNK_HEADS[qc][par]
                    hs = slice(par * 64, par * 64 + 64)
                    vs = slice(kc * 192 + par * 64, kc * 192 + par * 64 + 128)
                    ob, ok = self.obank()
                    for u in range(4):
                        qblk = t * 4 + u
                        rd = [("kt", kc, "own"), ("vt", "own"), ("qt", qc), ("cmask",)]
                        if qblk == 0:
                            rd += [("kt", kc, "h0"), ("vt", "h0")]
                        if qblk == 15:
                            rd += [("kt", kc, "h1"), ("vt", "h1")]
                        post = None
                        if u == 3:
                            post = (lambda ob=ob, ok=ok, par=par, qc=qc, t=t, head=head: self.normalize(
                                ob, ob, par, mixed[:, qc, tsl(t)], [ok], [("xn", qc, t)], sink_col=head))
                        self.attn_unit(qt[qc][hs, qblk * 128:(qblk + 1) * 128],
                                       [kt[kc][hs, (qblk + i) * 128:(qblk + i + 1) * 128] for i in range(3)],
                                       [vt[:, qblk + i, vs] for i in range(3)],
                                       maskA, ob[:, u * 128:(u + 1) * 128], ok, rd, post)
        self.attn_flush()
        for g, d in enumerate(B_DIL):
            T = NT // d
            nb = T // 128
            self.phase("attnB%d" % g)
            self.attn_setup()
            if g == 0:
                pass
            acc = [[self.take((NT,), F32) for _ in range(2)] for _ in range(2)]
            if g == 0:
                self.accB = acc
            klen, ntile = d * (T + 128), d * (nb + 1)
            ktb = self.take((klen,), BF16)
            vtb = self.take((ntile, 192), BF16)
            qtb = self.take((NT,), BF16)
            maskB = self.cmask[:, 384:640]
            for sp in range(2):
                B.dma("sync", ktb, kB_d[g][sp], self.dram_rd, [("kt", 0)])
                B.dma("sync", vtb, vB_d[g].rearrange("(n p) x -> p n x", p=128)[:, :, sp * 192:(sp + 1) * 192],
                      self.dram_rd, [("vt",)])
                vv = vtb.rearrange("p (c n) x -> p c n x", c=d)
                self.vmask(vv[0:64, :, 0, :], vv[64:128, :, nb, :])
                B.dma("sync", qtb, qk_d[8 + 2 * g + sp], self.dram_rd, [("qt", 0)])
                kt, vt, qt = [ktb, ktb], vtb, [qtb, qtb]
                for par in range(2):
                    hs = slice(par * 64, par * 64 + 64)
                    vs = slice(par * 64, par * 64 + 128)
                    rd = [("kt", 0), ("vt",), ("qt", 0), ("cmask",)]
                    a = acc[sp][par]
                    units = [(c, qb_) for c in range(d) for qb_ in range(nb)]
                    for u0 in range(0, len(units), 4):
                        ob, ok = self.obank()
                        if d == 1:
                            dst = a[:, u0 * 128:(u0 + 4) * 128]
                            src = ob
                        elif d == 4:
                            c = units[u0][0]
                            dst = a.rearrange("p (t c) -> p c t", c=4)[:, c, :]
                            src = ob
                        else:
                            c0 = units[u0][0]
                            dst = a.rearrange("p (t c) -> p c t", c=16)[:, c0:c0 + 4, :]
                            src = ob.rearrange("p (j t) -> p j t", j=4)
                        akey = ("acc", sp, par)

                        def post(dst=dst, src=src, ok=ok, akey=akey, g=g):
                            if g == 0:
                                B.copy("scalar", dst, src, [ok], [akey])
                            else:
                                B.tt("vector", dst, src, dst, ALU.add, [ok, akey], [akey])
                        for u in range(4):
                            c, qb_ = units[u0 + u]
                            kbase = c * (T + 128) + qb_ * 128
                            vbase = c * (nb + 1) + qb_
                            self.attn_unit(qt[sp][hs, c * T + qb_ * 128:c * T + (qb_ + 1) * 128],
                                           [kt[sp][hs, kbase + i * 128:kbase + (i + 1) * 128] for i in range(2)],
                                           [vt[:, vbase + i, vs] for i in range(2)],
                                           maskB, ob[:, u * 128:(u + 1) * 128], ok, rd, post if u == 3 else None)
                self.attn_flush()
            if g == 2:
                for sp in range(2):
                    for par in range(2):
                        a = acc[sp][par]
                        for t in range(NTT):
                            self.normalize(a[:, tsl(t)], a[:, tsl(t)], par, mixed[:, 6 + sp, tsl(t)],
                                           [("acc", sp, par)], [("xn", 6 + sp, t)])

    def outproj(self, w_d, scale=1.0):
        B = self
        self.phase("outproj")
        wsq = self.take((NC8, D), BF16)
        B.dma("gpsimd", wsq, w_d.rearrange("(k p) o -> p k o", p=128), [], [("wsq",)])
        for oc in range(NC8):
            for t in range(NTT):
                bi = 4 + (self.dn_cnt % 4)
                self.dn_cnt += 1
                bo, bk = self.banks[bi], ("bank", bi)
                for k in range(NC8):
                    B.mm(bo, wsq[:, k, oc * 128:(oc + 1) * 128], self.xn[:, k, tsl(t)], k == 0, k == NC8 - 1,
                         [("wsq",), ("xn", k, t)], [bk])
                B.stt("vector", self.h[:, oc, tsl(t)], bo, scale, self.h[:, oc, tsl(t)], ALU.mult, ALU.add,
                      [bk, ("h", oc, t)], [("h", oc, t)])

    def ple(self, wg_d, wp_d, pT_d):
        B = self
        self.phase("ple")
        wsq = self.take((NC8, D), BF16)
        wpp = self.take((2, D), BF16)
        pT = self.take((2, NT), BF16)
        B.dma("gpsimd", wsq, wg_d.rearrange("(k p) o -> p k o", p=128), [], [("wsq",)])
        B.dma("gpsimd", wpp, wp_d.rearrange("(k p) o -> p k o", p=128), [], [("wpp",)])
        B.dma("gpsimd", pT, pT_d.rearrange("(k p) t -> p k t", p=128), [], [("pT",)])
        cnt = 0
        for oc in range(NC8):
            for t in range(NTT):
                p = cnt % 2
                cnt += 1
                bg, kg = self.banks[2 * p], ("bank", 2 * p)
                bp, kp = self.banks[2 * p + 1], ("bank", 2 * p + 1)
                for k in range(NC8):
                    B.mm(bg, wsq[:, k, oc * 128:(oc + 1) * 128], self.xn[:, k, tsl(t)], k == 0, k == NC8 - 1,
                         [("wsq",), ("xn", k, t)], [kg])
                for k in range(2):
                    B.mm(bp, wpp[:, k, oc * 128:(oc + 1) * 128], pT[:, k, tsl(t)], k == 0, k == 1,
                         [("wpp",), ("pT",)], [kp])
                B.act(self.sg[p][:], bg, AF.Sigmoid, [kg], [("sg", p)])
                B.tt("vector", self.sg[p][:], self.sg[p][:], bp, ALU.mult, [("sg", p), kp], [("sg", p)])
                B.tt("vector", self.h[:, oc, tsl(t)], self.h[:, oc, tsl(t)], self.sg[p][:], ALU.add,
                     [("sg", p), ("h", oc, t)], [("h", oc, t)])

    C_CLS = {0: (5, 6, 0), 1: (11, 5, 0), 14: (16, 5, 0), 15: (21, 6, -1)}

    def attn_c(self, qk_d, kC_d, vC_d, bias_d, qm_d):
        B = self
        mixed = self.xn
        self.phase("attnC")
        self.attn_setup(s_banks=2)
        qm = self.take((27 * 128,), BF16)
        B.dma("gpsimd", qm, qm_d, [], [("qm",)])
        kt = [self.take((NT + 512,), BF16) for _ in range(2)]
        vt = self.take((20, 384), BF16)
        qt = [self.take((NT,), BF16) for _ in range(2)]
        bst = [self.take((896,), F32) for _ in range(2)]
        ed = [self.take((896,), BF16) for _ in range(2)]
        ecls = [self.take((27 * 128,), BF16) for _ in range(2)]
        for g4 in range(4):
            for i in range(2):
                B.dma("sync", kt[i], kC_d[g4, i], self.dram_rd, [("kt", i)])
                B.dma("sync", qt[i], qk_d[2 * g4 + i], self.dram_rd, [("qt", i)])
            B.dma("sync", vt, vC_d[g4].rearrange("(n p) x -> p n x", p=128), self.dram_rd, [("vt",)])
            self.vmask(vt[:, 0:2, :], vt[:, 18:20, :])
            for ci in range(2):
                for par in range(2):
                    head = 4 * g4 + 2 * ci + par
                    e = head % 2
                    B.dma("sync", bst[e], bias_d[head], [], [("bst", e)])
                    B.act(ed[e], bst[e], AF.Exp, [("bst", e)], [("ed", e)])
                    for off, n, i0 in ((0, 5, 1), (5, 6, 1), (11, 5, 1), (16, 5, 1), (21, 6, 0)):
                        B.tt("vector", ecls[e][:, off * 128:(off + n) * 128], ed[e][:, i0 * 128:(i0 + n) * 128],
                             qm[:, off * 128:(off + n) * 128], ALU.mult, [("ed", e), ("qm",)], [("ecls", e, off)])
                    hs = slice(par * 64, par * 64 + 64)
                    vs = slice(ci * 192 + par * 64, ci * 192 + par * 64 + 128)
                    for t in range(NTT):
                        ob, ok = self.obank()
                        for u in range(4):
                            m = t * 4 + u
                            off, n, o0 = self.C_CLS.get(m, (0, 5, 0))
                            rd = [("kt", ci), ("vt",), ("qt", ci), ("ecls", e, off)]
                            j0 = m + o0
                            post = None
                            if u == 3:
                                post = (lambda ob=ob, ok=ok, par=par, cc_=2 * g4 + ci, t=t: self.normalize(
                                    ob, ob, par, mixed[:, cc_, tsl(t)], [ok], [("xn", cc_, t)]))
                            self.attn_unit(qt[ci][hs, m * 128:(m + 1) * 128],
                                           [kt[ci][hs, (j0 + i) * 128:(j0 + i + 1) * 128] for i in range(n)],
                                           [vt[:, j0 + i, vs] for i in range(n)],
                                           ecls[e][:, off * 128:(off + n) * 128], ob[:, u * 128:(u + 1) * 128], ok, rd,
                                           post)
            self.attn_flush()


def build_segment(kind, part):
    nc = bass.Bass("TRN2", target_bir_lowering=False)
    es = ExitStack()
    M = Mixer(nc, es)
    hT_in = M.dram_in("hT_in", [D, NT])
    hT_out = M.dram_out("hT_out", [D, NT])
    M.setup_common()
    gains = M.load_gains("gains", 3)
    hv = hT_in.rearrange("(c p) t -> p c t", p=128)
    for c in range(NC8):
        M.dma("sync", M.h[:, c, :], hv[:, c, :], [], [("h", c, t) for t in range(NTT)])
    nfm = 20 if kind == "ab" else 16
    nvg = 4
    if part == "pre":
        wg = M.dram_in("wg", [D, DFF]); wu = M.dram_in("wu", [D, DFF]); wd = M.dram_in("wd", [DFF, D])
        w_in = M.dram_in("w_in", [D, nfm * 128 + nvg * 256])
        qk_out = M.dram_out("qk_out", [nfm, 128, NT], BF16)
        v_out = M.dram_out("v_out", [nvg, NT, 384], BF16)
        M.setup_mixer_consts(kind == "ab")
        M.rmsnorm(gains, 0, "gains")
        M.ffn(wg, wu, wd, "f1")
        M.rmsnorm(gains, 1, "gains")
        if kind == "ab":
            fm = [(True, 1)] * 8 + [(True, B_DIL[g]) for g in range(3) for _ in range(2)] * 2
            M.inproj(w_in, fm, [1, 1, 4, 16], qk_out, v_out)
        else:
            M.inproj(w_in, [(False, 1)] * 16, [1, 1, 1, 1], qk_out, v_out)
        outs = []
    else:
        qk_in = M.dram_in("qk_in", [nfm, 128, NT], BF16)
        w_out = M.dram_in("w_out", [D, D])
        wg = M.dram_in("wg", [D, DFF]); wu = M.dram_in("wu", [D, DFF]); wd = M.dram_in("wd", [DFF, D])
        wpg = M.dram_in("wpg", [D, D]); wpp = M.dram_in("wpp", [256, D]); pT = M.dram_in("pT", [256, NT])
        M.setup_mixer_consts(False)
        if kind == "ab":
            kA = M.dram_in("kA", [2, 128, NT + 256], BF16)
            vA = M.dram_in("vA", [NT + 256, 384], BF16)
            kB = [M.dram_in(f"kB{g}", [2, 128, NT + 128 * d], BF16) for g, d in enumerate(B_DIL)]
            vB = [M.dram_in(f"vB{g}", [NT + 128 * d, 384], BF16) for g, d in enumerate(B_DIL)]
            sink = M.dram_in("sink", [128, 12])
            M.attn_ab(qk_in, kA, vA, kB, vB, sink)
        else:
            kC = M.dram_in("kC", [4, 2, 128, NT + 512], BF16)
            vC = M.dram_in("vC", [4, NT + 512, 384], BF16)
            bias = M.dram_in("bias", [16, 128, 896])
            qm = M.dram_in("qm", [128, 27 * 128])
            M.attn_c(qk_in, kC, vC, bias, qm)
        M.outproj(w_out)
        M.rmsnorm(gains, 0, "gains")
        M.ffn(wg, wu, wd, "f2")
        M.rmsnorm(gains, 1, "gains")
        M.ple(wpg, wpp, pT)
    ov = hT_out.rearrange("(c p) t -> p c t", p=128)
    for t in range(NTT):
        o = M.dma("sync", ov[:, :, tsl(t)], M.h[:, :, tsl(t)], [("h", c, t) for c in range(NC8)], [("hout", t)])
        M.out_dmas.append(o)
    if part == "post" and kind == "c":
        fin = M.dram_out("finT", [D, NT])
        M.rmsnorm(gains, 2, "gains", out_f32_dram=fin.rearrange("(c p) t -> p c t", p=128))
    M.S.finalize(es, M.out_dmas)
    with nc.Block() as block:
        M.S.emit(block, final_waits=M.out_dmas)
    return nc, es


BF = ml_dtypes.bfloat16
A_ORDER = [0, 3, 1, 4, 2, 5, 6, 9, 7, 10, 8, 11]
NCORES = 8


def perm_w_in_ab(w):
    qa = w[:, 0:768].reshape(D, 12, 64)[:, A_ORDER].reshape(D, 768)
    return np.ascontiguousarray(np.concatenate(
        [qa, w[:, 768:1024], w[:, 1280:2048], w[:, 2048:2816], w[:, 1024:1280], w[:, 2816:3584]], axis=1))


def perm_w_out_ab(w):
    a = w[0:768].reshape(12, 64, D)[A_ORDER].reshape(768, D)
    return np.ascontiguousarray(np.concatenate([a, w[768:1024]], axis=0))


def const_tables():
    k = np.arange(128)[:, None]
    q = np.arange(128)[None, :]
    L = (k >= q).astype(np.float32)
    U = (k <= q).astype(np.float32)
    cmask = np.concatenate([L, np.ones((128, 128), np.float32), U, L, U], axis=1)
    pswap = (np.arange(128)[:, None] == (np.arange(128)[None, :] ^ 32)).astype(np.float32)
    return np.ascontiguousarray(cmask), np.ascontiguousarray(pswap)


def rope_table(core):
    pos = ((core % 4) * NT + np.arange(NT)).astype(np.float32)
    dd = np.arange(128) % 64
    inv = (10000.0 ** (-(dd % 32).astype(np.float32) / 32)).astype(np.float32)
    ang = pos[None, :] * inv[:, None]
    sign = np.where(dd < 32, -1.0, 1.0).astype(np.float32)[:, None]
    return np.ascontiguousarray(np.stack([np.cos(ang), np.sin(ang) * sign], axis=1).astype(np.float32))


def nbrs(core):
    pos = core % 4
    return (core - 1 if pos > 0 else None), (core + 1 if pos < 3 else None)


def halo_cols(arrs, core, hw):
    own = arrs[core]
    pv, nx = nbrs(core)
    z = np.zeros(own.shape[:-1] + (hw,), own.dtype)
    left = arrs[pv][..., -hw:] if pv is not None else z
    right = arrs[nx][..., :hw] if nx is not None else z
    return np.ascontiguousarray(np.concatenate([left, own, right], axis=-1))


def halo_rows(arrs, core, hw, axis):
    own = arrs[core]
    pv, nx = nbrs(core)
    zshape = list(own.shape)
    zshape[axis] = hw
    z = np.zeros(zshape, own.dtype)
    sl_l = [slice(None)] * own.ndim
    sl_l[axis] = slice(-hw, None)
    sl_r = [slice(None)] * own.ndim
    sl_r[axis] = slice(0, hw)
    left = arrs[pv][tuple(sl_l)] if pv is not None else z
    right = arrs[nx][tuple(sl_r)] if nx is not None else z
    return np.ascontiguousarray(np.concatenate([left, own, right], axis=axis))


def exchange_ab(qk, v):
    res = []
    kAs = [qk[c][6:8] for c in range(NCORES)]
    vAs = [v[c][0] for c in range(NCORES)]
    for c in range(NCORES):
        m = {"kA": halo_cols(kAs, c, 128), "vA": halo_rows(vAs, c, 128, 0)}
        res.append(m)
    for g, d in enumerate(B_DIL):
        T = NT // d
        kBs = [qk[c][14 + 2 * g:16 + 2 * g].reshape(2, 128, d, T) for c in range(NCORES)]
        vBs = [v[c][1 + g].reshape(d, T, 384) for c in range(NCORES)]
        for c in range(NCORES):
            res[c][f"kB{g}"] = halo_cols(kBs, c, 64).reshape(2, 128, d * (T + 128))
            res[c][f"vB{g}"] = halo_rows(vBs, c, 64, 1).reshape(d * (T + 128), 384)
    return res


def exchange_c(qk, v):
    res = []
    kCs = [qk[c][8:16].reshape(4, 2, 128, NT) for c in range(NCORES)]
    vCs = [v[c] for c in range(NCORES)]
    for c in range(NCORES):
        res.append({"kC": halo_cols(kCs, c, 256), "vC": halo_rows(vCs, c, 256, 1)})
    return res


def c_bias_table(rpb):
    a = np.arange(2)[:, None, None, None, None]
    jp = np.arange(64)[None, :, None, None, None]
    di = np.arange(7)[None, None, :, None, None]
    b = np.arange(2)[None, None, None, :, None]
    j = np.arange(64)[None, None, None, None, :]
    dr = (2 * di - 6) + a - b + 7
    dc = jp - j + 15
    cs = np.clip(j - 8, 0, 48)
    valid = (dr >= 0) & (dr <= 14) & (jp >= cs) & (jp < cs + 16)
    valid = np.broadcast_to(valid, (2, 64, 7, 2, 64))
    drc = np.broadcast_to(np.clip(dr, 0, 14), valid.shape)
    dcc = np.broadcast_to(np.clip(dc, 0, 30), valid.shape)
    g = rpb[:, drc, dcc]
    g = np.where(valid[None], g, np.float32(-30000.0)).astype(np.float32)
    return np.ascontiguousarray(g.reshape(16, 128, 896))


def c_qmask(core):
    R0 = (core % 4) * 32
    out = np.zeros((2, 64, 27, 2, 64), np.float32)
    cls = [(0, 5, 0, 4), (5, 6, 0, 0), (11, 5, 0, 1), (16, 5, 0, 14), (21, 6, -1, 15)]
    for off, n, o0, m in cls:
        for i in range(n):
            for a in range(2):
                for b in range(2):
                    if off == 0:
                        ok = 0 <= 2 * (o0 + i) + a - b <= 7
                    else:
                        r = R0 + 2 * m + b
                        kr = R0 + 2 * (m + o0 + i) - 4 + a
                        rs = min(max(r - 4, 0), 120)
                        ok = (0 <= kr < 128) and (rs <= kr < rs + 8)
                    if ok:
                        out[a, :, off + i, b, :] = 1.0
    return np.ascontiguousarray(out.reshape(128, 27 * 128))


_PROG_CACHE = {}


def get_prog(kind, part):
    key = (kind, part)
    if key not in _PROG_CACHE:
        _PROG_CACHE[key] = build_segment(kind, part)
    return _PROG_CACHE[key][0]


def kernel_impl_unfused(x, p, norm_ffn1, ffn1_w_gate, ffn1_w_up, ffn1_w_down, norm_mix, w_in_ab, sink_a,
           w_out_ab, w_in_c, rpb_c, w_out_c, norm_ffn2, ffn2_w_gate, ffn2_w_up, ffn2_w_down,
           norm_ple, w_ple_gate, w_ple_proj, norm_final, _debug=None):
    f32 = lambda a: np.ascontiguousarray(np.asarray(a, dtype=np.float32))
    x = f32(x); p = f32(p)
    cores = list(range(NCORES))
    hT = [np.ascontiguousarray(x[c // 4, (c % 4) * NT:(c % 4 + 1) * NT, :].T) for c in cores]
    cmask, pswap = const_tables()
    ropes = [rope_table(c) for c in cores]
    fin = None
    for li in range(4):
        kind = "ab" if li % 2 == 0 else "c"
        j = li // 2
        g_pre = gains_layout(np.stack([f32(norm_ffn1)[li], f32(norm_mix)[li], f32(norm_mix)[li]]))
        w_in = perm_w_in_ab(f32(w_in_ab)[j]) if kind == "ab" else f32(w_in_c)[j]
        wg, wu, wd = f32(ffn1_w_gate)[li], f32(ffn1_w_up)[li], f32(ffn1_w_down)[li]
        in_maps = []
        for c in cores:
            m = {"hT_in": hT[c], "gains": g_pre, "wg": wg, "wu": wu, "wd": wd, "w_in": w_in, "cmask": cmask}
            if kind == "ab":
                m["pswap"] = pswap
                m["ropecs"] = ropes[c]
            in_maps.append(m)
        res = run_bass_kernel_spmd(get_prog(kind, "pre"), in_maps, core_ids=cores).results
        hT = [res[c]["hT_out"] for c in cores]
        qk = [res[c]["qk_out"] for c in cores]
        v = [res[c]["v_out"] for c in cores]
        if _debug is not None:
            _debug(f"pre{li}", hT)
        ex = exchange_ab(qk, v) if kind == "ab" else exchange_c(qk, v)
        g_post = gains_layout(np.stack([f32(norm_ffn2)[li], f32(norm_ple)[li], f32(norm_final)]))
        wg, wu, wd = f32(ffn2_w_gate)[li], f32(ffn2_w_up)[li], f32(ffn2_w_down)[li]
        w_out = perm_w_out_ab(f32(w_out_ab)[j]) if kind == "ab" else f32(w_out_c)[j]
        if kind == "c":
            bias = c_bias_table(f32(rpb_c)[j])
        else:
            sink = np.ascontiguousarray(np.broadcast_to(f32(sink_a)[j][None, :], (128, 12)))
        in_maps = []
        for c in cores:
            pT = np.ascontiguousarray(p[li, c // 4, (c % 4) * NT:(c % 4 + 1) * NT, :].T)
            m = {"hT_in": hT[c], "gains": g_post, "wg": wg, "wu": wu, "wd": wd, "qk_in": qk[c], "w_out": w_out,
                 "wpg": f32(w_ple_gate)[li], "wpp": f32(w_ple_proj)[li], "pT": pT, "cmask": cmask}
            m.update(ex[c])
            if kind == "c":
                m["bias"] = bias
                m["qm"] = c_qmask(c)
            else:
                m["sink"] = sink
            in_maps.append(m)
        res = run_bass_kernel_spmd(get_prog(kind, "post"), in_maps, core_ids=cores).results
        hT = [res[c]["hT_out"] for c in cores]
        if kind == "c":
            fin = [res[c]["finT"] for c in cores]
        if _debug is not None:
            _debug(f"post{li}", hT)
    out = np.empty((2, 4 * NT, D), np.float32)
    for c in cores:
        out[c // 4, (c % 4) * NT:(c % 4 + 1) * NT, :] = fin[c].T
    return out


NR_X = 55680


def xb_layout(kind):
    lay = {}
    r = 0
    if kind == "ab":
        items = [("kA", 2 * 128 * 18)] + [(f"kB{g}", 2 * 128 * (NT + 128 * d) // 128) for g, d in enumerate(B_DIL)]
        items += [("vA", (NT + 256) * 3)] + [(f"vB{g}", (NT + 128 * d) * 3) for g, d in enumerate(B_DIL)]
    else:
        items = [("kC", 8 * 128 * 20), ("vC", 4 * (NT + 512) * 3)]
    for n, nr in items:
        lay[n] = (r, nr)
        r += nr
    assert r <= NR_X
    return lay


def xb_views(kind, base):
    lay = xb_layout(kind)
    v = {}
    for n, (r0, nr) in lay.items():
        ap = base(r0, nr)
        if n == "kA":
            v[n] = ap.rearrange("(c p a) b -> c p (a b)", c=2, p=128)
        elif n.startswith("kB"):
            v[n] = ap.rearrange("(c p a) b -> c p (a b)", c=2, p=128)
        elif n == "kC":
            v[n] = ap.rearrange("(g c p a) b -> g c p (a b)", g=4, c=2, p=128)
        elif n == "vC":
            v[n] = ap.rearrange("(g r a) b -> g r (a b)", g=4, a=3)
        else:
            v[n] = ap.rearrange("(r a) b -> r (a b)", a=3)
    return v


def build_fused(nlayers=4, dbg_h=False, no_cc=False):
    nc = bass.Bass("TRN2", target_bir_lowering=False)
    es = ExitStack()
    M = Mixer(nc, es)
    xT = M.dram_in("xT", [D, NT])
    finT = M.dram_out("finT", [D, NT])
    M.setup_common()
    gains = M.load_gains("gains", 17)
    valid_d = M.dram_in("valid", [128, 2])
    M.valid = M.sb("valid_sb", [128, 2], F32)
    M.dma("sync", M.valid[:], valid_d[:], [], [("valid",)])
    hv = xT.rearrange("(c p) t -> p c t", p=128)
    for c in range(NC8):
        M.dma("sync", M.h[:, c, :], hv[:, c, :], [], [("h", c, t) for t in range(NTT)])
    M.setup_mixer_consts(True)
    M._rank = {}
    for li in range(nlayers):
        kind = "ab" if li % 2 == 0 else "c"
        nfm = 20 if kind == "ab" else 16
        L = f"_{li}"
        wg1 = M.dram_in("wg1" + L, [D, DFF]); wu1 = M.dram_in("wu1" + L, [D, DFF]); wd1 = M.dram_in("wd1" + L, [DFF, D])
        wg2 = M.dram_in("wg2" + L, [D, DFF]); wu2 = M.dram_in("wu2" + L, [D, DFF]); wd2 = M.dram_in("wd2" + L, [DFF, D])
        w_in = M.dram_in("w_in" + L, [D, nfm * 128 + 1024])
        w_out = M.dram_in("w_out" + L, [D, D])
        wpg = M.dram_in("wpg" + L, [D, D]); wpp = M.dram_in("wpp" + L, [256, D]); pT = M.dram_in("pT" + L, [256, NT])
        q_dram = nc.dram_tensor("q_dram" + L, [nfm, 128, NT], BF16).ap()
        xb = nc.dram_tensor("xb" + L, [NR_X, 128], BF16)
        xv = xb_views(kind, lambda r0, nr: xb.ap()[r0:r0 + nr, :])
        lay = xb_layout(kind)

        def do_exchange(li=li, kind=kind, nfm=nfm, L=L, xv=xv, lay=lay):
            wr_keys = [("qk_out", ci) for ci in range(nfm)] + [("v_out", vg, j) for vg in range(4) for j in range(4)]
            NES = 14720
            edge_send = nc.dram_tensor("edge_send" + L, [2 * NES, 64], BF16)
            gath_e = nc.dram_tensor("gath_e" + L, [NCORES * 2 * NES, 64], BF16)
            edge_nb = nc.dram_tensor("edge_nb" + L, [2 * NES, 64], BF16)
            fills = []
            sends = []
            eoff = [0]
            pending = []

            def edge_view(t, side, name, r0):
                if name == "kA":
                    nr, pat, kw = 512, "(c p a) b -> c p (a b)", dict(c=2, p=128)
                elif name.startswith("kB"):
                    d = B_DIL[int(name[2])]
                    nr, pat, kw = 256 * d, "(c p k) b -> c p k b", dict(c=2, p=128)
                elif name == "kC":
                    nr, pat, kw = 4096, "(g c p a) b -> g c p (a b)", dict(g=4, c=2, p=128)
                elif name == "vA":
                    nr, pat, kw = 768, "(r a) b -> r (a b)", dict(a=6)
                elif name.startswith("vB"):
                    d = B_DIL[int(name[2])]
                    nr, pat, kw = 384 * d, "(k t a) b -> k t (a b)", dict(k=d, a=6)
                else:
                    nr, pat, kw = 6144, "(g r a) b -> g r (a b)", dict(g=4, a=6)
                base = side * NES + r0
                return t.ap()[base:base + nr, :].rearrange(pat, **kw), nr

            def fill(name, sel, side):
                own_region, halo_region = sel(xv[name], side, False)
                if side == 0:
                    pending.append((name, sel))
                r0 = sum(edge_view(edge_send, 0, n_, 0)[1] for n_, _ in pending[:[n_ for n_, _ in pending].index(name)])
                sv, _ = edge_view(edge_send, side, name, r0)
                nv, _ = edge_view(edge_nb, side, name, r0)
                k1 = ("esend", li, name, side)
                M.dma("sync", sv, own_region, wr_keys, [k1])
                sends.append(k1)
                fills.append((name, side, halo_region, nv))

            def sel_cols(hw, own):
                def f(v, side, is_src):
                    if side == 0:
                        return v[:, :, own:own + hw], v[:, :, 0:hw]
                    return v[:, :, hw:2 * hw], v[:, :, hw + own:hw + own + hw]
                return f

            def sel_rows(hw, own):
                def f(v, side, is_src):
                    if side == 0:
                        return v[own:own + hw, :], v[0:hw, :]
                    return v[hw:2 * hw, :], v[hw + own:hw + own + hw, :]
                return f

            if kind == "ab":
                for side in range(2):
                    fill("kA", sel_cols(128, NT), side)
                    fill("vA", sel_rows(128, NT), side)
                    for g, d in enumerate(B_DIL):
                        T = NT // d

                        def selk(v, side_, is_src, d=d, T=T):
                            vv = v.rearrange("c p (k t) -> c p k t", k=d)
                            if side_ == 0:
                                return vv[:, :, :, T:T + 64], vv[:, :, :, 0:64]
                            return vv[:, :, :, 64:128], vv[:, :, :, T + 64:T + 128]

                        def selv(v, side_, is_src, d=d, T=T):
                            vv = v.rearrange("(k t) x -> k t x", k=d)
                            if side_ == 0:
                                return vv[:, T:T + 64, :], vv[:, 0:64, :]
                            return vv[:, 64:128, :], vv[:, T + 64:T + 128, :]
                        fill(f"kB{g}", selk, side)
                        fill(f"vB{g}", selv, side)
            else:
                for side in range(2):
                    def selkc(v, side_, is_src):
                        if side_ == 0:
                            return v[:, :, :, NT:NT + 256], v[:, :, :, 0:256]
                        return v[:, :, :, 256:512], v[:, :, :, 256 + NT:512 + NT]

                    def selvc(v, side_, is_src):
                        if side_ == 0:
                            return v[:, NT:NT + 256, :], v[:, 0:256, :]
                        return v[:, 256:512, :], v[:, 256 + NT:512 + NT, :]
                    fill("kC", selkc, side)
                    fill("vC", selvc, side)
            if not no_cc:
                M.S.add("gpsimd", lambda e, edge_send=edge_send, gath_e=gath_e: e.collective_compute(
                    "AllGather", ALU.bypass, replica_groups=[list(range(NCORES))],
                    ins=[edge_send.ap()[:, :]], outs=[gath_e.ap()[:, :]]), sends, [("gath", li)], cc=True)

            def dyn(side, gath_e=gath_e, edge_nb=edge_nb, NES=NES):
                def fn(e):
                    if "pid_sync" not in M._rank:
                        M._rank["pid_sync"] = e.partition_id()
                    pid = M._rank["pid_sync"]
                    rank = (pid + 7) % NCORES if side == 0 else (pid + 1) % NCORES
                    src = gath_e.ap()[bass.ds(rank * (2 * NES) + side * NES, NES), :]
                    return e.dma_start(out=edge_nb.ap()[side * NES:(side + 1) * NES, :], in_=src)
                return fn
            def tail():
                for side in range(2):
                    M.S.add("sync", dyn(side), [("gath", li)], [("enb", li, side)], dma=True)
                fkeys = []
                for name, side, halo_region, nv in fills:
                    k2 = ("fill", li, name, side)
                    M.dma("sync", halo_region, nv, [("enb", li, side)], [k2])
                    fkeys.append(k2)
                M.dram_rd = fkeys + wr_keys
                M.exch_tail = None
            if kind == "ab":
                M.dram_rd = wr_keys
                M.exch_tail = tail
            else:
                tail()

        M.rmsnorm(gains, 4 * li + 0, "gains")
        M.ffn(wg1, wu1, wd1, "f1")
        M.rmsnorm(gains, 4 * li + 1, "gains")
        if kind == "ab":
            def qk_dst(ci, xv=xv, q_dram=q_dram):
                if 6 <= ci < 8:
                    return xv["kA"][ci - 6][:, 128:128 + NT]
                if ci >= 14:
                    g, sp = (ci - 14) // 2, (ci - 14) % 2
                    d = B_DIL[g]
                    T = NT // d
                    kv = xv[f"kB{g}"][sp]
                    if d == 1:
                        return kv[:, 64:64 + T]
                    return kv.rearrange("p (c t) -> p c t", c=d)[:, :, 64:64 + T]
                return q_dram[ci]

            def v_dst(vg, d, blk0, xv=xv):
                if vg == 0:
                    return xv["vA"][128 + blk0 * 128:128 + blk0 * 128 + 512, :].rearrange("(j p) x -> p j x", p=128)
                g = vg - 1
                T = NT // d
                nb = T // 128
                if nb >= 4:
                    c, tb0 = blk0 // nb, blk0 % nb
                    r = c * (T + 128) + 64 + tb0 * 128
                    return xv[f"vB{g}"][r:r + 512, :].rearrange("(j p) x -> p j x", p=128)
                return xv[f"vB{g}"].rearrange("(c t) x -> c t x", c=d)[blk0:blk0 + 4, 64:192, :].rearrange(
                    "j p x -> p j x")
            fm = [(True, 1)] * 8 + [(True, B_DIL[g]) for g in range(3) for _ in range(2)] * 2
            M.inproj(w_in, fm, [1, 1, 4, 16], qk_dst, v_dst, first_groups=[3, 7, 8, 9], mid=do_exchange)
        else:
            def qk_dst(ci, xv=xv, q_dram=q_dram):
                if ci >= 8:
                    return xv["kC"][(ci - 8) // 2, (ci - 8) % 2][:, 256:256 + NT]
                return q_dram[ci]

            def v_dst(vg, d, blk0, xv=xv):
                return xv["vC"][vg][256 + blk0 * 128:256 + blk0 * 128 + 512, :].rearrange("(j p) x -> p j x", p=128)
            M.inproj(w_in, [(False, 1)] * 16, [1, 1, 1, 1], qk_dst, v_dst, first_groups=[4, 5, 6, 7], mid=do_exchange)


        if kind == "ab":
            sink = M.dram_in("sink" + L, [128, 12])
            M.attn_ab(q_dram, xv["kA"], xv["vA"], [xv[f"kB{g}"] for g in range(3)],
                      [xv[f"vB{g}"] for g in range(3)], sink)
        else:
            bias = M.dram_in("bias" + L, [16, 128, 896])
            if li == 1:
                M.qm_d = M.dram_in("qm", [128, 27 * 128])
            M.attn_c(q_dram, xv["kC"], xv["vC"], bias, M.qm_d)
        M.outproj(w_out)
        M.rmsnorm(gains, 4 * li + 2, "gains")
        M.ffn(wg2, wu2, wd2, "f2")
        M.rmsnorm(gains, 4 * li + 3, "gains")
        M.ple(wpg, wpp, pT)
    if dbg_h:
        ov = finT.rearrange("(c p) t -> p c t", p=128)
        for t in range(NTT):
            o = M.dma("sync", ov[:, :, tsl(t)], M.h[:, :, tsl(t)], [("h", c, t) for c in range(NC8)], [("hout", t)])
            M.out_dmas.append(o)
    else:
        M.rmsnorm(gains, 16, "gains", out_f32_dram=finT.rearrange("(c p) t -> p c t", p=128))
    M.S.finalize(es, M.out_dmas)
    with nc.Block() as block:
        M.S.emit(block, final_waits=M.out_dmas)
    return nc, es


def kernel_unfused(*a, **k):
    return kernel_impl_unfused(*a, **k)


_FUSED = {}


def kernel(x, p, norm_ffn1, ffn1_w_gate, ffn1_w_up, ffn1_w_down, norm_mix, w_in_ab, sink_a,
           w_out_ab, w_in_c, rpb_c, w_out_c, norm_ffn2, ffn2_w_gate, ffn2_w_up, ffn2_w_down,
           norm_ple, w_ple_gate, w_ple_proj, norm_final, _nlayers=4, _dbg_h=False, _no_cc=False):
    f32 = lambda a: np.ascontiguousarray(np.asarray(a, dtype=np.float32))
    x = f32(x); p = f32(p)
    cores = list(range(NCORES))
    if "nc" not in _FUSED:
        _FUSED["nc"], _FUSED["es"] = build_fused(_nlayers, _dbg_h, _no_cc)
    cmask, pswap = const_tables()
    glist = []
    for li in range(4):
        glist += [f32(norm_ffn1)[li], f32(norm_mix)[li], f32(norm_ffn2)[li], f32(norm_ple)[li]]
    glist.append(f32(norm_final))
    shared = {"gains": gains_layout(np.stack(glist)), "cmask": cmask, "pswap": pswap}
    for li in range(_nlayers):
        L = f"_{li}"
        j = li // 2
        shared["wg1" + L] = f32(ffn1_w_gate)[li]; shared["wu1" + L] = f32(ffn1_w_up)[li]; shared["wd1" + L] = f32(ffn1_w_down)[li]
        shared["wg2" + L] = f32(ffn2_w_gate)[li]; shared["wu2" + L] = f32(ffn2_w_up)[li]; shared["wd2" + L] = f32(ffn2_w_down)[li]
        shared["wpg" + L] = f32(w_ple_gate)[li]; shared["wpp" + L] = f32(w_ple_proj)[li]
        if li % 2 == 0:
            shared["w_in" + L] = perm_w_in_ab(f32(w_in_ab)[j])
            shared["w_out" + L] = perm_w_out_ab(f32(w_out_ab)[j])
            shared["sink" + L] = np.ascontiguousarray(np.broadcast_to(f32(sink_a)[j][None, :], (128, 12)))
        else:
            shared["w_in" + L] = f32(w_in_c)[j]
            shared["w_out" + L] = f32(w_out_c)[j]
            shared["bias" + L] = c_bias_table(f32(rpb_c)[j])
    in_maps = []
    for c in cores:
        b, q = c // 4, c % 4
        m = dict(shared)
        m["xT"] = np.ascontiguousarray(x[b, q * NT:(q + 1) * NT, :].T)
        m["ropecs"] = rope_table(c)
        if _nlayers > 1:
            m["qm"] = c_qmask(c)
        m["valid"] = np.ascontiguousarray(np.broadcast_to(
            np.array([[1.0 if q > 0 else 0.0, 1.0 if q < 3 else 0.0]], np.float32), (128, 2)))
        for li in range(_nlayers):
            m[f"pT_{li}"] = np.ascontiguousarray(p[li, b, q * NT:(q + 1) * NT, :].T)
        in_maps.append(m)
    res = run_bass_kernel_spmd(_FUSED["nc"], in_maps, core_ids=cores).results
    out = np.empty((2, 4 * NT, D), np.float32)
    for c in cores:
        out[c // 4, (c % 4) * NT:(c % 4 + 1) * NT, :] = res[c]["finT"].T
    return out
```
